# Optimizing a Trainium2 kernel written in Bass

```python
import jax, jax.numpy as jnp
from jax import lax
import numpy as np

D_MODEL = 2048
BATCH = 1
SEQ = 8192
DEPTH = 2
DEC_BATCH = 2
DEC_SEQ = 8192
PAST_LEN = 128

HEAD_DIM = 128
A_Q_HEADS = 6
A_KV_HEADS = 2
A_GROUP = A_Q_HEADS // A_KV_HEADS
A_WINDOW = 128
A_BLOCK = 128
B_PATTERNS = ((128, 1), (512, 4), (2048, 16))
B_HEADS_PER_GROUP = 2
B_HEADS = B_HEADS_PER_GROUP * len(B_PATTERNS)
B_BLOCK = 64
C_HEADS = 4
GRID_W = 64
NA_ROWS = 8
NA_COLS = 16
NA_QCOLS = 16
NA_KCOLS = 2 * NA_COLS
MIX_WIDTH = (A_Q_HEADS + B_HEADS + C_HEADS) * HEAD_DIM
IN_WIDTH = (A_Q_HEADS + 2 * A_KV_HEADS + 3 * B_HEADS + 3 * C_HEADS) * HEAD_DIM
D_FF = -(-8 * D_MODEL // (3 * 256)) * 256
ROT_DIM = HEAD_DIM // 4
ROPE_THETA = 500000.0
EPS = 1e-6
NEG = -1e30

kernel_name = 'hybrid_parallel_local_dilated_neighbourhood_encoder'


def rms_norm(x, g):
    x32 = x.astype(jnp.float32)
    y = x32 * lax.rsqrt(jnp.mean(x32 * x32, axis=-1, keepdims=True) + EPS) * g.astype(jnp.float32)
    return y.astype(x.dtype)


def partial_rope(x):
    seq = x.shape[1]
    half = ROT_DIM // 2
    inv = jnp.asarray((ROPE_THETA ** (-np.arange(0, ROT_DIM, 2, dtype=np.float32) / ROT_DIM)).astype(np.float32))
    ang = jnp.arange(seq, dtype=jnp.float32)[:, None] * inv[None, :]
    shape = (1, seq) + (1,) * (x.ndim - 3) + (half,)
    cos = jnp.cos(ang).reshape(shape)
    sin = jnp.sin(ang).reshape(shape)
    xr = x[..., :ROT_DIM].astype(jnp.float32)
    x1, x2 = xr[..., :half], xr[..., half:]
    rot = jnp.concatenate([x1 * cos - x2 * sin, x2 * cos + x1 * sin], axis=-1)
    return jnp.concatenate([rot.astype(x.dtype), x[..., ROT_DIM:]], axis=-1)


def banded_attention(q, k, v, half_window, block, sink=None, with_lse=False):
    n, length, hk, g, hd = q.shape
    nb = -(-length // block)
    tail = nb * block - length
    qb = jnp.pad(q, ((0, 0), (0, tail), (0, 0), (0, 0), (0, 0))).reshape(n, nb, block, hk, g, hd)
    pad_kv = ((0, 0), (block, block + tail), (0, 0), (0, 0))

    def windows(t):
        tp = jnp.pad(t, pad_kv).reshape(n, nb + 2, block, hk, hd)
        return jnp.concatenate([tp[:, :-2], tp[:, 1:-1], tp[:, 2:]], axis=2)

    kw, vw = windows(k), windows(v)
    qpos = np.arange(nb * block).reshape(nb, block)
    kpos = (np.arange(nb)[:, None] - 1) * block + np.arange(3 * block)[None, :]
    valid = (np.abs(qpos[:, :, None] - kpos[:, None, :]) <= half_window) & ((kpos >= 0) & (kpos < length))[:, None, :]
    s = jnp.einsum('nbqhgd,nbkhd->nbhgqk', qb, kw, preferred_element_type=jnp.float32) * (hd ** -0.5)
    s = jnp.where(jnp.asarray(valid)[None, :, None, None], s, NEG)
    m = s.max(axis=-1)
    if sink is not None:
        sink = sink.astype(jnp.float32)[None, None, :, :, None]
        m = jnp.maximum(m, sink)
    p = jnp.exp(s - m[..., None])
    denom = p.sum(axis=-1)
    if sink is not None:
        denom = denom + jnp.exp(sink - m)
    o = jnp.einsum('nbhgqk,nbkhd->nbqhgd', p.astype(v.dtype), vw, preferred_element_type=jnp.float32)
    o = o / jnp.moveaxis(denom, 4, 2)[..., None]
    o = o.astype(q.dtype).reshape(n, nb * block, hk, g, hd)[:, :length]
    if with_lse:
        lse = jnp.moveaxis(m + jnp.log(denom), 4, 2).reshape(n, nb * block, hk, g)[:, :length]
        return o, lse
    return o


def to_residue(x, d):
    b, s = x.shape[:2]
    rest = x.shape[2:]
    x = jnp.moveaxis(x.reshape((b, s // d, d) + rest), 2, 1)
    return x.reshape((b * d, s // d) + rest)


def from_residue(x, d, b):
    sd = x.shape[1]
    rest = x.shape[2:]
    x = jnp.moveaxis(x.reshape((b, d, sd) + rest), 1, 2)
    return x.reshape((b, sd * d) + rest)


def dilated_attention(q, k, v):
    bsz, seq = q.shape[:2]
    outs, lses = [], []
    for gi, (window, dil) in enumerate(B_PATTERNS):
        sl = slice(gi * B_HEADS_PER_GROUP, (gi + 1) * B_HEADS_PER_GROUP)
        qg = to_residue(q[:, :, sl, None], dil)
        kg = to_residue(k[:, :, sl], dil)
        vg = to_residue(v[:, :, sl], dil)
        o, lse = banded_attention(qg, kg, vg, window // (2 * dil), B_BLOCK, with_lse=True)
        outs.append(from_residue(o[:, :, :, 0], dil, bsz))
        lses.append(from_residue(lse[:, :, :, 0], dil, bsz))
    alpha = jax.nn.softmax(jnp.stack(lses, axis=0), axis=0)
    o = jnp.concatenate([og * alpha[gi][..., None].astype(og.dtype) for gi, og in enumerate(outs)], axis=2)
    return o.reshape(bsz, seq, B_HEADS * HEAD_DIM)


def neighbourhood_attention(q, k, v, rpb):
    bsz, seq, h, hd = q.shape
    rows = seq // GRID_W
    kr = min(NA_ROWS, rows)
    ncb = GRID_W // NA_QCOLS
    r = np.arange(rows)
    key_rows = np.clip(r - kr // 2, 0, rows - kr)[:, None] + np.arange(kr)[None, :]
    cb = np.arange(ncb)
    key_cols = np.clip(cb * NA_QCOLS - NA_COLS // 2, 0, GRID_W - NA_KCOLS)[:, None] + np.arange(NA_KCOLS)[None, :]
    nk = kr * NA_KCOLS
    krow = np.broadcast_to(key_rows[:, :, None], (rows, kr, NA_KCOLS)).reshape(rows, nk)
    kcol = np.broadcast_to(key_cols[:, None, :], (ncb, kr, NA_KCOLS)).reshape(ncb, nk)
    idx = krow[:, None, :] * GRID_W + kcol[None, :, :]
    qcol = cb[:, None] * NA_QCOLS + np.arange(NA_QCOLS)[None, :]
    cstart = np.clip(qcol - NA_COLS // 2, 0, GRID_W - NA_COLS)
    col_valid = (kcol[:, None, :] >= cstart[:, :, None]) & (kcol[:, None, :] < cstart[:, :, None] + NA_COLS)
    rel_row = krow - r[:, None] + NA_ROWS - 1
    rel_col = np.clip(kcol[:, None, :] - qcol[:, :, None] + NA_COLS - 1, 0, 2 * NA_COLS - 2)
    bias = rpb.astype(jnp.float32)[:, rel_row[:, None, None, :], rel_col[None]]
    kg = k[:, idx]
    vg = v[:, idx]
    qb = q.reshape(bsz, rows, ncb, NA_QCOLS, h, hd)
    s = jnp.einsum('bnjqhd,bnjkhd->bnjhqk', qb, kg, preferred_element_type=jnp.float32) * (hd ** -0.5)
    s = s + jnp.transpose(bias, (1, 2, 0, 3, 4))[None]
    s = jnp.where(jnp.asarray(col_valid)[None, None, :, None], s, NEG)
    p = jax.nn.softmax(s, axis=-1)
    o = jnp.einsum('bnjhqk,bnjkhd->bnjqhd', p.astype(v.dtype), vg)
    return o.reshape(bsz, seq, h * hd)


def encoder_layer(x, norm_mix, w_in, qk_norm, sink_a, rpb_c, w_out, norm_ffn, w_gate, w_up, w_down):
    bsz, seq, _ = x.shape
    hn = rms_norm(x, norm_mix)
    proj = jnp.einsum('bsd,de->bse', hn, w_in)
    sizes = [A_Q_HEADS, A_KV_HEADS, A_KV_HEADS, B_HEADS, B_HEADS, B_HEADS, C_HEADS, C_HEADS, C_HEADS]
    cuts, acc = [], 0
    for sz in sizes[:-1]:
        acc += sz * HEAD_DIM
        cuts.append(acc)
    qa, ka, va, qb, kb, vb, qc, kc, vc = jnp.split(proj, cuts, axis=-1)
    qa = partial_rope(rms_norm(qa.reshape(bsz, seq, A_KV_HEADS, A_GROUP, HEAD_DIM), qk_norm[0, 0]))
    ka = partial_rope(rms_norm(ka.reshape(bsz, seq, A_KV_HEADS, HEAD_DIM), qk_norm[0, 1]))
    va = va.reshape(bsz, seq, A_KV_HEADS, HEAD_DIM)
    oa = banded_attention(qa, ka, va, A_WINDOW, A_BLOCK, sink=sink_a.reshape(A_KV_HEADS, A_GROUP))
    oa = oa.reshape(bsz, seq, A_Q_HEADS * HEAD_DIM)
    qb = partial_rope(rms_norm(qb.reshape(bsz, seq, B_HEADS, HEAD_DIM), qk_norm[1, 0]))
    kb = partial_rope(rms_norm(kb.reshape(bsz, seq, B_HEADS, HEAD_DIM), qk_norm[1, 1]))
    ob = dilated_attention(qb, kb, vb.reshape(bsz, seq, B_HEADS, HEAD_DIM))
    qc = rms_norm(qc.reshape(bsz, seq, C_HEADS, HEAD_DIM), qk_norm[2, 0])
    kc = rms_norm(kc.reshape(bsz, seq, C_HEADS, HEAD_DIM), qk_norm[2, 1])
    oc = neighbourhood_attention(qc, kc, vc.reshape(bsz, seq, C_HEADS, HEAD_DIM), rpb_c)
    mixed = jnp.concatenate([oa, ob, oc], axis=-1)
    x = x + jnp.einsum('bse,ed->bsd', mixed, w_out)
    hn = rms_norm(x, norm_ffn)
    ff = jax.nn.silu(jnp.einsum('bsd,df->bsf', hn, w_gate)) * jnp.einsum('bsd,df->bsf', hn, w_up)
    return x + jnp.einsum('bsf,fd->bsd', ff, w_down)


def trunk(x, norm_mix, w_in, qk_norm, sink_a, rpb_c, w_out, norm_ffn, w_gate, w_up, w_down):
    for l in range(DEPTH):
        x = encoder_layer(x, norm_mix[l], w_in[l], qk_norm[l], sink_a[l], rpb_c[l], w_out[l],
                          norm_ffn[l], w_gate[l], w_up[l], w_down[l])
    return x


def setup_inputs(seed: int = 0) -> dict:
    key = jax.random.key(seed)
    ks = jax.random.split(key, 12)
    f32 = jnp.float32
    nrm = lambda k, shape, scale: jax.random.normal(k, shape, f32) * scale
    return {
        'x_prompt': nrm(ks[0], (BATCH, SEQ, D_MODEL), 1.0),
        'x_sample': nrm(ks[1], (DEC_BATCH, DEC_SEQ, D_MODEL), 1.0),
        'norm_mix': 1.0 + nrm(ks[2], (DEPTH, D_MODEL), 0.02),
        'w_in': nrm(ks[3], (DEPTH, D_MODEL, IN_WIDTH), D_MODEL ** -0.5),
        'qk_norm': 1.0 + nrm(ks[4], (DEPTH, 3, 2, HEAD_DIM), 0.02),
        'sink_a': nrm(ks[5], (DEPTH, A_Q_HEADS), 0.5),
        'rpb_c': nrm(ks[6], (DEPTH, C_HEADS, 2 * NA_ROWS - 1, 2 * NA_COLS - 1), 0.1),
        'w_out': nrm(ks[7], (DEPTH, MIX_WIDTH, D_MODEL), MIX_WIDTH ** -0.5),
        'norm_ffn': 1.0 + nrm(ks[8], (DEPTH, D_MODEL), 0.02),
        'w_gate': nrm(ks[9], (DEPTH, D_MODEL, D_FF), D_MODEL ** -0.5),
        'w_up': nrm(ks[10], (DEPTH, D_MODEL, D_FF), D_MODEL ** -0.5),
        'w_down': nrm(ks[11], (DEPTH, D_FF, D_MODEL), D_FF ** -0.5),
    }


def reference(x_prompt, x_sample, norm_mix, w_in, qk_norm, sink_a, rpb_c, w_out, norm_ffn, w_gate, w_up, w_down):
    y_prompt = trunk(x_prompt, norm_mix, w_in, qk_norm, sink_a, rpb_c, w_out, norm_ffn, w_gate, w_up, w_down)
    y_sample = trunk(x_sample, norm_mix, w_in, qk_norm, sink_a, rpb_c, w_out, norm_ffn, w_gate, w_up, w_down)
    return (y_prompt, y_sample)
```

```python
import numpy as np
import concourse.bass as bass
import concourse.mybir as mybir
from concourse.bass_utils import run_bass_kernel_spmd

F32 = mybir.dt.float32
BF16 = mybir.dt.bfloat16
AF = mybir.ActivationFunctionType
ALU = mybir.AluOpType

NCORES = 8
D = 2048
NSEQ = 3
SEQ = 8192
OWN = 3072
HALO = 1024
U = OWN + 4 * HALO
TT = 512
DFF = 5632
NF = DFF // 128
NC16 = D // 128
EPS = 1e-6
NEGM = -30000.0
GRID_W = 64
ROPE_THETA = 500000.0

REGIONS = [(0, U, HALO, U - HALO), (HALO, U - HALO, 2 * HALO, U - 2 * HALO)]

KBLKS = [("ka", i) for i in range(2)] + [("kb", i) for i in range(6)] + [("kc", i) for i in range(4)]
QBLKS = [("qa", i) for i in range(6)] + [("qb", i) for i in range(6)] + [("qc", i) for i in range(4)]
QKBLKS = KBLKS + QBLKS

SAME_ENGINE_WAITS = True


def _col_index():
    idx = {}
    n = 0
    for qb in range(HALO // 128, (U - HALO) // 128):
        for j in range(3):
            idx[("A", qb, j)] = n; n += 1
    for qb in range(HALO // 128, (U - HALO) // 128):
        for j in range(2):
            idx[("B0", qb, j)] = n; n += 1
    for m in range(1, 6):
        for qh in range(2):
            for j in range(2):
                idx[("B1", m, qh, j)] = n; n += 1
    for m in range(1, 6):
        for j in range(2):
            idx[("B2", m, j)] = n; n += 1
    for qb in range(HALO // 128, (U - HALO) // 128):
        for dt in range(7):
            for qh in range(2):
                idx[("C", qb, dt, qh)] = n; n += 1
    return idx, n


COLIDX, NCOLS = _col_index()


def _seqid(g):
    g = np.asarray(g)
    return np.where((g >= 0) & (g < NSEQ * SEQ), g // SEQ, -1)


def _build_cols(core):
    base = core * OWN - 2 * HALO
    cols = np.zeros((128, NCOLS), np.float32)
    p = np.arange(128)

    def setcol(key, ktok_u, qtok_u, extra_valid=None):
        gq = base + qtok_u
        sq = int(_seqid(gq))
        if sq < 0:
            return
        gk = base + ktok_u
        valid = (_seqid(gk) == sq)
        if extra_valid is not None:
            valid = valid & extra_valid
        cols[:len(valid), COLIDX[key]] = np.where(valid, 0.0, NEGM)

    for qb in range(HALO // 128, (U - HALO) // 128):
        u0 = qb * 128
        for j in range(3):
            setcol(("A", qb, j), u0 + 128 * (j - 1) + p, u0)
        for j in range(2):
            setcol(("B0", qb, j), u0 - 64 + 128 * j + p, u0)
        gq0 = base + u0
        for dt in range(7):
            for qh in range(2):
                rq = (gq0 % SEQ) // GRID_W + qh
                ks = min(max(rq - 4, 0), SEQ // GRID_W - 8)
                ktok = u0 + 128 * (dt - 3) + p
                gk = base + ktok
                kr = (gk % SEQ) // GRID_W
                ev = (kr >= ks) & (kr < ks + 8)
                setcol(("C", qb, dt, qh), ktok, u0, ev)
    for m in range(1, 6):
        for qh in range(2):
            J0 = 256 * m + 128 * qh
            for j in range(2):
                setcol(("B1", m, qh, j), 4 * (J0 - 64 + 128 * j + p), 4 * J0)
        J0 = 64 * m
        setcol(("B2", m, 0), 16 * (J0 - 64 + p), 16 * J0)
        setcol(("B2", m, 1), 16 * (J0 + 64 + p[:64]), 16 * J0)
    return cols


def _ctab_index():
    k = np.arange(128)[:, None]
    q = np.arange(128)[None, :]
    kro, kc = k // 64, k % 64
    qro, qc = q // 64, q % 64
    cstart = np.clip(qc - 8, 0, GRID_W - 16)
    cvalid = (kc >= cstart) & (kc < cstart + 16)
    relc = np.clip(kc - qc + 15, 0, 30)
    out = np.zeros((7, 128, 4, 128), np.int64)
    for dt in range(7):
        dr = 2 * (dt - 3) + kro - qro
        relr = np.clip(dr + 7, 0, 14)
        for h in range(4):
            ii = h * 15 * 31 + relr * 31 + relc
            out[dt, :, h, :] = np.where(cvalid, ii, 4 * 15 * 31)
    return out


class _Op:
    __slots__ = ("eng", "fn", "deps", "dma", "dkey", "sem", "val", "hasdep", "i", "persist")


class Prog:
    ENG = ("pe", "act", "dve", "pool", "sp")

    def __init__(self):
        self.ops = []
        self.lastw = {}
        self.readers = {}
        self.lastop = {}
        self.lastdma = {}

    def add(self, eng, fn, reads=(), writes=(), dkey=None, persist=False):
        op = _Op()
        op.persist = persist
        op.eng, op.fn, op.dma, op.dkey = eng, fn, dkey is not None, dkey
        op.sem = None; op.val = 0; op.hasdep = False; op.i = len(self.ops)
        deps = set()
        for r in reads:
            w = self.lastw.get(r)
            if w is not None:
                deps.add(w)
        for w_ in writes:
            lw = self.lastw.get(w_)
            if lw is not None:
                deps.add(lw)
            for rd in self.readers.get(w_, ()):
                deps.add(rd)
        for r in reads:
            self.readers.setdefault(r, []).append(op)
        for w_ in writes:
            self.lastw[w_] = op
            self.readers[w_] = []
        op.deps = deps
        for d in deps:
            d.hasdep = True
        self.ops.append(op)
        if op.dma:
            self.lastdma[dkey] = op
        else:
            self.lastop[eng] = op
        return op

    def barrier(self):
        pend = set(o for o in (set(self.lastop.values()) | set(self.lastdma.values())) if not o.persist)
        keepw = {k: v for k, v in self.lastw.items() if v.persist}
        keepd = {k: v for k, v in self.lastdma.items() if v.persist}
        for e in self.ENG:
            op = _Op()
            op.persist = False
            op.eng, op.fn, op.dma, op.dkey = e, None, False, None
            op.sem = None; op.val = 0; op.hasdep = False; op.i = len(self.ops)
            op.deps = set(pend)
            for d in pend:
                d.hasdep = True
            self.ops.append(op)
        self.lastw.clear(); self.readers.clear(); self.lastop.clear(); self.lastdma.clear()
        self.lastw.update(keepw); self.lastdma.update(keepd)

    def emit(self, nc, block):
        engs = {"pe": "tensor", "act": "scalar", "dve": "vector", "pool": "gpsimd", "sp": "sync"}
        esem = {e: nc.alloc_semaphore(f"e_{e}") for e in self.ENG}
        ecnt = {e: 0 for e in self.ENG}
        dsem = {}
        dcnt = {}
        for op in self.ops:
            if op.fn is None:
                continue
            if op.dma:
                if op.dkey not in dsem:
                    dsem[op.dkey] = nc.alloc_semaphore("d_" + str(len(dsem)))
                    dcnt[op.dkey] = 0
                dcnt[op.dkey] += 16
                op.sem, op.val = dsem[op.dkey], dcnt[op.dkey]
            elif op.hasdep:
                ecnt[op.eng] += 1
                op.sem, op.val = esem[op.eng], ecnt[op.eng]
        self.nsem = len(dsem) + 5
        per = {e: [o for o in self.ops if o.eng == e] for e in self.ENG}

        def run(e):
            def body(eng):
                waited = {}
                for op in per[e]:
                    for d in sorted(op.deps, key=lambda o: o.i):
                        if d.sem is None:
                            continue
                        if (not d.dma) and d.eng == e and (e == "pe" or not SAME_ENGINE_WAITS):
                            continue
                        k = id(d.sem)
                        if waited.get(k, 0) < d.val:
                            eng.wait_ge(d.sem, d.val)
                            waited[k] = d.val
                    if op.fn is None:
                        continue
                    ins = op.fn(eng)
                    if op.dma:
                        ins.then_inc(op.sem, 16)
                    elif op.hasdep:
                        ins.then_inc(op.sem, 1)
            return body

        for e in self.ENG:
            getattr(block, engs[e])(run(e))


class Arena:
    def __init__(self, t, n):
        self.t, self.n, self.o = t, n, 0

    def reset(self):
        self.o = 0

    def get(self, n):
        assert self.o + n <= self.n, (self.o, n, self.n)
        v = self.t[:, self.o:self.o + n]
        self.o += n
        return v


def build_program(debug=False, stop=None):
    nc = bass.Bass("TRN2", target_bir_lowering=False)
    P = Prog()

    def dram(name, shape, dt=F32, kind="ExternalInput"):
        return nc.dram_tensor(name, list(shape), dt, kind=kind).ap()

    xT = dram("xT", [D, U])
    WQK = dram("WQK", [2, 28, 128, 2048])
    WV = dram("WV", [2, 3, 128, 8192])
    WOUT = dram("WOUT", [2, 16, 128, 2048])
    WG = dram("WG", [2, NF, 128, 2048])
    WU = dram("WU", [2, NF, 128, 2048])
    WD = dram("WD", [2, 16, 128, DFF])
    GN = dram("GN", [128, 2 * 2 * 16])
    GQK = dram("GQK", [128, 2 * 6])
    SINK = dram("SINK", [128, 2 * 6])
    COSW = dram("COSW", [128, U])
    SINW = dram("SINW", [32, U])
    COLS = dram("COLS", [128, NCOLS])
    TABC = dram("TABC", [2, 128, 7 * 512])
    CONST = dram("CONST", [128, 128 * 3 + 128])
    yT = dram("yT", [D, OWN], kind="ExternalOutput")
    ik = "ExternalOutput" if debug else "Internal"
    WQKb = dram("WQKb", [2, 28, 128, 2048], BF16, "Internal")
    WVb = dram("WVb", [2, 3, 128, 8192], BF16, "Internal")
    WOUTb = dram("WOUTb", [2, 16, 128, 2048], BF16, "Internal")
    WGb = dram("WGb", [2, NF, 128, 2048], BF16, "Internal")
    WUb = dram("WUb", [2, NF, 128, 2048], BF16, "Internal")
    WDb = dram("WDb", [2, 16, 128, DFF], BF16, "Internal")
    S = {
        "qa": dram("s_qa", [6, 128, U], BF16, ik), "ka": dram("s_ka", [2, 128, U], BF16, ik),
        "qc": dram("s_qc", [4, 128, U], BF16, ik), "kc": dram("s_kc", [4, 128, U], BF16, ik),
        "qb0": dram("s_qb0", [2, 128, U], BF16, ik), "kb0": dram("s_kb0", [2, 128, U], BF16, ik),
        "qb1": dram("s_qb1", [2, 4, 128, U // 4], BF16, ik), "kb1": dram("s_kb1", [2, 4, 128, U // 4], BF16, ik),
        "qb2": dram("s_qb2", [2, 16, 128, U // 16], BF16, ik), "kb2": dram("s_kb2", [2, 16, 128, U // 16], BF16, ik),
    }
    Vs = dram("s_v", [U, 1536], BF16, ik)
    MIX = dram("s_mix", [D, U], BF16, ik)
    X1 = dram("s_x1", [D, U], F32, ik)

    import contextlib
    es = contextlib.ExitStack()
    with es:
        NB16 = 63000
        NF32 = 15000
        abf_t = es.enter_context(nc.sbuf_tensor("abf", [128, NB16], BF16))
        af_t = es.enter_context(nc.sbuf_tensor("af32", [128, NF32], F32))
        cbf_t = es.enter_context(nc.sbuf_tensor("cbf", [128, 128 * 2 + 512 * 2 + 7 * 512 + 128], BF16))
        cf_t = es.enter_context(nc.sbuf_tensor("cf", [128, NCOLS + 64 + 12 + 12 + 12 + 768 + 512], F32))
        ps = [es.enter_context(nc.psum_tensor(f"ps{i}", [128, 512], F32)) for i in range(8)]
        AB = Arena(abf_t, NB16)
        AFP = Arena(af_t, NF32)
        CB = Arena(cbf_t, 128 * 2 + 512 * 2 + 7 * 512 + 128)
        CF = Arena(cf_t, NCOLS + 64 + 12 + 12 + 12 + 768 + 512)

        ident = CB.get(128)
        ones = CB.get(128)
        triGE = CB.get(512)
        triLE = CB.get(512)
        tabc = CB.get(7 * 512)
        rotT = CB.get(128)
        cols = CF.get(NCOLS)
        gn = CF.get(64)
        gqk = CF.get(12)
        gqs = CF.get(12)
        esink = CF.get(12)
        esinkb = CF.get(768)

        AB.reset(); AFP.reset()
        cst32 = CF.get(512)
        P.add("sp", lambda e: e.dma_start(out=cst32, in_=CONST[:, :]), writes=["cst32"], dkey="cst32")
        P.add("sp", lambda e: e.dma_start(out=cols, in_=COLS[:, :]), writes=["cols"], dkey="cols")
        P.add("sp", lambda e: e.dma_start(out=gn, in_=GN[:, :]), writes=["gn"], dkey="gn")
        P.add("sp", lambda e: e.dma_start(out=gqk, in_=GQK[:, :]), writes=["gqk"], dkey="gqk")
        P.add("sp", lambda e: e.dma_start(out=esink, in_=SINK[:, :]), writes=["esink"], dkey="esink")
        P.add("dve", lambda e: e.tensor_copy(out=ident, in_=cst32[:, 0:128]), reads=["cst32"], writes=["ident"])
        P.add("dve", lambda e: e.memset(ones, 1.0), writes=["ones"])
        for r in range(4):
            P.add("dve", lambda e, r=r: e.tensor_copy(out=triGE[:, r * 128:(r + 1) * 128], in_=cst32[:, 128:256]),
                  reads=["cst32"], writes=["triGE"])
            P.add("dve", lambda e, r=r: e.tensor_copy(out=triLE[:, r * 128:(r + 1) * 128], in_=cst32[:, 256:384]),
                  reads=["cst32"], writes=["triLE"])
        P.add("act", lambda e: e.activation(out=esink, in_=esink, func=AF.Exp), reads=["esink"], writes=["esink"])
        P.add("dve", lambda e: e.tensor_copy(out=gqs, in_=gqk), reads=["gqk"], writes=["gqs"])
        for l in range(2):
            for mx in range(3):
                cc = l * 6 + mx * 2
                P.add("dve", lambda e, cc=cc: e.tensor_scalar(out=gqs[:, cc:cc + 1], in0=gqk[:, cc:cc + 1],
                                                             scalar1=float(128 ** -0.5), scalar2=None, op0=ALU.mult),
                      reads=["gqk", "gqs"], writes=["gqs"])

        convq = []

        def conv(src, dst, nblk, key, step=8):
            for b0 in range(0, nblk, step):
                b1 = min(nblk, b0 + step)
                convq.append(lambda b0=b0, b1=b1, src=src, dst=dst, key=key: P.add(
                    "pool", lambda e: e.dma_start(out=dst[b0:b1], in_=src[b0:b1]), writes=[key], dkey=key, persist=True))
        for l in range(2):
            conv(WQK[l], WQKb[l], 28, f"D_wqk{l}", 2 if l == 0 else 1)
            conv(WV[l], WVb[l], 3, f"D_wv{l}", 1)
            conv(WOUT[l], WOUTb[l], 16, f"D_wout{l}", 1)
            conv(WG[l], WGb[l], NF, f"D_wg{l}", 1)
            conv(WU[l], WUb[l], NF, f"D_wu{l}", 1)
            conv(WD[l], WDb[l], 16, f"D_wd{l}", 1)
        NJ1 = 28 + 3 + 16 + NF + NF + 16

        def conv_some(n):
            for _ in range(n):
                if convq:
                    convq.pop(0)()
        conv_some(17)
        P.barrier()

        xin = [xT, X1]
        xout = [X1, None]

        def emit_layer(l):
            kv_lo, kv_hi, q_lo, q_hi = REGIONS[l]
            AB.reset(); AFP.reset()
            t32 = AFP.get(7 * 512)
            P.add("sp", lambda e, l=l: e.dma_start(out=t32, in_=TABC[l]), writes=["t32"], dkey="t32")
            P.add("dve", lambda e: e.tensor_copy(out=tabc, in_=t32), reads=["t32"], writes=["tabc"])
            for h in range(6):
                P.add("dve", lambda e, h=h, l=l: e.tensor_scalar(
                    out=esinkb[:, h * 128:(h + 1) * 128], in0=cst32[:, 0:128], scalar1=0.0,
                    scalar2=esink[:, l * 6 + h:l * 6 + h + 1], op0=ALU.mult, op1=ALU.add),
                    reads=["cst32", "esink"], writes=["esinkb"])
            rg = [AB.get(128) for _ in range(4)]
            gcols = [l * 6 + 0, l * 6 + 1, l * 6 + 2, l * 6 + 3]
            for j in range(4):
                P.add("dve", lambda e, j=j: e.tensor_scalar(
                    out=rg[j], in0=cst32[:, 384:512], scalar1=gqs[:, gcols[j]:gcols[j] + 1], scalar2=None,
                    op0=ALU.mult), reads=["cst32", "gqs"], writes=[f"rg{j}"])
            P.barrier()
            ab_base, af_base = AB.o, 0
            AFP.o = 0

            xt = AFP.get(16 * TT).rearrange("p (c t) -> p c t", c=16)
            rstd = AFP.get(TT)
            cosb = AFP.get(TT)
            sinb = AFP.get(TT)
            cosg = [AFP.get(TT) for _ in range(4)]
            rsh = [AFP.get(TT) for _ in range(2)]
            t1 = [AFP.get(TT) for _ in range(2)]
            t2 = [AFP.get(TT) for _ in range(2)]
            sq = AB.get(16 * TT).rearrange("p (c t) -> p c t", c=16)
            xn = [AB.get(16 * TT).rearrange("p (c t) -> p c t", c=16) for _ in range(2)]
            wqk = [AB.get(2048).rearrange("p (c m) -> p c m", c=16) for _ in range(4)]
            wv = [AB.get(8192).rearrange("p (c m) -> p c m", c=16) for _ in range(2)]
            sqh = [AB.get(TT) for _ in range(3)]
            qbf = [AB.get(TT) for _ in range(3)]
            outq = [AB.get(TT) for _ in range(4)]
            vst = [AB.get(1536) for _ in range(2)]
            xin_v = xin[l].rearrange("(c p) u -> p c u", p=128)
            gmix = l * 32
            wslot = 0
            vslot = 0
            hcount = 0
            vcount = 0
            pscnt = 0
            tiles = list(range(kv_lo // TT, kv_hi // TT))
            cnt = {"w": 0, "v": 0, "h": 0, "vc": 0, "ps": 0}

            def prepA(ti):
                u0 = tiles[ti] * TT
                xs = ti % 2
                P.add("sp", lambda e: e.dma_start(out=xt, in_=xin_v[:, :, u0:u0 + TT]), writes=["xt"], dkey="xt")
                for c in range(16):
                    P.add("act", lambda e, c=c: e.activation(out=sq[:, c, :], in_=xt[:, c, :], func=AF.Square),
                          reads=["xt"], writes=[f"sq{c}"])
                for c in range(16):
                    P.add("pe", lambda e, c=c: e.matmul(ps[7][:], lhsT=ones, rhs=sq[:, c, :], start=(c == 0), stop=(c == 15)),
                          reads=[f"sq{c}"], writes=["ps7"])
                P.add("act", lambda e: e.activation(out=rstd, in_=ps[7][:], func=AF.Sqrt, bias=float(EPS), scale=1.0 / D),
                      reads=["ps7"], writes=["rstd"])
                P.add("dve", lambda e: e.reciprocal(out=rstd, in_=rstd), reads=["rstd"], writes=["rstd"])
                for c in range(16):
                    P.add("dve", lambda e, c=c: e.scalar_tensor_tensor(
                        out=xn[xs][:, c, :], in0=xt[:, c, :], scalar=gn[:, gmix + c:gmix + c + 1], in1=rstd,
                        op0=ALU.mult, op1=ALU.mult), reads=["xt", "rstd"], writes=[f"xn{xs}_{c}"])

            def prepB(ti):
                u0 = tiles[ti] * TT
                P.add("sp", lambda e: e.dma_start(out=cosb, in_=COSW[:, u0:u0 + TT]), writes=["cosb"], dkey="cosb")
                P.add("sp", lambda e: e.dma_start(out=sinb[0:32, :], in_=SINW[:, u0:u0 + TT]), writes=["sinb"], dkey="sinb")
                for j in range(4):
                    P.add("pool", lambda e, j=j: e.tensor_scalar(out=cosg[j], in0=cosb, scalar1=gqs[:, gcols[j]:gcols[j] + 1],
                                                                 scalar2=None, op0=ALU.mult),
                          reads=["cosb"], writes=[f"cosg{j}"])

            def v_section(ti, vbs=(0, 1, 2)):
                u0 = tiles[ti] * TT
                xs = ti % 2
                xnr = [f"xn{xs}_{c}" for c in range(16)]
                for vb in vbs:
                    vs_ = cnt["v"] % 2; cnt["v"] += 1
                    P.add("sp", lambda e, vs_=vs_, vb=vb: e.dma_start(out=wv[vs_], in_=WVb[l, vb].rearrange("p (c m) -> p c m", c=16)),
                          reads=[f"D_wv{l}"], writes=[f"wv{vs_}"], dkey=f"wv{vs_}")
                    for s4 in range(4):
                        pb = cnt["ps"] % 4; cnt["ps"] += 1
                        for c in range(16):
                            P.add("pe", lambda e, c=c, vs_=vs_, pb=pb, s4=s4: e.matmul(
                                ps[pb][:], lhsT=xn[xs][:, c, s4 * 128:(s4 + 1) * 128], rhs=wv[vs_][:, c, :],
                                start=(c == 0), stop=(c == 15)),
                                reads=[f"wv{vs_}", xnr[c]], writes=[f"ps{pb}"])
                        vo = s4 % 2
                        ce = "act"
                        cnt["vc"] += 1
                        if ce == "act":
                            P.add("act", lambda e, pb=pb, vo=vo, vb=vb: e.activation(out=vst[vo][:, vb * 512:(vb + 1) * 512], in_=ps[pb][:], func=AF.Copy),
                                  reads=[f"ps{pb}"], writes=[f"vst{vo}_{vb}"])
                        else:
                            P.add("dve", lambda e, pb=pb, vo=vo, vb=vb: e.tensor_copy(out=vst[vo][:, vb * 512:(vb + 1) * 512], in_=ps[pb][:]),
                                  reads=[f"ps{pb}"], writes=[f"vst{vo}_{vb}"])
                        P.add("pool", lambda e, vo=vo, vb=vb, s4=s4: e.dma_start(
                            out=Vs[u0 + s4 * 128:u0 + (s4 + 1) * 128, vb * 512:(vb + 1) * 512], in_=vst[vo][:, vb * 512:(vb + 1) * 512]),
                            reads=[f"vst{vo}_{vb}"], dkey=f"vst{vo}_{vb}")

            def head(ti, nm, hi):
                u0 = tiles[ti] * TT
                xs = ti % 2
                xnr = [f"xn{xs}_{c}" for c in range(16)]
                gb = QKBLKS.index((nm, hi))
                ws = cnt["w"] % 4; cnt["w"] += 1
                pb = cnt["ps"] % 4; cnt["ps"] += 1
                hs = cnt["h"] % 2
                h3 = cnt["h"] % 3
                os_ = cnt["h"] % 4
                cnt["h"] += 1
                isq = nm[0] == "q"
                mixer = {"a": 0, "b": 1, "c": 2}[nm[1]]
                gcol = l * 6 + mixer * 2 + (0 if isq else 1)
                sb_ = 4 + hs
                rb = 6 + hs
                if nm[1] == "b" and hi >= 2:
                    dil = 4 if hi < 4 else 16
                    ov = outq[os_].rearrange("p (r j) -> p j r", r=dil)
                else:
                    ov = outq[os_]

                def stage1():
                    P.add("sp", lambda e: e.dma_start(out=wqk[ws], in_=WQKb[l, gb].rearrange("p (c m) -> p c m", c=16)),
                          reads=[f"D_wqk{l}"], writes=[f"wqk{ws}"], dkey=f"wqk{ws}")
                    for c in range(16):
                        P.add("pe", lambda e, c=c: e.matmul(
                            ps[pb][:], lhsT=wqk[ws][:, c, :], rhs=xn[xs][:, c, :], start=(c == 0), stop=(c == 15)),
                            reads=[f"wqk{ws}", xnr[c]], writes=[f"ps{pb}"])
                    P.add("act", lambda e: e.activation(out=sqh[h3], in_=ps[pb][:], func=AF.Square),
                          reads=[f"ps{pb}"], writes=[f"sqh{h3}"])
                    if mixer != 2:
                        P.add("act", lambda e: e.activation(out=qbf[h3], in_=ps[pb][:], func=AF.Copy),
                              reads=[f"ps{pb}"], writes=[f"qbf{h3}"])

                def stage2():
                    P.add("pe", lambda e: e.matmul(ps[sb_][:], lhsT=ones, rhs=sqh[h3], start=True, stop=True),
                          reads=[f"sqh{h3}"], writes=[f"ps{sb_}"])
                    P.add("act", lambda e: e.activation(out=rsh[hs], in_=ps[sb_][:], func=AF.Sqrt, bias=float(EPS), scale=1.0 / 128),
                          reads=[f"ps{sb_}"], writes=[f"rsh{hs}"])
                    P.add("dve", lambda e: e.reciprocal(out=rsh[hs], in_=rsh[hs]), reads=[f"rsh{hs}"], writes=[f"rsh{hs}"])
                    if mixer == 2:
                        P.add("dve", lambda e: e.scalar_tensor_tensor(
                            out=ov, in0=ps[pb][:], scalar=gqs[:, gcol:gcol + 1], in1=rsh[hs], op0=ALU.mult, op1=ALU.mult),
                            reads=[f"ps{pb}", f"rsh{hs}"], writes=[f"outq{os_}"])
                    else:
                        j = mixer * 2 + (0 if isq else 1)
                        P.add("pe", lambda e: e.matmul(ps[rb][0:32, :], lhsT=rg[j][:, 0:32], rhs=qbf[h3], start=True, stop=True),
                              reads=[f"qbf{h3}", f"rg{j}"], writes=[f"ps{rb}"])
                        P.add("dve", lambda e: e.tensor_tensor(out=t1[hs], in0=ps[pb][:], in1=cosg[j], op=ALU.mult),
                              reads=[f"ps{pb}", f"cosg{j}"], writes=[f"t1{hs}"])
                        P.add("dve", lambda e: e.tensor_tensor(out=t2[hs][0:32, :], in0=ps[rb][0:32, :], in1=sinb[0:32, :], op=ALU.mult),
                              reads=[f"ps{rb}", "sinb"], writes=[f"t2{hs}"])
                        P.add("pool", lambda e: e.tensor_tensor(out=t1[hs][0:32, :], in0=t1[hs][0:32, :], in1=t2[hs][0:32, :], op=ALU.add),
                              reads=[f"t1{hs}", f"t2{hs}"], writes=[f"t1{hs}"])
                        P.add("pool", lambda e: e.tensor_tensor(out=ov, in0=t1[hs], in1=rsh[hs], op=ALU.mult),
                              reads=[f"t1{hs}", f"rsh{hs}"], writes=[f"outq{os_}"])
                    if nm[1] == "b":
                        g = hi // 2
                        hh = hi % 2
                        if g == 0:
                            dst = S[nm + "0"][hh][:, u0:u0 + TT]
                            src = outq[os_]
                        else:
                            dil_ = 4 if g == 1 else 16
                            n = TT // dil_
                            dst = S[nm + str(g)][hh][:, :, (u0 // dil_):(u0 // dil_) + n].rearrange("r p j -> p r j")
                            src = outq[os_].rearrange("p (r j) -> p r j", r=dil_)
                    else:
                        dst = S[nm][hi][:, u0:u0 + TT]
                        src = outq[os_]
                    P.add("pool", lambda e: e.dma_start(out=dst, in_=src), reads=[f"outq{os_}"], dkey=f"outq{os_}")
                return stage1, stage2

            prepA(0)
            prepB(0)
            for ti, t in enumerate(tiles):
                u0 = t * TT
                need_q = (u0 >= q_lo) and (u0 < q_hi)
                far = (ti == 0) or (ti == len(tiles) - 1)
                v_section(ti, (1,) if far else (0, 1, 2))
                if l == 0:
                    conv_some(1)
                if ti + 1 < len(tiles):
                    prepA(ti + 1)
                blks = QKBLKS if need_q else KBLKS
                if far:
                    blks = [("kb", 4), ("kb", 5)]
                pend = []
                for hidx, (nm, hi) in enumerate(blks):
                    if l == 0 and hidx % 2 == 1:
                        conv_some(1)
                    s1, s2 = head(ti, nm, hi)
                    s1()
                    pend.append(s2)
                    if len(pend) > 2:
                        pend.pop(0)()
                while pend:
                    pend.pop(0)()
                if ti + 1 < len(tiles):
                    prepB(ti + 1)
            P.barrier()
            if stop == ("P", l):
                return True

            AB.o, AFP.o = ab_base, af_base
            qa_t = AB.get(6 * 512).rearrange("p (h t) -> p h t", h=6)
            ka_t = AB.get(2 * 768).rearrange("p (h t) -> p h t", h=2)
            va_t = AB.get(6 * 256).rearrange("p (j h d) -> p j h d", j=6, h=2)
            qb0_t = AB.get(2 * 1024).rearrange("p (h t) -> p h t", h=2)
            kb0_t = AB.get(2 * 1152).rearrange("p (h t) -> p h t", h=2)
            vb0_t = AB.get(9 * 256).rearrange("p (j h d) -> p j h d", j=9, h=2)
            qb1_t = AB.get(2 * 4 * 256).rearrange("p (h r t) -> p h r t", h=2, r=4)
            kb1_t = AB.get(2 * 4 * 384).rearrange("p (h r t) -> p h r t", h=2, r=4)
            vb1_t = AB.get(4 * 3 * 256).rearrange("p (r j h d) -> p r j h d", r=4, j=3, h=2)
            qb2_t = AB.get(2 * 16 * 64).rearrange("p (h r t) -> p h r t", h=2, r=16)
            kb2_t = AB.get(2 * 16 * 192).rearrange("p (h r t) -> p h r t", h=2, r=16)
            vb2a_t = AB.get(16 * 256).rearrange("p (r h d) -> p r h d", r=16, h=2)
            vb2b_t = AB.get(16 * 256).rearrange("p (r h d) -> p r h d", r=16, h=2)
            qc_t = AB.get(4 * 512).rearrange("p (h t) -> p h t", h=4)
            kc_t = AB.get(4 * 1280).rearrange("p (h t) -> p h t", h=4)
            vc_t = AB.get(10 * 512).rearrange("p (j h d) -> p j h d", j=10, h=4)
            pT = [AB.get(512) for _ in range(3)]
            oA = AB.get(6 * 512).rearrange("p (h t) -> p h t", h=6)
            oC = AB.get(4 * 512).rearrange("p (h t) -> p h t", h=4)
            oBb = AB.get(6 * 1024).rearrange("p (h t) -> p h t", h=6)
            rden = [AFP.get(512) for _ in range(2)]
            OB = AFP.get(6 * 1024).rearrange("p (g t) -> p g t", g=6)
            DB = AFP.get(6 * 1024).rearrange("p (g t) -> p g t", g=6)
            MIXv = MIX.rearrange("(h p) u -> p h u", p=128)
            scnt = [0]
            acnt = [0]

            def s_bank():
                b = scnt[0] % 4; scnt[0] += 1
                return b

            def exp_to(pt_i, b, ncol, colkey, kpart=128, view=None):
                ci = COLIDX[colkey]
                if view is None:
                    o_ap, i_ap = pT[pt_i][0:kpart, 0:ncol], ps[b][0:kpart, 0:ncol]
                else:
                    o_ap, i_ap = view(pT[pt_i][0:kpart, :]), view(ps[b][0:kpart, :])
                P.add("act", lambda e: e.activation(out=o_ap, in_=i_ap, func=AF.Exp, bias=cols[0:kpart, ci:ci + 1], scale=1.0),
                      reads=[f"ps{b}"], writes=[f"pT{pt_i}"])

            pcnt = [0]

            def p_slot():
                s = pcnt[0] % 3; pcnt[0] += 1
                return s

            LOOK = 2
            fifo = []

            def step(s_fn, pv_fn):
                s_fn()
                fifo.append(pv_fn)
                while len(fifo) > LOOK:
                    fifo.pop(0)()

            def flush():
                while fifo:
                    fifo.pop(0)()

            def tri_of(j):
                return triGE if j == 0 else triLE

            def loadB(m):
                ub = 1024 * m
                P.add("sp", lambda e, ub=ub: e.dma_start(out=qb0_t, in_=S["qb0"][:, :, ub:ub + 1024].rearrange("h p t -> p h t")),
                      writes=["qb0_t"], dkey="qb0_t")
                P.add("sp", lambda e, ub=ub: e.dma_start(out=kb0_t, in_=S["kb0"][:, :, ub - 64:ub + 1088].rearrange("h p t -> p h t")),
                      writes=["kb0_t"], dkey="kb0_t")
                P.add("sp", lambda e, ub=ub: e.dma_start(
                    out=vb0_t, in_=Vs[ub - 64:ub + 1088, 256:512].rearrange("(j p) (h d) -> p j h d", p=128, h=2)),
                    writes=["vb0_t"], dkey="vb0_t")
                J1 = 256 * m
                for hh in range(2):
                    P.add("sp", lambda e, hh=hh, J1=J1: e.dma_start(out=qb1_t[:, hh], in_=S["qb1"][hh][:, :, J1:J1 + 256].rearrange("r p t -> p r t")),
                          writes=[f"qb1_t{hh}"], dkey=f"qb1_t{hh}")
                    P.add("sp", lambda e, hh=hh, J1=J1: e.dma_start(out=kb1_t[:, hh], in_=S["kb1"][hh][:, :, J1 - 64:J1 + 320].rearrange("r p t -> p r t")),
                          writes=[f"kb1_t{hh}"], dkey=f"kb1_t{hh}")
                for r in range(4):
                    t0 = 4 * (J1 - 64) + r
                    P.add("sp", lambda e, r=r, t0=t0: e.dma_start(
                        out=vb1_t[:, r], in_=Vs[t0:t0 + 4 * 384 - 3:4, 512:768].rearrange("(j p) (h d) -> p j h d", p=128, h=2)),
                        writes=[f"vb1_t{r}"], dkey=f"vb1_t{r}")
                J2 = 64 * m
                for hh in range(2):
                    P.add("sp", lambda e, hh=hh, J2=J2: e.dma_start(out=qb2_t[:, hh], in_=S["qb2"][hh][:, :, J2:J2 + 64].rearrange("r p t -> p r t")),
                          writes=[f"qb2_t{hh}"], dkey=f"qb2_t{hh}")
                    P.add("sp", lambda e, hh=hh, J2=J2: e.dma_start(out=kb2_t[:, hh], in_=S["kb2"][hh][:, :, J2 - 64:J2 + 128].rearrange("r p t -> p r t")),
                          writes=[f"kb2_t{hh}"], dkey=f"kb2_t{hh}")
                t0 = 16 * (J2 - 64)
                P.add("sp", lambda e, t0=t0: e.dma_start(
                    out=vb2a_t, in_=Vs[t0:t0 + 2048, 768:1024].rearrange("(p r) (h d) -> p r h d", r=16, h=2)),
                    writes=["vb2a_t"], dkey="vb2a_t")
                t1_ = 16 * (J2 + 64)
                P.add("sp", lambda e, t1_=t1_: e.dma_start(
                    out=vb2b_t[0:64], in_=Vs[t1_:t1_ + 1024, 768:1024].rearrange("(p r) (h d) -> p r h d", r=16, h=2)),
                    writes=["vb2b_t"], dkey="vb2b_t")


            def loadA(m, half):
                ubh = 1024 * m + 512 * half
                P.add("sp", lambda e, ubh=ubh: e.dma_start(out=qa_t, in_=S["qa"][:, :, ubh:ubh + 512].rearrange("h p t -> p h t")),
                      writes=["qa_t"], dkey="qa_t")
                P.add("sp", lambda e, ubh=ubh: e.dma_start(out=ka_t, in_=S["ka"][:, :, ubh - 128:ubh + 640].rearrange("h p t -> p h t")),
                      writes=["ka_t"], dkey="ka_t")
                P.add("sp", lambda e, ubh=ubh: e.dma_start(
                    out=va_t, in_=Vs[ubh - 128:ubh + 640, 0:256].rearrange("(j p) (h d) -> p j h d", p=128, h=2)),
                    writes=["va_t"], dkey="va_t")

            def computeA(m, half):
                ub = 1024 * m
                ubh = ub + 512 * half
                for i in range(4):
                    qb = (ubh // 128) + i
                    for g in range(2):
                        ob = 4 + (acnt[0] % 2); db = 6 + (acnt[0] % 2); acnt[0] += 1
                        rd = acnt[0] % 2
                        for j in range(3):
                            b = s_bank(); pi = p_slot()

                            def s_fn(b=b, pi=pi, g=g, i=i, j=j, qb=qb):
                                if j != 1:
                                    tri = tri_of(j)
                                    P.add("pe", lambda e: e.matmul(ps[b][:, 0:384], lhsT=ident, rhs=tri[:, 0:384], start=True, stop=False),
                                          reads=["ident", "triGE", "triLE"], writes=[f"ps{b}"])
                                P.add("pe", lambda e: e.matmul(
                                    ps[b][:, 0:384], lhsT=ka_t[:, g, (i + j) * 128:(i + j + 1) * 128],
                                    rhs=qa_t[:, 3 * g:3 * g + 3, i * 128:(i + 1) * 128], start=(j == 1), stop=True),
                                    reads=["ka_t", "qa_t"], writes=[f"ps{b}"])
                                exp_to(pi, b, 384, ("A", qb, j))

                            def pv_fn(pi=pi, g=g, i=i, j=j, ob=ob, db=db, rd=rd, ubh=ubh):
                                P.add("pe", lambda e: e.matmul(
                                    ps[ob][:, 0:384], lhsT=va_t[:, i + j, g, :], rhs=pT[pi][:, 0:384], start=(j == 0), stop=(j == 2)),
                                    reads=["va_t", f"pT{pi}"], writes=[f"ps{ob}"])
                                P.add("pe", lambda e: e.matmul(
                                    ps[db][:, 0:384], lhsT=ones, rhs=pT[pi][:, 0:384], start=(j == 0), stop=(j == 2)),
                                    reads=["ones", f"pT{pi}"], writes=[f"ps{db}"])
                                if j == 2:
                                    P.add("dve", lambda e: e.tensor_tensor(
                                        out=rden[rd][:, 0:384], in0=ps[db][:, 0:384], in1=esinkb[:, 384 * g:384 * g + 384], op=ALU.add),
                                        reads=[f"ps{db}", "esinkb"], writes=[f"rden{rd}"])
                                    P.add("dve", lambda e: e.reciprocal(out=rden[rd][:, 0:384], in_=rden[rd][:, 0:384]),
                                          reads=[f"rden{rd}"], writes=[f"rden{rd}"])
                                    P.add("dve", lambda e: e.tensor_tensor(
                                        out=oA[:, 3 * g:3 * g + 3, i * 128:(i + 1) * 128],
                                        in0=ps[ob][:, 0:384].rearrange("p (h t) -> p h t", h=3),
                                        in1=rden[rd][:, 0:384].rearrange("p (h t) -> p h t", h=3), op=ALU.mult),
                                        reads=[f"ps{ob}", f"rden{rd}"], writes=["oA"])
                                    if i == 3 and g == 1:
                                        P.add("sp", lambda e: e.dma_start(out=MIXv[:, 0:6, ubh:ubh + 512], in_=oA), reads=["oA"], dkey="oA")
                            step(s_fn, pv_fn)

            def computeB(m):
                ub = 1024 * m
                for i in range(8):
                    qb = (ub // 128) + i
                    ob = 4 + (acnt[0] % 2); db = 6 + (acnt[0] % 2); acnt[0] += 1
                    for j in range(2):
                        b = s_bank(); pi = p_slot()

                        def s_fn(b=b, pi=pi, i=i, j=j, qb=qb):
                            tri = tri_of(j)
                            P.add("pe", lambda e: e.matmul(ps[b][:, 0:256], lhsT=ident, rhs=tri[:, 0:256], start=True, stop=False),
                                  reads=["ident", "triGE", "triLE"], writes=[f"ps{b}"])
                            for hh in range(2):
                                P.add("pe", lambda e, hh=hh: e.matmul(
                                    ps[b][:, hh * 128:(hh + 1) * 128], lhsT=kb0_t[:, hh, (i + j) * 128:(i + j + 1) * 128],
                                    rhs=qb0_t[:, hh, i * 128:(i + 1) * 128], start=False, stop=(hh == 1)),
                                    reads=["kb0_t", "qb0_t"], writes=[f"ps{b}"])
                            exp_to(pi, b, 256, ("B0", qb, j))

                        def pv_fn(pi=pi, i=i, j=j, ob=ob, db=db):
                            for hh in range(2):
                                P.add("pe", lambda e, hh=hh: e.matmul(
                                    ps[ob][:, hh * 128:(hh + 1) * 128], lhsT=vb0_t[:, i + j, hh, :], rhs=pT[pi][:, hh * 128:(hh + 1) * 128],
                                    start=(j == 0 and hh == 0), stop=(j == 1 and hh == 1)), reads=["vb0_t", f"pT{pi}"], writes=[f"ps{ob}"])
                            P.add("pe", lambda e: e.matmul(ps[db][:, 0:256], lhsT=ones, rhs=pT[pi][:, 0:256], start=(j == 0), stop=(j == 1)),
                                  reads=["ones", f"pT{pi}"], writes=[f"ps{db}"])
                            if j == 1:
                                P.add("act", lambda e: e.activation(
                                    out=OB[:, 0:2, i * 128:(i + 1) * 128], in_=ps[ob][:, 0:256].rearrange("p (h t) -> p h t", h=2), func=AF.Copy),
                                    reads=[f"ps{ob}"], writes=["OB0"])
                                P.add("dve", lambda e: e.tensor_copy(
                                    out=DB[:, 0:2, i * 128:(i + 1) * 128], in_=ps[db][:, 0:256].rearrange("p (h t) -> p h t", h=2)),
                                    reads=[f"ps{db}"], writes=["DB0"])
                        step(s_fn, pv_fn)

                for qh in range(2):
                    for r in range(4):
                        ob = 4 + (acnt[0] % 2); db = 6 + (acnt[0] % 2); acnt[0] += 1
                        o0 = 512 * qh + r
                        for j in range(2):
                            b = s_bank(); pi = p_slot()
                            k0 = 128 * qh + 128 * j

                            def s_fn(b=b, pi=pi, r=r, j=j, qh=qh, k0=k0):
                                tri = tri_of(j)
                                P.add("pe", lambda e: e.matmul(ps[b][:, 0:256], lhsT=ident, rhs=tri[:, 0:256], start=True, stop=False),
                                      reads=["ident", "triGE", "triLE"], writes=[f"ps{b}"])
                                for hh in range(2):
                                    P.add("pe", lambda e, hh=hh: e.matmul(
                                        ps[b][:, hh * 128:(hh + 1) * 128], lhsT=kb1_t[:, hh, r, k0:k0 + 128],
                                        rhs=qb1_t[:, hh, r, qh * 128:(qh + 1) * 128], start=False, stop=(hh == 1)),
                                        reads=[f"kb1_t{hh}", f"qb1_t{hh}"], writes=[f"ps{b}"])
                                exp_to(pi, b, 256, ("B1", m, qh, j))

                            def pv_fn(pi=pi, r=r, j=j, qh=qh, ob=ob, db=db, o0=o0):
                                for hh in range(2):
                                    P.add("pe", lambda e, hh=hh: e.matmul(
                                        ps[ob][:, hh * 128:(hh + 1) * 128], lhsT=vb1_t[:, r, qh + j, hh, :],
                                        rhs=pT[pi][:, hh * 128:(hh + 1) * 128], start=(j == 0 and hh == 0), stop=(j == 1 and hh == 1)),
                                        reads=[f"vb1_t{r}", f"pT{pi}"], writes=[f"ps{ob}"])
                                P.add("pe", lambda e: e.matmul(ps[db][:, 0:256], lhsT=ones, rhs=pT[pi][:, 0:256], start=(j == 0), stop=(j == 1)),
                                      reads=["ones", f"pT{pi}"], writes=[f"ps{db}"])
                                if j == 1:
                                    P.add("act", lambda e: e.activation(
                                        out=OB[:, 2:4, o0:o0 + 509:4], in_=ps[ob][:, 0:256].rearrange("p (h t) -> p h t", h=2), func=AF.Copy),
                                        reads=[f"ps{ob}"], writes=["OB1"])
                                    P.add("dve", lambda e: e.tensor_copy(
                                        out=DB[:, 2:4, o0:o0 + 509:4], in_=ps[db][:, 0:256].rearrange("p (h t) -> p h t", h=2)),
                                        reads=[f"ps{db}"], writes=["DB1"])
                            step(s_fn, pv_fn)

                for rg4 in range(4):
                    ob = 4 + (acnt[0] % 2); db = 6 + (acnt[0] % 2); acnt[0] += 1
                    for j in range(2):
                        kp = 128 if j == 0 else 64
                        b = s_bank(); pi = p_slot()

                        def s_fn(b=b, pi=pi, j=j, kp=kp, rg4=rg4):
                            tri = tri_of(j)
                            for half in range(2):
                                P.add("pe", lambda e, half=half: e.matmul(
                                    ps[b][0:kp, 256 * half:256 * half + 256].rearrange("p (a t) -> p a t", a=4),
                                    lhsT=ident[0:kp, 0:kp], rhs=tri[0:kp, :].rearrange("p (a t) -> p a t", a=4)[:, :, 0:64],
                                    start=(half == 0), stop=False),
                                    reads=["ident", "triGE", "triLE"], writes=[f"ps{b}"])
                            for rr in range(4):
                                r = rg4 * 4 + rr
                                for hh in range(2):
                                    c0 = (rr * 2 + hh) * 64
                                    P.add("pe", lambda e, hh=hh, r=r, c0=c0, rr=rr: e.matmul(
                                        ps[b][0:kp, c0:c0 + 64], lhsT=kb2_t[:, hh, r, 128 * j:128 * j + kp],
                                        rhs=qb2_t[:, hh, r, :], start=False, stop=(rr == 3 and hh == 1)),
                                        reads=[f"kb2_t{hh}", f"qb2_t{hh}"], writes=[f"ps{b}"])
                            exp_to(pi, b, 512, ("B2", m, j), kpart=kp)

                        def pv_fn(pi=pi, j=j, kp=kp, rg4=rg4, ob=ob, db=db):
                            vt = vb2a_t if j == 0 else vb2b_t
                            for rr in range(4):
                                r = rg4 * 4 + rr
                                for hh in range(2):
                                    c0 = (rr * 2 + hh) * 64
                                    P.add("pe", lambda e, hh=hh, r=r, c0=c0: e.matmul(
                                        ps[ob][:, c0:c0 + 64], lhsT=vt[0:kp, r, hh, :], rhs=pT[pi][0:kp, c0:c0 + 64],
                                        start=(j == 0 and c0 == 0), stop=(j == 1 and c0 == 448)),
                                        reads=["vb2a_t", "vb2b_t", f"pT{pi}"], writes=[f"ps{ob}"])
                            P.add("pe", lambda e: e.matmul(ps[db][:, :], lhsT=ones[0:kp, :], rhs=pT[pi][0:kp, :], start=(j == 0), stop=(j == 1)),
                                  reads=["ones", f"pT{pi}"], writes=[f"ps{db}"])
                            if j == 1:
                                for hh in range(2):
                                    src_o = ps[ob][:, :].rearrange("p (rr h t) -> p rr h t", rr=4, h=2)[:, :, hh, :]
                                    src_d = ps[db][:, :].rearrange("p (rr h t) -> p rr h t", rr=4, h=2)[:, :, hh, :]
                                    dst_o = OB[:, 4 + hh, :].rearrange("p (t r) -> p r t", r=16)[:, rg4 * 4:rg4 * 4 + 4, :]
                                    dst_d = DB[:, 4 + hh, :].rearrange("p (t r) -> p r t", r=16)[:, rg4 * 4:rg4 * 4 + 4, :]
                                    P.add("act", lambda e, src_o=src_o, dst_o=dst_o: e.activation(out=dst_o, in_=src_o, func=AF.Copy),
                                          reads=[f"ps{ob}"], writes=["OB2"])
                                    P.add("dve", lambda e, src_d=src_d, dst_d=dst_d: e.tensor_copy(out=dst_d, in_=src_d),
                                          reads=[f"ps{db}"], writes=["DB2"])
                        step(s_fn, pv_fn)

                def combine(ub=ub):
                    for hh in range(2):
                        P.add("dve", lambda e, hh=hh: e.tensor_tensor(out=DB[:, hh, :], in0=DB[:, hh, :], in1=DB[:, 2 + hh, :], op=ALU.add),
                              reads=["DB0", "DB1"], writes=["DB0"])
                        P.add("dve", lambda e, hh=hh: e.tensor_tensor(out=DB[:, hh, :], in0=DB[:, hh, :], in1=DB[:, 4 + hh, :], op=ALU.add),
                              reads=["DB0", "DB2"], writes=["DB0"])
                        P.add("dve", lambda e, hh=hh: e.reciprocal(out=DB[:, hh, :], in_=DB[:, hh, :]), reads=["DB0"], writes=["DB0"])
                        for g in range(3):
                            eng = "dve"
                            P.add(eng, lambda e, hh=hh, g=g: e.tensor_tensor(out=oBb[:, 2 * g + hh, :], in0=OB[:, 2 * g + hh, :], in1=DB[:, hh, :], op=ALU.mult),
                                  reads=["DB0", f"OB{g}"], writes=["oBb"])
                    P.add("sp", lambda e: e.dma_start(out=MIXv[:, 6:12, ub:ub + 1024], in_=oBb), reads=["oBb"], dkey="oBb")
                step(lambda: None, combine)


            def loadC(m, half):
                ubh = 1024 * m + 512 * half
                P.add("sp", lambda e, ubh=ubh: e.dma_start(out=qc_t, in_=S["qc"][:, :, ubh:ubh + 512].rearrange("h p t -> p h t")),
                      writes=["qc_t"], dkey="qc_t")
                P.add("sp", lambda e, ubh=ubh: e.dma_start(out=kc_t, in_=S["kc"][:, :, ubh - 384:ubh + 896].rearrange("h p t -> p h t")),
                      writes=["kc_t"], dkey="kc_t")
                for jj in range(2):
                    P.add("sp", lambda e, ubh=ubh, jj=jj: e.dma_start(
                        out=vc_t[:, 5 * jj:5 * jj + 5],
                        in_=Vs[ubh - 384 + 640 * jj:ubh - 384 + 640 * (jj + 1), 1024:1536].rearrange("(j p) (h d) -> p j h d", p=128, h=4)),
                        writes=[f"vc_t{jj}"], dkey=f"vc_t{jj}")

            def computeC(m, half):
                ub = 1024 * m
                ubh = ub + 512 * half
                for i in range(4):
                    qb = (ubh // 128) + i
                    ob = 4 + (acnt[0] % 2); db = 6 + (acnt[0] % 2); acnt[0] += 1
                    rd = acnt[0] % 2
                    for dt in range(7):
                        b = s_bank(); pi = p_slot()
                        kt = i + dt

                        def s_fn(b=b, pi=pi, i=i, dt=dt, kt=kt, qb=qb):
                            P.add("pe", lambda e: e.matmul(ps[b][:, :], lhsT=ident, rhs=tabc[:, dt * 512:(dt + 1) * 512], start=True, stop=False),
                                  reads=["ident", "tabc"], writes=[f"ps{b}"])
                            for h in range(4):
                                P.add("pe", lambda e, h=h: e.matmul(
                                    ps[b][:, h * 128:(h + 1) * 128], lhsT=kc_t[:, h, kt * 128:(kt + 1) * 128],
                                    rhs=qc_t[:, h, i * 128:(i + 1) * 128], start=False, stop=(h == 3)),
                                    reads=["kc_t", "qc_t"], writes=[f"ps{b}"])
                            for qh in range(2):
                                exp_to(pi, b, 0, ("C", qb, dt, qh),
                                       view=lambda a, qh=qh: a.rearrange("p (h t) -> p h t", h=4)[:, :, qh * 64:(qh + 1) * 64])

                        def pv_fn(pi=pi, i=i, dt=dt, kt=kt, ob=ob, db=db, rd=rd, ubh=ubh):
                            for h in range(4):
                                P.add("pe", lambda e, h=h: e.matmul(
                                    ps[ob][:, h * 128:(h + 1) * 128], lhsT=vc_t[:, kt, h, :], rhs=pT[pi][:, h * 128:(h + 1) * 128],
                                    start=(dt == 0 and h == 0), stop=(dt == 6 and h == 3)),
                                    reads=["vc_t0", "vc_t1", f"pT{pi}"], writes=[f"ps{ob}"])
                            P.add("pe", lambda e: e.matmul(ps[db][:, :], lhsT=ones, rhs=pT[pi][:, :], start=(dt == 0), stop=(dt == 6)),
                                  reads=["ones", f"pT{pi}"], writes=[f"ps{db}"])
                            if dt == 6:
                                P.add("dve", lambda e: e.reciprocal(out=rden[rd], in_=ps[db][:, :]), reads=[f"ps{db}"], writes=[f"rden{rd}"])
                                P.add("dve", lambda e: e.tensor_tensor(
                                    out=oC[:, :, i * 128:(i + 1) * 128], in0=ps[ob][:, :].rearrange("p (h t) -> p h t", h=4),
                                    in1=rden[rd].rearrange("p (h t) -> p h t", h=4), op=ALU.mult),
                                    reads=[f"ps{ob}", f"rden{rd}"], writes=["oC"])
                                if i == 3:
                                    P.add("sp", lambda e: e.dma_start(out=MIXv[:, 12:16, ubh:ubh + 512], in_=oC), reads=["oC"], dkey="oC")
                        step(s_fn, pv_fn)

            ms = list(range(q_lo // 1024, q_hi // 1024))
            loadB(ms[0]); loadA(ms[0], 0)
            computeA(ms[0], 0)
            for idx, m in enumerate(ms):
                nxt = ms[idx + 1] if idx + 1 < len(ms) else None
                flush(); loadA(m, 1)
                if idx > 0:
                    computeC(ms[idx - 1], 1)
                flush(); loadC(m, 0)
                computeB(m)
                computeA(m, 1)
                flush()
                if nxt is not None:
                    loadB(nxt); loadA(nxt, 0)
                computeC(m, 0)
                flush(); loadC(m, 1)
                if nxt is not None:
                    computeA(nxt, 0)
            computeC(ms[-1], 1)
            flush()
            P.barrier()
            if stop == ("T", l):
                return True

            while len(convq) > (NJ1 if l == 0 else 0):
                conv_some(1)
            AB.o, AFP.o = ab_base, af_base
            x1 = AFP.get(16 * TT).rearrange("p (c t) -> p c t", c=16)
            rstd2 = AFP.get(TT)
            xs_t = [AFP.get(TT) for _ in range(3)]
            ost = [AFP.get(TT) for _ in range(3)]
            sg = [AFP.get(TT) for _ in range(3)]
            mix_t = AB.get(16 * TT).rearrange("p (c t) -> p c t", c=16)
            xn2 = AB.get(16 * TT).rearrange("p (c t) -> p c t", c=16)
            h_t = AB.get(NF * TT).rearrange("p (f t) -> p f t", f=NF)
            sq2 = h_t
            wo_t = [AB.get(2048).rearrange("p (c m) -> p c m", c=16) for _ in range(2)]
            wgu_t = [AB.get(2048).rearrange("p (c m) -> p c m", c=16) for _ in range(4)]
            wd_t = [AB.get(DFF).rearrange("p (f m) -> p f m", f=NF) for _ in range(2)]
            xres = xin[l].rearrange("(c p) u -> p c u", p=128)
            gffn = l * 32 + 16
            woc = 0; wgc = 0; wdc = 0; pc = 0; xc = 0; oc = 0; sc = 0
            for t in range(q_lo // TT, q_hi // TT):
                u0 = t * TT
                P.add("sp", lambda e, u0=u0: e.dma_start(out=mix_t, in_=MIXv[:, :, u0:u0 + TT]), writes=["mix_t"], dkey="mix_t")
                for o in range(16):
                    ws = woc % 2; woc += 1
                    P.add("sp", lambda e, ws=ws, o=o: e.dma_start(out=wo_t[ws], in_=WOUTb[l, o].rearrange("p (c m) -> p c m", c=16)),
                          reads=[f"D_wout{l}"], writes=[f"wo{ws}"], dkey=f"wo{ws}")
                    xi = xc % 3; xc += 1
                    P.add("sp", lambda e, xi=xi, o=o, u0=u0: e.dma_start(out=xs_t[xi], in_=xres[:, o, u0:u0 + TT]),
                          writes=[f"xs{xi}"], dkey=f"xs{xi}")
                    pb = pc % 8; pc += 1
                    for c in range(16):
                        P.add("pe", lambda e, c=c, ws=ws, pb=pb: e.matmul(ps[pb][:], lhsT=wo_t[ws][:, c, :], rhs=mix_t[:, c, :],
                                                                          start=(c == 0), stop=(c == 15)),
                              reads=[f"wo{ws}", "mix_t"], writes=[f"ps{pb}"])
                    P.add("dve", lambda e, pb=pb, xi=xi, o=o: e.tensor_tensor(out=x1[:, o, :], in0=ps[pb][:], in1=xs_t[xi], op=ALU.add),
                          reads=[f"ps{pb}", f"xs{xi}"], writes=[f"x1_{o}"])
                    P.add("act", lambda e, o=o: e.activation(out=sq2[:, o, :], in_=x1[:, o, :], func=AF.Square),
                          reads=[f"x1_{o}"], writes=[f"h{o}"])
                pb = pc % 8; pc += 1
                for o in range(16):
                    P.add("pe", lambda e, o=o, pb=pb: e.matmul(ps[pb][:], lhsT=ones, rhs=sq2[:, o, :], start=(o == 0), stop=(o == 15)),
                          reads=[f"h{o}", "ones"], writes=[f"ps{pb}"])
                P.add("act", lambda e, pb=pb: e.activation(out=rstd2, in_=ps[pb][:], func=AF.Sqrt, bias=float(EPS), scale=1.0 / D),
                      reads=[f"ps{pb}"], writes=["rstd2"])
                P.add("dve", lambda e: e.reciprocal(out=rstd2, in_=rstd2), reads=["rstd2"], writes=["rstd2"])
                for c in range(16):
                    eng = "dve"
                    P.add(eng, lambda e, c=c: e.scalar_tensor_tensor(
                        out=xn2[:, c, :], in0=x1[:, c, :], scalar=gn[:, gffn + c:gffn + c + 1], in1=rstd2,
                        op0=ALU.mult, op1=ALU.mult), reads=[f"x1_{c}", "rstd2"], writes=[f"xn2_{c}"])
                for f in range(NF):
                    if f % 3 == 1:
                        conv_some(1)
                    pbs = []
                    for wi, Wsrc in enumerate((WGb, WUb)):
                        ws = wgc % 4; wgc += 1
                        dk_ = f"D_wg{l}" if wi == 0 else f"D_wu{l}"
                        P.add("sp", lambda e, ws=ws, f=f, Wsrc=Wsrc: e.dma_start(out=wgu_t[ws], in_=Wsrc[l, f].rearrange("p (c m) -> p c m", c=16)),
                              reads=[dk_], writes=[f"wgu{ws}"], dkey=f"wgu{ws}")
                        pb = pc % 8; pc += 1
                        pbs.append(pb)
                        for c in range(16):
                            P.add("pe", lambda e, c=c, ws=ws, pb=pb: e.matmul(ps[pb][:], lhsT=wgu_t[ws][:, c, :], rhs=xn2[:, c, :],
                                                                              start=(c == 0), stop=(c == 15)),
                                  reads=[f"wgu{ws}", f"xn2_{c}"], writes=[f"ps{pb}"])
                    si = sc % 3; sc += 1
                    P.add("act", lambda e, si=si, pb=pbs[0]: e.activation(out=sg[si], in_=ps[pb][:], func=AF.Silu),
                          reads=[f"ps{pbs[0]}"], writes=[f"sg{si}"])
                    P.add("dve", lambda e, si=si, pb=pbs[1], f=f: e.tensor_tensor(out=h_t[:, f, :], in0=ps[pb][:], in1=sg[si], op=ALU.mult),
                          reads=[f"ps{pbs[1]}", f"sg{si}"], writes=[f"h{f}"])
                for o in range(16):
                    ws = wdc % 2; wdc += 1
                    P.add("sp", lambda e, ws=ws, o=o: e.dma_start(out=wd_t[ws], in_=WDb[l, o].rearrange("p (f m) -> p f m", f=NF)),
                          reads=[f"D_wd{l}"], writes=[f"wd{ws}"], dkey=f"wd{ws}")
                    pb = pc % 8; pc += 1
                    for f in range(NF):
                        P.add("pe", lambda e, f=f, ws=ws, pb=pb: e.matmul(ps[pb][:], lhsT=wd_t[ws][:, f, :], rhs=h_t[:, f, :],
                                                                          start=(f == 0), stop=(f == NF - 1)),
                              reads=[f"wd{ws}", f"h{f}"], writes=[f"ps{pb}"])
                    oi = oc % 3; oc += 1
                    P.add("dve", lambda e, pb=pb, oi=oi, o=o: e.tensor_tensor(out=ost[oi], in0=ps[pb][:], in1=x1[:, o, :], op=ALU.add),
                          reads=[f"ps{pb}", f"x1_{o}"], writes=[f"ost{oi}"])
                    if l == 0:
                        dst = X1.rearrange("(c p) u -> p c u", p=128)[:, o, u0:u0 + TT]
                    else:
                        dst = yT.rearrange("(c p) u -> p c u", p=128)[:, o, u0 - 2 * HALO:u0 - 2 * HALO + TT]
                    P.add("pool", lambda e, oi=oi, dst=dst: e.dma_start(out=dst, in_=ost[oi]), reads=[f"ost{oi}"], dkey=f"ost{oi}")
            P.barrier()
            return stop == ("F", l)

        for l_ in range(2):
            if emit_layer(l_):
                break

        with nc.Block() as block:
            P.emit(nc, block)
    return nc


def _blk(w, cols):
    K = w.shape[0]
    return np.ascontiguousarray(w.reshape(K // 128, 128, -1).transpose(1, 0, 2).reshape(128, -1))


def prepare_inputs(x_prompt, x_sample, norm_mix, w_in, qk_norm, sink_a, rpb_c, w_out, norm_ffn, w_gate, w_up, w_down):
    f32 = np.float32
    xs = np.concatenate([np.asarray(x_prompt, f32).reshape(-1, D), np.asarray(x_sample, f32).reshape(-1, D)], axis=0)
    NTOK = xs.shape[0]
    xpad = np.zeros((NTOK + 4 * HALO, D), f32)
    xpad[2 * HALO:2 * HALO + NTOK] = xs
    w_in = np.asarray(w_in, f32); w_out = np.asarray(w_out, f32)
    w_gate = np.asarray(w_gate, f32); w_up = np.asarray(w_up, f32); w_down = np.asarray(w_down, f32)
    cg = {"qa": 0, "ka": 768, "va": 1024, "qb": 1280, "kb": 2048, "vb": 2816, "qc": 3584, "kc": 4096, "vc": 4608}
    WQK = np.zeros((2, 28, 128, 2048), f32)
    WV = np.zeros((2, 3, 128, 8192), f32)
    WOUT = np.zeros((2, 16, 128, 2048), f32)
    WG = np.zeros((2, NF, 128, 2048), f32)
    WU = np.zeros((2, NF, 128, 2048), f32)
    WD = np.zeros((2, 16, 128, DFF), f32)
    for l in range(2):
        for b, (nm, hi) in enumerate(QKBLKS):
            c0 = cg[nm] + hi * 128
            WQK[l, b] = _blk(w_in[l][:, c0:c0 + 128], 128)
        wvv = np.concatenate([w_in[l][:, 1024:1280], w_in[l][:, 2816:3584], w_in[l][:, 4608:5120]], axis=1)
        for b in range(3):
            WV[l, b] = _blk(wvv[:, b * 512:(b + 1) * 512], 512)
        for o in range(16):
            WOUT[l, o] = _blk(w_out[l][:, o * 128:(o + 1) * 128], 128)
            WD[l, o] = _blk(w_down[l][:, o * 128:(o + 1) * 128], 128)
        for f in range(NF):
            WG[l, f] = _blk(w_gate[l][:, f * 128:(f + 1) * 128], 128)
            WU[l, f] = _blk(w_up[l][:, f * 128:(f + 1) * 128], 128)
    GN = np.zeros((128, 64), f32)
    for l in range(2):
        GN[:, l * 32:l * 32 + 16] = np.asarray(norm_mix, f32)[l].reshape(16, 128).T
        GN[:, l * 32 + 16:l * 32 + 32] = np.asarray(norm_ffn, f32)[l].reshape(16, 128).T
    GQK = np.ascontiguousarray(np.asarray(qk_norm, f32).reshape(12, 128).T)
    SINK = np.ascontiguousarray(np.broadcast_to(np.asarray(sink_a, f32).reshape(1, 12), (128, 12)))
    cidx = _ctab_index()
    TABC = np.zeros((2, 128, 7 * 512), f32)
    for l in range(2):
        ext = np.concatenate([np.asarray(rpb_c, f32)[l].reshape(-1), np.array([NEGM], f32)])
        tab = ext[cidx]
        TABC[l] = tab.transpose(1, 0, 2, 3).reshape(128, 7 * 512)
    k = np.arange(128)[:, None]; q = np.arange(128)[None, :]
    CONST = np.zeros((128, 512), f32)
    CONST[:, 0:128] = np.eye(128, dtype=f32)
    CONST[:, 128:256] = np.where(k >= q, 0.0, NEGM)
    CONST[:, 256:384] = np.where(k <= q, 0.0, NEGM)
    for m_ in range(16):
        CONST[m_ + 16, 384 + m_] = -1.0
        CONST[m_, 384 + 16 + m_] = 1.0
    inv = (ROPE_THETA ** (-np.arange(0, 32, 2, dtype=np.float32) / 32)).astype(np.float32)
    pos = np.arange(SEQ, dtype=np.float32)
    ang = pos[:, None] * inv[None, :]
    cos_t = np.cos(ang).astype(f32).T
    sin_t = np.sin(ang).astype(f32).T
    common = dict(WQK=WQK, WV=WV, WOUT=WOUT, WG=WG, WU=WU, WD=WD, GN=GN, GQK=GQK, SINK=SINK, TABC=TABC, CONST=CONST)
    in_maps = []
    for c in range(NCORES):
        g0 = c * OWN
        win = xpad[g0:g0 + U]
        m = dict(common)
        m["xT"] = np.ascontiguousarray(win.T)
        gpos = (np.arange(U) + g0 - 2 * HALO) % SEQ
        COSW = np.ones((128, U), f32)
        COSW[0:16] = cos_t[:, gpos]; COSW[16:32] = cos_t[:, gpos]
        SINW = np.zeros((32, U), f32)
        SINW[0:16] = sin_t[:, gpos]; SINW[16:32] = sin_t[:, gpos]
        m["COSW"] = COSW; m["SINW"] = SINW
        m["COLS"] = _build_cols(c)
        in_maps.append(m)
    return in_maps


_NC_CACHE = {}


def kernel(x_prompt, x_sample, norm_mix, w_in, qk_norm, sink_a, rpb_c, w_out, norm_ffn, w_gate, w_up, w_down):
    in_maps = prepare_inputs(x_prompt, x_sample, norm_mix, w_in, qk_norm, sink_a, rpb_c, w_out, norm_ffn, w_gate, w_up, w_down)
    nc = build_program()
    res = run_bass_kernel_spmd(nc, in_maps, core_ids=list(range(NCORES)))
    ys = [np.asarray(res.results[c]["yT"], np.float32).T for c in range(NCORES)]
    y = np.concatenate(ys, axis=0)
    nb = np.asarray(x_prompt).shape[0] * SEQ
    y_prompt = np.ascontiguousarray(y[:nb].reshape(np.asarray(x_prompt).shape))
    y_sample = np.ascontiguousarray(y[nb:].reshape(np.asarray(x_sample).shape))
    return (y_prompt, y_sample)
```

```python
import numpy as np
import concourse.bass as bass
import concourse.mybir as mybir
from concourse.bass_utils import run_bass_kernel_spmd

F32 = mybir.dt.float32
BF16 = mybir.dt.bfloat16
AF = mybir.ActivationFunctionType
ALU = mybir.AluOpType

NCORES = 8
D = 2048
NSEQ = 3
SEQ = 8192
OWN = 3072
HALO = 1024
U = OWN + 4 * HALO
TT = 512
DFF = 5632
NF = DFF // 128
NC16 = D // 128
EPS = 1e-6
NEGM = -30000.0
GRID_W = 64
ROPE_THETA = 500000.0

REGIONS = [(0, U, HALO, U - HALO), (HALO, U - HALO, 2 * HALO, U - 2 * HALO)]

KBLKS = [("ka", i) for i in range(2)] + [("kb", i) for i in range(6)] + [("kc", i) for i in range(4)]
QBLKS = [("qa", i) for i in range(6)] + [("qb", i) for i in range(6)] + [("qc", i) for i in range(4)]
QKBLKS = KBLKS + QBLKS

SAME_ENGINE_WAITS = True


def _col_index():
    idx = {}
    n = 0
    for qb in range(HALO // 128, (U - HALO) // 128):
        for j in range(3):
            idx[("A", qb, j)] = n; n += 1
    for qb in range(HALO // 128, (U - HALO) // 128):
        for j in range(2):
            idx[("B0", qb, j)] = n; n += 1
    for m in range(1, 6):
        for qh in range(2):
            for j in range(2):
                idx[("B1", m, qh, j)] = n; n += 1
    for m in range(1, 6):
        for j in range(2):
            idx[("B2", m, j)] = n; n += 1
    for qb in range(HALO // 128, (U - HALO) // 128):
        for dt in range(7):
            for qh in range(2):
                idx[("C", qb, dt, qh)] = n; n += 1
    return idx, n


COLIDX, NCOLS = _col_index()


def _seqid(g):
    g = np.asarray(g)
    return np.where((g >= 0) & (g < NSEQ * SEQ), g // SEQ, -1)


def _build_cols(core):
    base = core * OWN - 2 * HALO
    cols = np.zeros((128, NCOLS), np.float32)
    p = np.arange(128)

    def setcol(key, ktok_u, qtok_u, extra_valid=None):
        gq = base + qtok_u
        sq = int(_seqid(gq))
        if sq < 0:
            return
        gk = base + ktok_u
        valid = (_seqid(gk) == sq)
        if extra_valid is not None:
            valid = valid & extra_valid
        cols[:len(valid), COLIDX[key]] = np.where(valid, 0.0, NEGM)

    for qb in range(HALO // 128, (U - HALO) // 128):
        u0 = qb * 128
        for j in range(3):
            setcol(("A", qb, j), u0 + 128 * (j - 1) + p, u0)
        for j in range(2):
            setcol(("B0", qb, j), u0 - 64 + 128 * j + p, u0)
        gq0 = base + u0
        for dt in range(7):
            for qh in range(2):
                rq = (gq0 % SEQ) // GRID_W + qh
                ks = min(max(rq - 4, 0), SEQ // GRID_W - 8)
                ktok = u0 + 128 * (dt - 3) + p
                gk = base + ktok
                kr = (gk % SEQ) // GRID_W
                ev = (kr >= ks) & (kr < ks + 8)
                setcol(("C", qb, dt, qh), ktok, u0, ev)
    for m in range(1, 6):
        for qh in range(2):
            J0 = 256 * m + 128 * qh
            for j in range(2):
                setcol(("B1", m, qh, j), 4 * (J0 - 64 + 128 * j + p), 4 * J0)
        J0 = 64 * m
        setcol(("B2", m, 0), 16 * (J0 - 64 + p), 16 * J0)
        setcol(("B2", m, 1), 16 * (J0 + 64 + p[:64]), 16 * J0)
    return cols


def _ctab_index():
    k = np.arange(128)[:, None]
    q = np.arange(128)[None, :]
    kro, kc = k // 64, k % 64
    qro, qc = q // 64, q % 64
    cstart = np.clip(qc - 8, 0, GRID_W - 16)
    cvalid = (kc >= cstart) & (kc < cstart + 16)
    relc = np.clip(kc - qc + 15, 0, 30)
    out = np.zeros((7, 128, 4, 128), np.int64)
    for dt in range(7):
        dr = 2 * (dt - 3) + kro - qro
        relr = np.clip(dr + 7, 0, 14)
        for h in range(4):
            ii = h * 15 * 31 + relr * 31 + relc
            out[dt, :, h, :] = np.where(cvalid, ii, 4 * 15 * 31)
    return out


class _Op:
    __slots__ = ("eng", "fn", "deps", "dma", "dkey", "sem", "val", "hasdep", "i", "persist")


class Prog:
    ENG = ("pe", "act", "dve", "pool", "sp")

    def __init__(self):
        self.ops = []
        self.lastw = {}
        self.readers = {}
        self.lastop = {}
        self.lastdma = {}

    def add(self, eng, fn, reads=(), writes=(), dkey=None, persist=False):
        op = _Op()
        op.persist = persist
        op.eng, op.fn, op.dma, op.dkey = eng, fn, dkey is not None, dkey
        op.sem = None; op.val = 0; op.hasdep = False; op.i = len(self.ops)
        deps = set()
        for r in reads:
            w = self.lastw.get(r)
            if w is not None:
                deps.add(w)
        for w_ in writes:
            lw = self.lastw.get(w_)
            if lw is not None:
                deps.add(lw)
            for rd in self.readers.get(w_, ()):
                deps.add(rd)
        for r in reads:
            self.readers.setdefault(r, []).append(op)
        for w_ in writes:
            self.lastw[w_] = op
            self.readers[w_] = []
        op.deps = deps
        for d in deps:
            d.hasdep = True
        self.ops.append(op)
        if op.dma:
            self.lastdma[dkey] = op
        else:
            self.lastop[eng] = op
        return op

    def barrier(self):
        pend = set(o for o in (set(self.lastop.values()) | set(self.lastdma.values())) if not o.persist)
        keepw = {k: v for k, v in self.lastw.items() if v.persist}
        keepd = {k: v for k, v in self.lastdma.items() if v.persist}
        for e in self.ENG:
            op = _Op()
            op.persist = False
            op.eng, op.fn, op.dma, op.dkey = e, None, False, None
            op.sem = None; op.val = 0; op.hasdep = False; op.i = len(self.ops)
            op.deps = set(pend)
            for d in pend:
                d.hasdep = True
            self.ops.append(op)
        self.lastw.clear(); self.readers.clear(); self.lastop.clear(); self.lastdma.clear()
        self.lastw.update(keepw); self.lastdma.update(keepd)

    def emit(self, nc, block):
        engs = {"pe": "tensor", "act": "scalar", "dve": "vector", "pool": "gpsimd", "sp": "sync"}
        esem = {e: nc.alloc_semaphore(f"e_{e}") for e in self.ENG}
        ecnt = {e: 0 for e in self.ENG}
        dsem = {}
        dcnt = {}
        for op in self.ops:
            if op.fn is None:
                continue
            if op.dma:
                if op.dkey not in dsem:
                    dsem[op.dkey] = nc.alloc_semaphore("d_" + str(len(dsem)))
                    dcnt[op.dkey] = 0
                dcnt[op.dkey] += 16
                op.sem, op.val = dsem[op.dkey], dcnt[op.dkey]
            elif op.hasdep:
                ecnt[op.eng] += 1
                op.sem, op.val = esem[op.eng], ecnt[op.eng]
        self.nsem = len(dsem) + 5
        per = {e: [o for o in self.ops if o.eng == e] for e in self.ENG}

        def run(e):
            def body(eng):
                waited = {}
                for op in per[e]:
                    for d in sorted(op.deps, key=lambda o: o.i):
                        if d.sem is None:
                            continue
                        if (not d.dma) and d.eng == e and (e == "pe" or not SAME_ENGINE_WAITS):
                            continue
                        k = id(d.sem)
                        if waited.get(k, 0) < d.val:
                            eng.wait_ge(d.sem, d.val)
                            waited[k] = d.val
                    if op.fn is None:
                        continue
                    ins = op.fn(eng)
                    if op.dma:
                        ins.then_inc(op.sem, 16)
                    elif op.hasdep:
                        ins.then_inc(op.sem, 1)
            return body

        for e in self.ENG:
            getattr(block, engs[e])(run(e))


class Arena:
    def __init__(self, t, n):
        self.t, self.n, self.o = t, n, 0

    def reset(self):
        self.o = 0

    def get(self, n):
        assert self.o + n <= self.n, (self.o, n, self.n)
        v = self.t[:, self.o:self.o + n]
        self.o += n
        return v


def build_program(debug=False, stop=None):
    nc = bass.Bass("TRN2", target_bir_lowering=False)
    P = Prog()

    def dram(name, shape, dt=F32, kind="ExternalInput"):
        return nc.dram_tensor(name, list(shape), dt, kind=kind).ap()

    xT = dram("xT", [D, U])
    WQK = dram("WQK", [2, 28, 128, 2048])
    WV = dram("WV", [2, 3, 128, 8192])
    WOUT = dram("WOUT", [2, 16, 128, 2048])
    WG = dram("WG", [2, NF, 128, 2048])
    WU = dram("WU", [2, NF, 128, 2048])
    WD = dram("WD", [2, 16, 128, DFF])
    GN = dram("GN", [128, 2 * 2 * 16])
    GQK = dram("GQK", [128, 2 * 6])
    SINK = dram("SINK", [128, 2 * 6])
    COSW = dram("COSW", [128, U])
    SINW = dram("SINW", [32, U])
    COLS = dram("COLS", [128, NCOLS])
    TABC = dram("TABC", [2, 128, 7 * 512])
    CONST = dram("CONST", [128, 128 * 3 + 128])
    yT = dram("yT", [D, OWN], kind="ExternalOutput")
    ik = "ExternalOutput" if debug else "Internal"
    WQKb = dram("WQKb", [2, 28, 128, 2048], BF16, "Internal")
    WVb = dram("WVb", [2, 3, 128, 8192], BF16, "Internal")
    WOUTb = dram("WOUTb", [2, 16, 128, 2048], BF16, "Internal")
    WGb = dram("WGb", [2, NF, 128, 2048], BF16, "Internal")
    WUb = dram("WUb", [2, NF, 128, 2048], BF16, "Internal")
    WDb = dram("WDb", [2, 16, 128, DFF], BF16, "Internal")
    S = {
        "qa": dram("s_qa", [6, 128, U], BF16, ik), "ka": dram("s_ka", [2, 128, U], BF16, ik),
        "qc": dram("s_qc", [4, 128, U], BF16, ik), "kc": dram("s_kc", [4, 128, U], BF16, ik),
        "qb0": dram("s_qb0", [2, 128, U], BF16, ik), "kb0": dram("s_kb0", [2, 128, U], BF16, ik),
        "qb1": dram("s_qb1", [2, 4, 128, U // 4], BF16, ik), "kb1": dram("s_kb1", [2, 4, 128, U // 4], BF16, ik),
        "qb2": dram("s_qb2", [2, 16, 128, U // 16], BF16, ik), "kb2": dram("s_kb2", [2, 16, 128, U // 16], BF16, ik),
    }
    Vs = dram("s_v", [U, 1536], BF16, ik)
    MIX = dram("s_mix", [D, U], BF16, ik)
    X1 = dram("s_x1", [D, U], F32, ik)

    import contextlib
    es = contextlib.ExitStack()
    with es:
        NB16 = 63000
        NF32 = 15000
        abf_t = es.enter_context(nc.sbuf_tensor("abf", [128, NB16], BF16))
        af_t = es.enter_context(nc.sbuf_tensor("af32", [128, NF32], F32))
        cbf_t = es.enter_context(nc.sbuf_tensor("cbf", [128, 128 * 2 + 512 * 2 + 7 * 512 + 128], BF16))
        cf_t = es.enter_context(nc.sbuf_tensor("cf", [128, NCOLS + 64 + 12 + 12 + 12 + 768 + 512], F32))
        ps = [es.enter_context(nc.psum_tensor(f"ps{i}", [128, 512], F32)) for i in range(8)]
        AB = Arena(abf_t, NB16)
        AFP = Arena(af_t, NF32)
        CB = Arena(cbf_t, 128 * 2 + 512 * 2 + 7 * 512 + 128)
        CF = Arena(cf_t, NCOLS + 64 + 12 + 12 + 12 + 768 + 512)

        ident = CB.get(128)
        ones = CB.get(128)
        triGE = CB.get(512)
        triLE = CB.get(512)
        tabc = CB.get(7 * 512)
        rotT = CB.get(128)
        cols = CF.get(NCOLS)
        gn = CF.get(64)
        gqk = CF.get(12)
        gqs = CF.get(12)
        esink = CF.get(12)
        esinkb = CF.get(768)

        AB.reset(); AFP.reset()
        cst32 = CF.get(512)
        P.add("sp", lambda e: e.dma_start(out=cst32, in_=CONST[:, :]), writes=["cst32"], dkey="cst32")
        P.add("sp", lambda e: e.dma_start(out=cols, in_=COLS[:, :]), writes=["cols"], dkey="cols")
        P.add("sp", lambda e: e.dma_start(out=gn, in_=GN[:, :]), writes=["gn"], dkey="gn")
        P.add("sp", lambda e: e.dma_start(out=gqk, in_=GQK[:, :]), writes=["gqk"], dkey="gqk")
        P.add("sp", lambda e: e.dma_start(out=esink, in_=SINK[:, :]), writes=["esink"], dkey="esink")
        P.add("dve", lambda e: e.tensor_copy(out=ident, in_=cst32[:, 0:128]), reads=["cst32"], writes=["ident"])
        P.add("dve", lambda e: e.memset(ones, 1.0), writes=["ones"])
        for r in range(4):
            P.add("dve", lambda e, r=r: e.tensor_copy(out=triGE[:, r * 128:(r + 1) * 128], in_=cst32[:, 128:256]),
                  reads=["cst32"], writes=["triGE"])
            P.add("dve", lambda e, r=r: e.tensor_copy(out=triLE[:, r * 128:(r + 1) * 128], in_=cst32[:, 256:384]),
                  reads=["cst32"], writes=["triLE"])
        P.add("act", lambda e: e.activation(out=esink, in_=esink, func=AF.Exp), reads=["esink"], writes=["esink"])
        P.add("dve", lambda e: e.tensor_copy(out=gqs, in_=gqk), reads=["gqk"], writes=["gqs"])
        for l in range(2):
            for mx in range(3):
                cc = l * 6 + mx * 2
                P.add("dve", lambda e, cc=cc: e.tensor_scalar(out=gqs[:, cc:cc + 1], in0=gqk[:, cc:cc + 1],
                                                             scalar1=float(128 ** -0.5), scalar2=None, op0=ALU.mult),
                      reads=["gqk", "gqs"], writes=["gqs"])

        convq = []

        def conv(src, dst, nblk, key, step=8):
            for b0 in range(0, nblk, step):
                b1 = min(nblk, b0 + step)
                convq.append(lambda b0=b0, b1=b1, src=src, dst=dst, key=key: P.add(
                    "pool", lambda e: e.dma_start(out=dst[b0:b1], in_=src[b0:b1]), writes=[key], dkey=key, persist=True))
        for l in range(2):
            conv(WQK[l], WQKb[l], 28, f"D_wqk{l}", 2 if l == 0 else 1)
            conv(WV[l], WVb[l], 3, f"D_wv{l}", 1)
            conv(WOUT[l], WOUTb[l], 16, f"D_wout{l}", 1)
            conv(WG[l], WGb[l], NF, f"D_wg{l}", 1)
            conv(WU[l], WUb[l], NF, f"D_wu{l}", 1)
            conv(WD[l], WDb[l], 16, f"D_wd{l}", 1)
        NJ1 = 28 + 3 + 16 + NF + NF + 16

        def conv_some(n):
            for _ in range(n):
                if convq:
                    convq.pop(0)()
        conv_some(17)
        P.barrier()

        xin = [xT, X1]
        xout = [X1, None]

        def emit_layer(l):
            kv_lo, kv_hi, q_lo, q_hi = REGIONS[l]
            AB.reset(); AFP.reset()
            t32 = AFP.get(7 * 512)
            P.add("sp", lambda e, l=l: e.dma_start(out=t32, in_=TABC[l]), writes=["t32"], dkey="t32")
            P.add("dve", lambda e: e.tensor_copy(out=tabc, in_=t32), reads=["t32"], writes=["tabc"])
            for h in range(6):
                P.add("dve", lambda e, h=h, l=l: e.tensor_scalar(
                    out=esinkb[:, h * 128:(h + 1) * 128], in0=cst32[:, 0:128], scalar1=0.0,
                    scalar2=esink[:, l * 6 + h:l * 6 + h + 1], op0=ALU.mult, op1=ALU.add),
                    reads=["cst32", "esink"], writes=["esinkb"])
            rg = [AB.get(128) for _ in range(4)]
            gcols = [l * 6 + 0, l * 6 + 1, l * 6 + 2, l * 6 + 3]
            for j in range(4):
                P.add("dve", lambda e, j=j: e.tensor_scalar(
                    out=rg[j], in0=cst32[:, 384:512], scalar1=gqs[:, gcols[j]:gcols[j] + 1], scalar2=None,
                    op0=ALU.mult), reads=["cst32", "gqs"], writes=[f"rg{j}"])
            P.barrier()
            ab_base, af_base = AB.o, 0
            AFP.o = 0

            xt = AFP.get(16 * TT).rearrange("p (c t) -> p c t", c=16)
            rstd = AFP.get(TT)
            cosb = AFP.get(TT)
            sinb = AFP.get(TT)
            cosg = [AFP.get(TT) for _ in range(4)]
            rsh = [AFP.get(TT) for _ in range(2)]
            t1 = [AFP.get(TT) for _ in range(2)]
            t2 = [AFP.get(TT) for _ in range(2)]
            sq = AB.get(16 * TT).rearrange("p (c t) -> p c t", c=16)
            xn = [AB.get(16 * TT).rearrange("p (c t) -> p c t", c=16) for _ in range(2)]
            wqk = [AB.get(2048).rearrange("p (c m) -> p c m", c=16) for _ in range(4)]
            wv = [AB.get(8192).rearrange("p (c m) -> p c m", c=16) for _ in range(2)]
            sqh = [AB.get(TT) for _ in range(3)]
            qbf = [AB.get(TT) for _ in range(3)]
            outq = [AB.get(TT) for _ in range(4)]
            vst = [AB.get(1536) for _ in range(2)]
            xin_v = xin[l].rearrange("(c p) u -> p c u", p=128)
            gmix = l * 32
            wslot = 0
            vslot = 0
            hcount = 0
            vcount = 0
            pscnt = 0
            tiles = list(range(kv_lo // TT, kv_hi // TT))
            cnt = {"w": 0, "v": 0, "h": 0, "vc": 0, "ps": 0}

            def prepA(ti):
                u0 = tiles[ti] * TT
                xs = ti % 2
                P.add("sp", lambda e: e.dma_start(out=xt, in_=xin_v[:, :, u0:u0 + TT]), writes=["xt"], dkey="xt")
                for c in range(16):
                    P.add("act", lambda e, c=c: e.activation(out=sq[:, c, :], in_=xt[:, c, :], func=AF.Square),
                          reads=["xt"], writes=[f"sq{c}"])
                for c in range(16):
                    P.add("pe", lambda e, c=c: e.matmul(ps[7][:], lhsT=ones, rhs=sq[:, c, :], start=(c == 0), stop=(c == 15)),
                          reads=[f"sq{c}"], writes=["ps7"])
                P.add("act", lambda e: e.activation(out=rstd, in_=ps[7][:], func=AF.Sqrt, bias=float(EPS), scale=1.0 / D),
                      reads=["ps7"], writes=["rstd"])
                P.add("dve", lambda e: e.reciprocal(out=rstd, in_=rstd), reads=["rstd"], writes=["rstd"])
                for c in range(16):
                    P.add("dve", lambda e, c=c: e.scalar_tensor_tensor(
                        out=xn[xs][:, c, :], in0=xt[:, c, :], scalar=gn[:, gmix + c:gmix + c + 1], in1=rstd,
                        op0=ALU.mult, op1=ALU.mult), reads=["xt", "rstd"], writes=[f"xn{xs}_{c}"])

            def prepB(ti):
                u0 = tiles[ti] * TT
                P.add("sp", lambda e: e.dma_start(out=cosb, in_=COSW[:, u0:u0 + TT]), writes=["cosb"], dkey="cosb")
                P.add("sp", lambda e: e.dma_start(out=sinb[0:32, :], in_=SINW[:, u0:u0 + TT]), writes=["sinb"], dkey="sinb")
                for j in range(4):
                    P.add("pool", lambda e, j=j: e.tensor_scalar(out=cosg[j], in0=cosb, scalar1=gqs[:, gcols[j]:gcols[j] + 1],
                                                                 scalar2=None, op0=ALU.mult),
                          reads=["cosb"], writes=[f"cosg{j}"])

            def v_section(ti, vbs=(0, 1, 2)):
                u0 = tiles[ti] * TT
                xs = ti % 2
                xnr = [f"xn{xs}_{c}" for c in range(16)]
                for vb in vbs:
                    vs_ = cnt["v"] % 2; cnt["v"] += 1
                    P.add("sp", lambda e, vs_=vs_, vb=vb: e.dma_start(out=wv[vs_], in_=WVb[l, vb].rearrange("p (c m) -> p c m", c=16)),
                          reads=[f"D_wv{l}"], writes=[f"wv{vs_}"], dkey=f"wv{vs_}")
                    for s4 in range(4):
                        pb = cnt["ps"] % 4; cnt["ps"] += 1
                        for c in range(16):
                            P.add("pe", lambda e, c=c, vs_=vs_, pb=pb, s4=s4: e.matmul(
                                ps[pb][:], lhsT=xn[xs][:, c, s4 * 128:(s4 + 1) * 128], rhs=wv[vs_][:, c, :],
                                start=(c == 0), stop=(c == 15)),
                                reads=[f"wv{vs_}", xnr[c]], writes=[f"ps{pb}"])
                        vo = s4 % 2
                        ce = "act"
                        cnt["vc"] += 1
                        if ce == "act":
                            P.add("act", lambda e, pb=pb, vo=vo, vb=vb: e.activation(out=vst[vo][:, vb * 512:(vb + 1) * 512], in_=ps[pb][:], func=AF.Copy),
                                  reads=[f"ps{pb}"], writes=[f"vst{vo}_{vb}"])
                        else:
                            P.add("dve", lambda e, pb=pb, vo=vo, vb=vb: e.tensor_copy(out=vst[vo][:, vb * 512:(vb + 1) * 512], in_=ps[pb][:]),
                                  reads=[f"ps{pb}"], writes=[f"vst{vo}_{vb}"])
                        P.add("pool", lambda e, vo=vo, vb=vb, s4=s4: e.dma_start(
                            out=Vs[u0 + s4 * 128:u0 + (s4 + 1) * 128, vb * 512:(vb + 1) * 512], in_=vst[vo][:, vb * 512:(vb + 1) * 512]),
                            reads=[f"vst{vo}_{vb}"], dkey=f"vst{vo}_{vb}")

            def head(ti, nm, hi):
                u0 = tiles[ti] * TT
                xs = ti % 2
                xnr = [f"xn{xs}_{c}" for c in range(16)]
                gb = QKBLKS.index((nm, hi))
                ws = cnt["w"] % 4; cnt["w"] += 1
                pb = cnt["ps"] % 4; cnt["ps"] += 1
                hs = cnt["h"] % 2
                h3 = cnt["h"] % 3
                os_ = cnt["h"] % 4
                cnt["h"] += 1
                isq = nm[0] == "q"
                mixer = {"a": 0, "b": 1, "c": 2}[nm[1]]
                gcol = l * 6 + mixer * 2 + (0 if isq else 1)
                sb_ = 4 + hs
                rb = 6 + hs
                if nm[1] == "b" and hi >= 2:
                    dil = 4 if hi < 4 else 16
                    ov = outq[os_].rearrange("p (r j) -> p j r", r=dil)
                else:
                    ov = outq[os_]

                def stage1():
                    P.add("sp", lambda e: e.dma_start(out=wqk[ws], in_=WQKb[l, gb].rearrange("p (c m) -> p c m", c=16)),
                          reads=[f"D_wqk{l}"], writes=[f"wqk{ws}"], dkey=f"wqk{ws}")
                    for c in range(16):
                        P.add("pe", lambda e, c=c: e.matmul(
                            ps[pb][:], lhsT=wqk[ws][:, c, :], rhs=xn[xs][:, c, :], start=(c == 0), stop=(c == 15)),
                            reads=[f"wqk{ws}", xnr[c]], writes=[f"ps{pb}"])
                    P.add("act", lambda e: e.activation(out=sqh[h3], in_=ps[pb][:], func=AF.Square),
                          reads=[f"ps{pb}"], writes=[f"sqh{h3}"])
                    if mixer != 2:
                        P.add("act", lambda e: e.activation(out=qbf[h3], in_=ps[pb][:], func=AF.Copy),
                              reads=[f"ps{pb}"], writes=[f"qbf{h3}"])

                def stage2():
                    P.add("pe", lambda e: e.matmul(ps[sb_][:], lhsT=ones, rhs=sqh[h3], start=True, stop=True),
                          reads=[f"sqh{h3}"], writes=[f"ps{sb_}"])
                    P.add("act", lambda e: e.activation(out=rsh[hs], in_=ps[sb_][:], func=AF.Sqrt, bias=float(EPS), scale=1.0 / 128),
                          reads=[f"ps{sb_}"], writes=[f"rsh{hs}"])
                    P.add("act", lambda e: e.activation(out=rsh[hs], in_=rsh[hs], func=AF.Ln), reads=[f"rsh{hs}"], writes=[f"rsh{hs}"])
                    P.add("act", lambda e: e.activation(out=rsh[hs], in_=rsh[hs], func=AF.Exp, scale=-1.0), reads=[f"rsh{hs}"], writes=[f"rsh{hs}"])
                    if mixer == 2:
                        P.add("dve", lambda e: e.scalar_tensor_tensor(
                            out=ov, in0=ps[pb][:], scalar=gqs[:, gcol:gcol + 1], in1=rsh[hs], op0=ALU.mult, op1=ALU.mult),
                            reads=[f"ps{pb}", f"rsh{hs}"], writes=[f"outq{os_}"])
                    else:
                        j = mixer * 2 + (0 if isq else 1)
                        P.add("pe", lambda e: e.matmul(ps[rb][0:32, :], lhsT=rg[j][:, 0:32], rhs=qbf[h3], start=True, stop=True),
                              reads=[f"qbf{h3}", f"rg{j}"], writes=[f"ps{rb}"])
                        P.add("dve", lambda e: e.tensor_tensor(out=t1[hs], in0=ps[pb][:], in1=cosg[j], op=ALU.mult),
                              reads=[f"ps{pb}", f"cosg{j}"], writes=[f"t1{hs}"])
                        P.add("dve", lambda e: e.tensor_tensor(out=t2[hs][0:32, :], in0=ps[rb][0:32, :], in1=sinb[0:32, :], op=ALU.mult),
                              reads=[f"ps{rb}", "sinb"], writes=[f"t2{hs}"])
                        P.add("pool", lambda e: e.tensor_tensor(out=t1[hs][0:32, :], in0=t1[hs][0:32, :], in1=t2[hs][0:32, :], op=ALU.add),
                              reads=[f"t1{hs}", f"t2{hs}"], writes=[f"t1{hs}"])
                        P.add("pool", lambda e: e.tensor_tensor(out=ov, in0=t1[hs], in1=rsh[hs], op=ALU.mult),
                              reads=[f"t1{hs}", f"rsh{hs}"], writes=[f"outq{os_}"])
                    if nm[1] == "b":
                        g = hi // 2
                        hh = hi % 2
                        if g == 0:
                            dst = S[nm + "0"][hh][:, u0:u0 + TT]
                            src = outq[os_]
                        else:
                            dil_ = 4 if g == 1 else 16
                            n = TT // dil_
                            dst = S[nm + str(g)][hh][:, :, (u0 // dil_):(u0 // dil_) + n].rearrange("r p j -> p r j")
                            src = outq[os_].rearrange("p (r j) -> p r j", r=dil_)
                    else:
                        dst = S[nm][hi][:, u0:u0 + TT]
                        src = outq[os_]
                    P.add("pool", lambda e: e.dma_start(out=dst, in_=src), reads=[f"outq{os_}"], dkey=f"outq{os_}")
                return stage1, stage2

            prepA(0)
            prepB(0)
            for ti, t in enumerate(tiles):
                u0 = t * TT
                need_q = (u0 >= q_lo) and (u0 < q_hi)
                far = (ti == 0) or (ti == len(tiles) - 1)
                v_section(ti, (1,) if far else (0, 1, 2))
                if l == 0:
                    conv_some(1)
                if ti + 1 < len(tiles):
                    prepA(ti + 1)
                blks = QKBLKS if need_q else KBLKS
                if far:
                    blks = [("kb", 4), ("kb", 5)]
                pend = []
                for hidx, (nm, hi) in enumerate(blks):
                    if l == 0 and hidx % 2 == 1:
                        conv_some(1)
                    s1, s2 = head(ti, nm, hi)
                    s1()
                    pend.append(s2)
                    if len(pend) > 2:
                        pend.pop(0)()
                while pend:
                    pend.pop(0)()
                if ti + 1 < len(tiles):
                    prepB(ti + 1)
            P.barrier()
            if stop == ("P", l):
                return True

            AB.o, AFP.o = ab_base, af_base
            qa_t = AB.get(6 * 512).rearrange("p (h t) -> p h t", h=6)
            ka_t = AB.get(2 * 768).rearrange("p (h t) -> p h t", h=2)
            va_t = AB.get(6 * 256).rearrange("p (j h d) -> p j h d", j=6, h=2)
            qb0_t = AB.get(2 * 1024).rearrange("p (h t) -> p h t", h=2)
            kb0_t = AB.get(2 * 1152).rearrange("p (h t) -> p h t", h=2)
            vb0_t = AB.get(9 * 256).rearrange("p (j h d) -> p j h d", j=9, h=2)
            qb1_t = AB.get(2 * 4 * 256).rearrange("p (h r t) -> p h r t", h=2, r=4)
            kb1_t = AB.get(2 * 4 * 384).rearrange("p (h r t) -> p h r t", h=2, r=4)
            vb1_t = AB.get(4 * 3 * 256).rearrange("p (r j h d) -> p r j h d", r=4, j=3, h=2)
            qb2_t = AB.get(2 * 16 * 64).rearrange("p (h r t) -> p h r t", h=2, r=16)
            kb2_t = AB.get(2 * 16 * 192).rearrange("p (h r t) -> p h r t", h=2, r=16)
            vb2a_t = AB.get(16 * 256).rearrange("p (r h d) -> p r h d", r=16, h=2)
            vb2b_t = AB.get(16 * 256).rearrange("p (r h d) -> p r h d", r=16, h=2)
            qc_t = AB.get(4 * 512).rearrange("p (h t) -> p h t", h=4)
            kc_t = AB.get(4 * 1280).rearrange("p (h t) -> p h t", h=4)
            vc_t = AB.get(10 * 512).rearrange("p (j h d) -> p j h d", j=10, h=4)
            pT = [AB.get(512) for _ in range(3)]
            oA = AB.get(6 * 512).rearrange("p (h t) -> p h t", h=6)
            oC = AB.get(4 * 512).rearrange("p (h t) -> p h t", h=4)
            oBb = AB.get(6 * 1024).rearrange("p (h t) -> p h t", h=6)
            rden = [AFP.get(512) for _ in range(2)]
            OB = AFP.get(6 * 1024).rearrange("p (g t) -> p g t", g=6)
            DB = AFP.get(6 * 1024).rearrange("p (g t) -> p g t", g=6)
            MIXv = MIX.rearrange("(h p) u -> p h u", p=128)
            scnt = [0]
            acnt = [0]

            def s_bank():
                b = scnt[0] % 4; scnt[0] += 1
                return b

            def exp_to(pt_i, b, ncol, colkey, kpart=128, view=None):
                ci = COLIDX[colkey]
                if view is None:
                    o_ap, i_ap = pT[pt_i][0:kpart, 0:ncol], ps[b][0:kpart, 0:ncol]
                else:
                    o_ap, i_ap = view(pT[pt_i][0:kpart, :]), view(ps[b][0:kpart, :])
                P.add("act", lambda e: e.activation(out=o_ap, in_=i_ap, func=AF.Exp, bias=cols[0:kpart, ci:ci + 1], scale=1.0),
                      reads=[f"ps{b}"], writes=[f"pT{pt_i}"])

            pcnt = [0]

            def p_slot():
                s = pcnt[0] % 3; pcnt[0] += 1
                return s

            LOOK = 2
            fifo = []

            def step(s_fn, pv_fn):
                s_fn()
                fifo.append(pv_fn)
                while len(fifo) > LOOK:
                    fifo.pop(0)()

            def flush():
                while fifo:
                    fifo.pop(0)()

            def tri_of(j):
                return triGE if j == 0 else triLE

            def loadB(m):
                ub = 1024 * m
                P.add("sp", lambda e, ub=ub: e.dma_start(out=qb0_t, in_=S["qb0"][:, :, ub:ub + 1024].rearrange("h p t -> p h t")),
                      writes=["qb0_t"], dkey="qb0_t")
                P.add("sp", lambda e, ub=ub: e.dma_start(out=kb0_t, in_=S["kb0"][:, :, ub - 64:ub + 1088].rearrange("h p t -> p h t")),
                      writes=["kb0_t"], dkey="kb0_t")
                P.add("sp", lambda e, ub=ub: e.dma_start(
                    out=vb0_t, in_=Vs[ub - 64:ub + 1088, 256:512].rearrange("(j p) (h d) -> p j h d", p=128, h=2)),
                    writes=["vb0_t"], dkey="vb0_t")
                J1 = 256 * m
                for hh in range(2):
                    P.add("sp", lambda e, hh=hh, J1=J1: e.dma_start(out=qb1_t[:, hh], in_=S["qb1"][hh][:, :, J1:J1 + 256].rearrange("r p t -> p r t")),
                          writes=[f"qb1_t{hh}"], dkey=f"qb1_t{hh}")
                    P.add("sp", lambda e, hh=hh, J1=J1: e.dma_start(out=kb1_t[:, hh], in_=S["kb1"][hh][:, :, J1 - 64:J1 + 320].rearrange("r p t -> p r t")),
                          writes=[f"kb1_t{hh}"], dkey=f"kb1_t{hh}")
                for r in range(4):
                    t0 = 4 * (J1 - 64) + r
                    P.add("sp", lambda e, r=r, t0=t0: e.dma_start(
                        out=vb1_t[:, r], in_=Vs[t0:t0 + 4 * 384 - 3:4, 512:768].rearrange("(j p) (h d) -> p j h d", p=128, h=2)),
                        writes=[f"vb1_t{r}"], dkey=f"vb1_t{r}")
                J2 = 64 * m
                for hh in range(2):
                    P.add("sp", lambda e, hh=hh, J2=J2: e.dma_start(out=qb2_t[:, hh], in_=S["qb2"][hh][:, :, J2:J2 + 64].rearrange("r p t -> p r t")),
                          writes=[f"qb2_t{hh}"], dkey=f"qb2_t{hh}")
                    P.add("sp", lambda e, hh=hh, J2=J2: e.dma_start(out=kb2_t[:, hh], in_=S["kb2"][hh][:, :, J2 - 64:J2 + 128].rearrange("r p t -> p r t")),
                          writes=[f"kb2_t{hh}"], dkey=f"kb2_t{hh}")
                t0 = 16 * (J2 - 64)
                P.add("sp", lambda e, t0=t0: e.dma_start(
                    out=vb2a_t, in_=Vs[t0:t0 + 2048, 768:1024].rearrange("(p r) (h d) -> p r h d", r=16, h=2)),
                    writes=["vb2a_t"], dkey="vb2a_t")
                t1_ = 16 * (J2 + 64)
                P.add("sp", lambda e, t1_=t1_: e.dma_start(
                    out=vb2b_t[0:64], in_=Vs[t1_:t1_ + 1024, 768:1024].rearrange("(p r) (h d) -> p r h d", r=16, h=2)),
                    writes=["vb2b_t"], dkey="vb2b_t")


            def loadA(m, half):
                ubh = 1024 * m + 512 * half
                P.add("sp", lambda e, ubh=ubh: e.dma_start(out=qa_t, in_=S["qa"][:, :, ubh:ubh + 512].rearrange("h p t -> p h t")),
                      writes=["qa_t"], dkey="qa_t")
                P.add("sp", lambda e, ubh=ubh: e.dma_start(out=ka_t, in_=S["ka"][:, :, ubh - 128:ubh + 640].rearrange("h p t -> p h t")),
                      writes=["ka_t"], dkey="ka_t")
                P.add("sp", lambda e, ubh=ubh: e.dma_start(
                    out=va_t, in_=Vs[ubh - 128:ubh + 640, 0:256].rearrange("(j p) (h d) -> p j h d", p=128, h=2)),
                    writes=["va_t"], dkey="va_t")

            def computeA(m, half):
                ub = 1024 * m
                ubh = ub + 512 * half
                for i in range(4):
                    qb = (ubh // 128) + i
                    for g in range(2):
                        ob = 4 + (acnt[0] % 2); db = 6 + (acnt[0] % 2); acnt[0] += 1
                        rd = acnt[0] % 2
                        for j in range(3):
                            b = s_bank(); pi = p_slot()

                            def s_fn(b=b, pi=pi, g=g, i=i, j=j, qb=qb):
                                if j != 1:
                                    tri = tri_of(j)
                                    P.add("pe", lambda e: e.matmul(ps[b][:, 0:384], lhsT=ident, rhs=tri[:, 0:384], start=True, stop=False),
                                          reads=["ident", "triGE", "triLE"], writes=[f"ps{b}"])
                                P.add("pe", lambda e: e.matmul(
                                    ps[b][:, 0:384], lhsT=ka_t[:, g, (i + j) * 128:(i + j + 1) * 128],
                                    rhs=qa_t[:, 3 * g:3 * g + 3, i * 128:(i + 1) * 128], start=(j == 1), stop=True),
                                    reads=["ka_t", "qa_t"], writes=[f"ps{b}"])
                                exp_to(pi, b, 384, ("A", qb, j))

                            def pv_fn(pi=pi, g=g, i=i, j=j, ob=ob, db=db, rd=rd, ubh=ubh):
                                P.add("pe", lambda e: e.matmul(
                                    ps[ob][:, 0:384], lhsT=va_t[:, i + j, g, :], rhs=pT[pi][:, 0:384], start=(j == 0), stop=(j == 2)),
                                    reads=["va_t", f"pT{pi}"], writes=[f"ps{ob}"])
                                P.add("pe", lambda e: e.matmul(
                                    ps[db][:, 0:384], lhsT=ones, rhs=pT[pi][:, 0:384], start=(j == 0), stop=(j == 2)),
                                    reads=["ones", f"pT{pi}"], writes=[f"ps{db}"])
                                if j == 2:
                                    P.add("dve", lambda e: e.tensor_tensor(
                                        out=rden[rd][:, 0:384], in0=ps[db][:, 0:384], in1=esinkb[:, 384 * g:384 * g + 384], op=ALU.add),
                                        reads=[f"ps{db}", "esinkb"], writes=[f"rden{rd}"])
                                    P.add("dve", lambda e: e.reciprocal(out=rden[rd][:, 0:384], in_=rden[rd][:, 0:384]),
                                          reads=[f"rden{rd}"], writes=[f"rden{rd}"])
                                    P.add("dve", lambda e: e.tensor_tensor(
                                        out=oA[:, 3 * g:3 * g + 3, i * 128:(i + 1) * 128],
                                        in0=ps[ob][:, 0:384].rearrange("p (h t) -> p h t", h=3),
                                        in1=rden[rd][:, 0:384].rearrange("p (h t) -> p h t", h=3), op=ALU.mult),
                                        reads=[f"ps{ob}", f"rden{rd}"], writes=["oA"])
                                    if i == 3 and g == 1:
                                        P.add("sp", lambda e: e.dma_start(out=MIXv[:, 0:6, ubh:ubh + 512], in_=oA), reads=["oA"], dkey="oA")
                            step(s_fn, pv_fn)

            def computeB(m):
                ub = 1024 * m
                for i in range(8):
                    qb = (ub // 128) + i
                    ob = 4 + (acnt[0] % 2); db = 6 + (acnt[0] % 2); acnt[0] += 1
                    for j in range(2):
                        b = s_bank(); pi = p_slot()

                        def s_fn(b=b, pi=pi, i=i, j=j, qb=qb):
                            tri = tri_of(j)
                            P.add("pe", lambda e: e.matmul(ps[b][:, 0:256], lhsT=ident, rhs=tri[:, 0:256], start=True, stop=False),
                                  reads=["ident", "triGE", "triLE"], writes=[f"ps{b}"])
                            for hh in range(2):
                                P.add("pe", lambda e, hh=hh: e.matmul(
                                    ps[b][:, hh * 128:(hh + 1) * 128], lhsT=kb0_t[:, hh, (i + j) * 128:(i + j + 1) * 128],
                                    rhs=qb0_t[:, hh, i * 128:(i + 1) * 128], start=False, stop=(hh == 1)),
                                    reads=["kb0_t", "qb0_t"], writes=[f"ps{b}"])
                            exp_to(pi, b, 256, ("B0", qb, j))

                        def pv_fn(pi=pi, i=i, j=j, ob=ob, db=db):
                            for hh in range(2):
                                P.add("pe", lambda e, hh=hh: e.matmul(
                                    ps[ob][:, hh * 128:(hh + 1) * 128], lhsT=vb0_t[:, i + j, hh, :], rhs=pT[pi][:, hh * 128:(hh + 1) * 128],
                                    start=(j == 0 and hh == 0), stop=(j == 1 and hh == 1)), reads=["vb0_t", f"pT{pi}"], writes=[f"ps{ob}"])
                            P.add("pe", lambda e: e.matmul(ps[db][:, 0:256], lhsT=ones, rhs=pT[pi][:, 0:256], start=(j == 0), stop=(j == 1)),
                                  reads=["ones", f"pT{pi}"], writes=[f"ps{db}"])
                            if j == 1:
                                P.add("act", lambda e: e.activation(
                                    out=OB[:, 0:2, i * 128:(i + 1) * 128], in_=ps[ob][:, 0:256].rearrange("p (h t) -> p h t", h=2), func=AF.Copy),
                                    reads=[f"ps{ob}"], writes=["OB0"])
                                P.add("dve", lambda e: e.tensor_copy(
                                    out=DB[:, 0:2, i * 128:(i + 1) * 128], in_=ps[db][:, 0:256].rearrange("p (h t) -> p h t", h=2)),
                                    reads=[f"ps{db}"], writes=["DB0"])
                        step(s_fn, pv_fn)

                for qh in range(2):
                    for r in range(4):
                        ob = 4 + (acnt[0] % 2); db = 6 + (acnt[0] % 2); acnt[0] += 1
                        o0 = 512 * qh + r
                        for j in range(2):
                            b = s_bank(); pi = p_slot()
                            k0 = 128 * qh + 128 * j

                            def s_fn(b=b, pi=pi, r=r, j=j, qh=qh, k0=k0):
                                tri = tri_of(j)
                                P.add("pe", lambda e: e.matmul(ps[b][:, 0:256], lhsT=ident, rhs=tri[:, 0:256], start=True, stop=False),
                                      reads=["ident", "triGE", "triLE"], writes=[f"ps{b}"])
                                for hh in range(2):
                                    P.add("pe", lambda e, hh=hh: e.matmul(
                                        ps[b][:, hh * 128:(hh + 1) * 128], lhsT=kb1_t[:, hh, r, k0:k0 + 128],
                                        rhs=qb1_t[:, hh, r, qh * 128:(qh + 1) * 128], start=False, stop=(hh == 1)),
                                        reads=[f"kb1_t{hh}", f"qb1_t{hh}"], writes=[f"ps{b}"])
                                exp_to(pi, b, 256, ("B1", m, qh, j))

                            def pv_fn(pi=pi, r=r, j=j, qh=qh, ob=ob, db=db, o0=o0):
                                for hh in range(2):
                                    P.add("pe", lambda e, hh=hh: e.matmul(
                                        ps[ob][:, hh * 128:(hh + 1) * 128], lhsT=vb1_t[:, r, qh + j, hh, :],
                                        rhs=pT[pi][:, hh * 128:(hh + 1) * 128], start=(j == 0 and hh == 0), stop=(j == 1 and hh == 1)),
                                        reads=[f"vb1_t{r}", f"pT{pi}"], writes=[f"ps{ob}"])
                                P.add("pe", lambda e: e.matmul(ps[db][:, 0:256], lhsT=ones, rhs=pT[pi][:, 0:256], start=(j == 0), stop=(j == 1)),
                                      reads=["ones", f"pT{pi}"], writes=[f"ps{db}"])
                                if j == 1:
                                    P.add("act", lambda e: e.activation(
                                        out=OB[:, 2:4, o0:o0 + 509:4], in_=ps[ob][:, 0:256].rearrange("p (h t) -> p h t", h=2), func=AF.Copy),
                                        reads=[f"ps{ob}"], writes=["OB1"])
                                    P.add("dve", lambda e: e.tensor_copy(
                                        out=DB[:, 2:4, o0:o0 + 509:4], in_=ps[db][:, 0:256].rearrange("p (h t) -> p h t", h=2)),
                                        reads=[f"ps{db}"], writes=["DB1"])
                            step(s_fn, pv_fn)

                for rg4 in range(4):
                    ob = 4 + (acnt[0] % 2); db = 6 + (acnt[0] % 2); acnt[0] += 1
                    for j in range(2):
                        kp = 128 if j == 0 else 64
                        b = s_bank(); pi = p_slot()

                        def s_fn(b=b, pi=pi, j=j, kp=kp, rg4=rg4):
                            tri = tri_of(j)
                            for half in range(2):
                                P.add("pe", lambda e, half=half: e.matmul(
                                    ps[b][0:kp, 256 * half:256 * half + 256].rearrange("p (a t) -> p a t", a=4),
                                    lhsT=ident[0:kp, 0:kp], rhs=tri[0:kp, :].rearrange("p (a t) -> p a t", a=4)[:, :, 0:64],
                                    start=(half == 0), stop=False),
                                    reads=["ident", "triGE", "triLE"], writes=[f"ps{b}"])
                            for rr in range(4):
                                r = rg4 * 4 + rr
                                for hh in range(2):
                                    c0 = (rr * 2 + hh) * 64
                                    P.add("pe", lambda e, hh=hh, r=r, c0=c0, rr=rr: e.matmul(
                                        ps[b][0:kp, c0:c0 + 64], lhsT=kb2_t[:, hh, r, 128 * j:128 * j + kp],
                                        rhs=qb2_t[:, hh, r, :], start=False, stop=(rr == 3 and hh == 1)),
                                        reads=[f"kb2_t{hh}", f"qb2_t{hh}"], writes=[f"ps{b}"])
                            exp_to(pi, b, 512, ("B2", m, j), kpart=kp)

                        def pv_fn(pi=pi, j=j, kp=kp, rg4=rg4, ob=ob, db=db):
                            vt = vb2a_t if j == 0 else vb2b_t
                            for rr in range(4):
                                r = rg4 * 4 + rr
                                for hh in range(2):
                                    c0 = (rr * 2 + hh) * 64
                                    P.add("pe", lambda e, hh=hh, r=r, c0=c0: e.matmul(
                                        ps[ob][:, c0:c0 + 64], lhsT=vt[0:kp, r, hh, :], rhs=pT[pi][0:kp, c0:c0 + 64],
                                        start=(j == 0 and c0 == 0), stop=(j == 1 and c0 == 448)),
                                        reads=["vb2a_t", "vb2b_t", f"pT{pi}"], writes=[f"ps{ob}"])
                            P.add("pe", lambda e: e.matmul(ps[db][:, :], lhsT=ones[0:kp, :], rhs=pT[pi][0:kp, :], start=(j == 0), stop=(j == 1)),
                                  reads=["ones", f"pT{pi}"], writes=[f"ps{db}"])
                            if j == 1:
                                for hh in range(2):
                                    src_o = ps[ob][:, :].rearrange("p (rr h t) -> p rr h t", rr=4, h=2)[:, :, hh, :]
                                    src_d = ps[db][:, :].rearrange("p (rr h t) -> p rr h t", rr=4, h=2)[:, :, hh, :]
                                    dst_o = OB[:, 4 + hh, :].rearrange("p (t r) -> p r t", r=16)[:, rg4 * 4:rg4 * 4 + 4, :]
                                    dst_d = DB[:, 4 + hh, :].rearrange("p (t r) -> p r t", r=16)[:, rg4 * 4:rg4 * 4 + 4, :]
                                    P.add("act", lambda e, src_o=src_o, dst_o=dst_o: e.activation(out=dst_o, in_=src_o, func=AF.Copy),
                                          reads=[f"ps{ob}"], writes=["OB2"])
                                    P.add("dve", lambda e, src_d=src_d, dst_d=dst_d: e.tensor_copy(out=dst_d, in_=src_d),
                                          reads=[f"ps{db}"], writes=["DB2"])
                        step(s_fn, pv_fn)

                def combine(ub=ub):
                    for hh in range(2):
                        P.add("dve", lambda e, hh=hh: e.tensor_tensor(out=DB[:, hh, :], in0=DB[:, hh, :], in1=DB[:, 2 + hh, :], op=ALU.add),
                              reads=["DB0", "DB1"], writes=["DB0"])
                        P.add("dve", lambda e, hh=hh: e.tensor_tensor(out=DB[:, hh, :], in0=DB[:, hh, :], in1=DB[:, 4 + hh, :], op=ALU.add),
                              reads=["DB0", "DB2"], writes=["DB0"])
                        P.add("dve", lambda e, hh=hh: e.reciprocal(out=DB[:, hh, :], in_=DB[:, hh, :]), reads=["DB0"], writes=["DB0"])
                        for g in range(3):
                            eng = "dve"
                            P.add(eng, lambda e, hh=hh, g=g: e.tensor_tensor(out=oBb[:, 2 * g + hh, :], in0=OB[:, 2 * g + hh, :], in1=DB[:, hh, :], op=ALU.mult),
                                  reads=["DB0", f"OB{g}"], writes=["oBb"])
                    P.add("sp", lambda e: e.dma_start(out=MIXv[:, 6:12, ub:ub + 1024], in_=oBb), reads=["oBb"], dkey="oBb")
                step(lambda: None, combine)


            def loadC(m, half):
                ubh = 1024 * m + 512 * half
                P.add("sp", lambda e, ubh=ubh: e.dma_start(out=qc_t, in_=S["qc"][:, :, ubh:ubh + 512].rearrange("h p t -> p h t")),
                      writes=["qc_t"], dkey="qc_t")
                P.add("sp", lambda e, ubh=ubh: e.dma_start(out=kc_t, in_=S["kc"][:, :, ubh - 384:ubh + 896].rearrange("h p t -> p h t")),
                      writes=["kc_t"], dkey="kc_t")
                for jj in range(2):
                    P.add("sp", lambda e, ubh=ubh, jj=jj: e.dma_start(
                        out=vc_t[:, 5 * jj:5 * jj + 5],
                        in_=Vs[ubh - 384 + 640 * jj:ubh - 384 + 640 * (jj + 1), 1024:1536].rearrange("(j p) (h d) -> p j h d", p=128, h=4)),
                        writes=[f"vc_t{jj}"], dkey=f"vc_t{jj}")

            def computeC(m, half):
                ub = 1024 * m
                ubh = ub + 512 * half
                for i in range(4):
                    qb = (ubh // 128) + i
                    ob = 4 + (acnt[0] % 2); db = 6 + (acnt[0] % 2); acnt[0] += 1
                    rd = acnt[0] % 2
                    for dt in range(7):
                        b = s_bank(); pi = p_slot()
                        kt = i + dt

                        def s_fn(b=b, pi=pi, i=i, dt=dt, kt=kt, qb=qb):
                            P.add("pe", lambda e: e.matmul(ps[b][:, :], lhsT=ident, rhs=tabc[:, dt * 512:(dt + 1) * 512], start=True, stop=False),
                                  reads=["ident", "tabc"], writes=[f"ps{b}"])
                            for h in range(4):
                                P.add("pe", lambda e, h=h: e.matmul(
                                    ps[b][:, h * 128:(h + 1) * 128], lhsT=kc_t[:, h, kt * 128:(kt + 1) * 128],
                                    rhs=qc_t[:, h, i * 128:(i + 1) * 128], start=False, stop=(h == 3)),
                                    reads=["kc_t", "qc_t"], writes=[f"ps{b}"])
                            for qh in range(2):
                                exp_to(pi, b, 0, ("C", qb, dt, qh),
                                       view=lambda a, qh=qh: a.rearrange("p (h t) -> p h t", h=4)[:, :, qh * 64:(qh + 1) * 64])

                        def pv_fn(pi=pi, i=i, dt=dt, kt=kt, ob=ob, db=db, rd=rd, ubh=ubh):
                            for h in range(4):
                                P.add("pe", lambda e, h=h: e.matmul(
                                    ps[ob][:, h * 128:(h + 1) * 128], lhsT=vc_t[:, kt, h, :], rhs=pT[pi][:, h * 128:(h + 1) * 128],
                                    start=(dt == 0 and h == 0), stop=(dt == 6 and h == 3)),
                                    reads=["vc_t0", "vc_t1", f"pT{pi}"], writes=[f"ps{ob}"])
                            P.add("pe", lambda e: e.matmul(ps[db][:, :], lhsT=ones, rhs=pT[pi][:, :], start=(dt == 0), stop=(dt == 6)),
                                  reads=["ones", f"pT{pi}"], writes=[f"ps{db}"])
                            if dt == 6:
                                P.add("dve", lambda e: e.reciprocal(out=rden[rd], in_=ps[db][:, :]), reads=[f"ps{db}"], writes=[f"rden{rd}"])
                                P.add("dve", lambda e: e.tensor_tensor(
                                    out=oC[:, :, i * 128:(i + 1) * 128], in0=ps[ob][:, :].rearrange("p (h t) -> p h t", h=4),
                                    in1=rden[rd].rearrange("p (h t) -> p h t", h=4), op=ALU.mult),
                                    reads=[f"ps{ob}", f"rden{rd}"], writes=["oC"])
                                if i == 3:
                                    P.add("sp", lambda e: e.dma_start(out=MIXv[:, 12:16, ubh:ubh + 512], in_=oC), reads=["oC"], dkey="oC")
                        step(s_fn, pv_fn)

            ms = list(range(q_lo // 1024, q_hi // 1024))
            loadB(ms[0]); loadA(ms[0], 0)
            computeA(ms[0], 0)
            for idx, m in enumerate(ms):
                nxt = ms[idx + 1] if idx + 1 < len(ms) else None
                flush(); loadA(m, 1)
                if idx > 0:
                    computeC(ms[idx - 1], 1)
                flush(); loadC(m, 0)
                computeB(m)
                computeA(m, 1)
                flush()
                if nxt is not None:
                    loadB(nxt); loadA(nxt, 0)
                computeC(m, 0)
                flush(); loadC(m, 1)
                if nxt is not None:
                    computeA(nxt, 0)
            computeC(ms[-1], 1)
            flush()
            P.barrier()
            if stop == ("T", l):
                return True

            while len(convq) > (NJ1 if l == 0 else 0):
                conv_some(1)
            AB.o, AFP.o = ab_base, af_base
            x1 = AFP.get(16 * TT).rearrange("p (c t) -> p c t", c=16)
            rstd2 = AFP.get(TT)
            xs_t = [AFP.get(TT) for _ in range(3)]
            ost = [AFP.get(TT) for _ in range(3)]
            sg = [AFP.get(TT) for _ in range(3)]
            mix_t = AB.get(16 * TT).rearrange("p (c t) -> p c t", c=16)
            xn2 = AB.get(16 * TT).rearrange("p (c t) -> p c t", c=16)
            h_t = AB.get(NF * TT).rearrange("p (f t) -> p f t", f=NF)
            sq2 = h_t
            wo_t = [AB.get(2048).rearrange("p (c m) -> p c m", c=16) for _ in range(2)]
            wgu_t = [AB.get(2048).rearrange("p (c m) -> p c m", c=16) for _ in range(4)]
            wd_t = [AB.get(DFF).rearrange("p (f m) -> p f m", f=NF) for _ in range(2)]
            xres = xin[l].rearrange("(c p) u -> p c u", p=128)
            gffn = l * 32 + 16
            woc = 0; wgc = 0; wdc = 0; pc = 0; xc = 0; oc = 0; sc = 0
            for t in range(q_lo // TT, q_hi // TT):
                u0 = t * TT
                P.add("sp", lambda e, u0=u0: e.dma_start(out=mix_t, in_=MIXv[:, :, u0:u0 + TT]), writes=["mix_t"], dkey="mix_t")
                for o in range(16):
                    ws = woc % 2; woc += 1
                    P.add("sp", lambda e, ws=ws, o=o: e.dma_start(out=wo_t[ws], in_=WOUTb[l, o].rearrange("p (c m) -> p c m", c=16)),
                          reads=[f"D_wout{l}"], writes=[f"wo{ws}"], dkey=f"wo{ws}")
                    xi = xc % 3; xc += 1
                    P.add("sp", lambda e, xi=xi, o=o, u0=u0: e.dma_start(out=xs_t[xi], in_=xres[:, o, u0:u0 + TT]),
                          writes=[f"xs{xi}"], dkey=f"xs{xi}")
                    pb = pc % 8; pc += 1
                    for c in range(16):
                        P.add("pe", lambda e, c=c, ws=ws, pb=pb: e.matmul(ps[pb][:], lhsT=wo_t[ws][:, c, :], rhs=mix_t[:, c, :],
                                                                          start=(c == 0), stop=(c == 15)),
                              reads=[f"wo{ws}", "mix_t"], writes=[f"ps{pb}"])
                    P.add("dve", lambda e, pb=pb, xi=xi, o=o: e.tensor_tensor(out=x1[:, o, :], in0=ps[pb][:], in1=xs_t[xi], op=ALU.add),
                          reads=[f"ps{pb}", f"xs{xi}"], writes=[f"x1_{o}"])
                    P.add("act", lambda e, o=o: e.activation(out=sq2[:, o, :], in_=x1[:, o, :], func=AF.Square),
                          reads=[f"x1_{o}"], writes=[f"h{o}"])
                pb = pc % 8; pc += 1
                for o in range(16):
                    P.add("pe", lambda e, o=o, pb=pb: e.matmul(ps[pb][:], lhsT=ones, rhs=sq2[:, o, :], start=(o == 0), stop=(o == 15)),
                          reads=[f"h{o}", "ones"], writes=[f"ps{pb}"])
                P.add("act", lambda e, pb=pb: e.activation(out=rstd2, in_=ps[pb][:], func=AF.Sqrt, bias=float(EPS), scale=1.0 / D),
                      reads=[f"ps{pb}"], writes=["rstd2"])
                P.add("dve", lambda e: e.reciprocal(out=rstd2, in_=rstd2), reads=["rstd2"], writes=["rstd2"])
                for c in range(16):
                    eng = "dve"
                    P.add(eng, lambda e, c=c: e.scalar_tensor_tensor(
                        out=xn2[:, c, :], in0=x1[:, c, :], scalar=gn[:, gffn + c:gffn + c + 1], in1=rstd2,
                        op0=ALU.mult, op1=ALU.mult), reads=[f"x1_{c}", "rstd2"], writes=[f"xn2_{c}"])
                for f in range(NF):
                    if f % 3 == 1:
                        conv_some(1)
                    pbs = []
                    for wi, Wsrc in enumerate((WGb, WUb)):
                        ws = wgc % 4; wgc += 1
                        dk_ = f"D_wg{l}" if wi == 0 else f"D_wu{l}"
                        P.add("sp", lambda e, ws=ws, f=f, Wsrc=Wsrc: e.dma_start(out=wgu_t[ws], in_=Wsrc[l, f].rearrange("p (c m) -> p c m", c=16)),
                              reads=[dk_], writes=[f"wgu{ws}"], dkey=f"wgu{ws}")
                        pb = pc % 8; pc += 1
                        pbs.append(pb)
                        for c in range(16):
                            P.add("pe", lambda e, c=c, ws=ws, pb=pb: e.matmul(ps[pb][:], lhsT=wgu_t[ws][:, c, :], rhs=xn2[:, c, :],
                                                                              start=(c == 0), stop=(c == 15)),
                                  reads=[f"wgu{ws}", f"xn2_{c}"], writes=[f"ps{pb}"])
                    si = sc % 3; sc += 1
                    P.add("act", lambda e, si=si, pb=pbs[0]: e.activation(out=sg[si], in_=ps[pb][:], func=AF.Silu),
                          reads=[f"ps{pbs[0]}"], writes=[f"sg{si}"])
                    P.add("dve", lambda e, si=si, pb=pbs[1], f=f: e.tensor_tensor(out=h_t[:, f, :], in0=ps[pb][:], in1=sg[si], op=ALU.mult),
                          reads=[f"ps{pbs[1]}", f"sg{si}"], writes=[f"h{f}"])
                for o in range(16):
                    ws = wdc % 2; wdc += 1
                    P.add("sp", lambda e, ws=ws, o=o: e.dma_start(out=wd_t[ws], in_=WDb[l, o].rearrange("p (f m) -> p f m", f=NF)),
                          reads=[f"D_wd{l}"], writes=[f"wd{ws}"], dkey=f"wd{ws}")
                    pb = pc % 8; pc += 1
                    for f in range(NF):
                        P.add("pe", lambda e, f=f, ws=ws, pb=pb: e.matmul(ps[pb][:], lhsT=wd_t[ws][:, f, :], rhs=h_t[:, f, :],
                                                                          start=(f == 0), stop=(f == NF - 1)),
                              reads=[f"wd{ws}", f"h{f}"], writes=[f"ps{pb}"])
                    oi = oc % 3; oc += 1
                    P.add("dve", lambda e, pb=pb, oi=oi, o=o: e.tensor_tensor(out=ost[oi], in0=ps[pb][:], in1=x1[:, o, :], op=ALU.add),
                          reads=[f"ps{pb}", f"x1_{o}"], writes=[f"ost{oi}"])
                    if l == 0:
                        dst = X1.rearrange("(c p) u -> p c u", p=128)[:, o, u0:u0 + TT]
                    else:
                        dst = yT.rearrange("(c p) u -> p c u", p=128)[:, o, u0 - 2 * HALO:u0 - 2 * HALO + TT]
                    P.add("pool", lambda e, oi=oi, dst=dst: e.dma_start(out=dst, in_=ost[oi]), reads=[f"ost{oi}"], dkey=f"ost{oi}")
            P.barrier()
            return stop == ("F", l)

        for l_ in range(2):
            if emit_layer(l_):
                break

        with nc.Block() as block:
            P.emit(nc, block)
    return nc


def _blk(w, cols):
    K = w.shape[0]
    return np.ascontiguousarray(w.reshape(K // 128, 128, -1).transpose(1, 0, 2).reshape(128, -1))


def prepare_inputs(x_prompt, x_sample, norm_mix, w_in, qk_norm, sink_a, rpb_c, w_out, norm_ffn, w_gate, w_up, w_down):
    f32 = np.float32
    xs = np.concatenate([np.asarray(x_prompt, f32).reshape(-1, D), np.asarray(x_sample, f32).reshape(-1, D)], axis=0)
    NTOK = xs.shape[0]
    xpad = np.zeros((NTOK + 4 * HALO, D), f32)
    xpad[2 * HALO:2 * HALO + NTOK] = xs
    w_in = np.asarray(w_in, f32); w_out = np.asarray(w_out, f32)
    w_gate = np.asarray(w_gate, f32); w_up = np.asarray(w_up, f32); w_down = np.asarray(w_down, f32)
    cg = {"qa": 0, "ka": 768, "va": 1024, "qb": 1280, "kb": 2048, "vb": 2816, "qc": 3584, "kc": 4096, "vc": 4608}
    WQK = np.zeros((2, 28, 128, 2048), f32)
    WV = np.zeros((2, 3, 128, 8192), f32)
    WOUT = np.zeros((2, 16, 128, 2048), f32)
    WG = np.zeros((2, NF, 128, 2048), f32)
    WU = np.zeros((2, NF, 128, 2048), f32)
    WD = np.zeros((2, 16, 128, DFF), f32)
    for l in range(2):
        for b, (nm, hi) in enumerate(QKBLKS):
            c0 = cg[nm] + hi * 128
            WQK[l, b] = _blk(w_in[l][:, c0:c0 + 128], 128)
        wvv = np.concatenate([w_in[l][:, 1024:1280], w_in[l][:, 2816:3584], w_in[l][:, 4608:5120]], axis=1)
        for b in range(3):
            WV[l, b] = _blk(wvv[:, b * 512:(b + 1) * 512], 512)
        for o in range(16):
            WOUT[l, o] = _blk(w_out[l][:, o * 128:(o + 1) * 128], 128)
            WD[l, o] = _blk(w_down[l][:, o * 128:(o + 1) * 128], 128)
        for f in range(NF):
            WG[l, f] = _blk(w_gate[l][:, f * 128:(f + 1) * 128], 128)
            WU[l, f] = _blk(w_up[l][:, f * 128:(f + 1) * 128], 128)
    GN = np.zeros((128, 64), f32)
    for l in range(2):
        GN[:, l * 32:l * 32 + 16] = np.asarray(norm_mix, f32)[l].reshape(16, 128).T
        GN[:, l * 32 + 16:l * 32 + 32] = np.asarray(norm_ffn, f32)[l].reshape(16, 128).T
    GQK = np.ascontiguousarray(np.asarray(qk_norm, f32).reshape(12, 128).T)
    SINK = np.ascontiguousarray(np.broadcast_to(np.asarray(sink_a, f32).reshape(1, 12), (128, 12)))
    cidx = _ctab_index()
    TABC = np.zeros((2, 128, 7 * 512), f32)
    for l in range(2):
        ext = np.concatenate([np.asarray(rpb_c, f32)[l].reshape(-1), np.array([NEGM], f32)])
        tab = ext[cidx]
        TABC[l] = tab.transpose(1, 0, 2, 3).reshape(128, 7 * 512)
    k = np.arange(128)[:, None]; q = np.arange(128)[None, :]
    CONST = np.zeros((128, 512), f32)
    CONST[:, 0:128] = np.eye(128, dtype=f32)
    CONST[:, 128:256] = np.where(k >= q, 0.0, NEGM)
    CONST[:, 256:384] = np.where(k <= q, 0.0, NEGM)
    for m_ in range(16):
        CONST[m_ + 16, 384 + m_] = -1.0
        CONST[m_, 384 + 16 + m_] = 1.0
    inv = (ROPE_THETA ** (-np.arange(0, 32, 2, dtype=np.float32) / 32)).astype(np.float32)
    pos = np.arange(SEQ, dtype=np.float32)
    ang = pos[:, None] * inv[None, :]
    cos_t = np.cos(ang).astype(f32).T
    sin_t = np.sin(ang).astype(f32).T
    common = dict(WQK=WQK, WV=WV, WOUT=WOUT, WG=WG, WU=WU, WD=WD, GN=GN, GQK=GQK, SINK=SINK, TABC=TABC, CONST=CONST)
    in_maps = []
    for c in range(NCORES):
        g0 = c * OWN
        win = xpad[g0:g0 + U]
        m = dict(common)
        m["xT"] = np.ascontiguousarray(win.T)
        gpos = (np.arange(U) + g0 - 2 * HALO) % SEQ
        COSW = np.ones((128, U), f32)
        COSW[0:16] = cos_t[:, gpos]; COSW[16:32] = cos_t[:, gpos]
        SINW = np.zeros((32, U), f32)
        SINW[0:16] = sin_t[:, gpos]; SINW[16:32] = sin_t[:, gpos]
        m["COSW"] = COSW; m["SINW"] = SINW
        m["COLS"] = _build_cols(c)
        in_maps.append(m)
    return in_maps


_NC_CACHE = {}


def kernel(x_prompt, x_sample, norm_mix, w_in, qk_norm, sink_a, rpb_c, w_out, norm_ffn, w_gate, w_up, w_down):
    in_maps = prepare_inputs(x_prompt, x_sample, norm_mix, w_in, qk_norm, sink_a, rpb_c, w_out, norm_ffn, w_gate, w_up, w_down)
    nc = build_program()
    res = run_bass_kernel_spmd(nc, in_maps, core_ids=list(range(NCORES)))
    ys = [np.asarray(res.results[c]["yT"], np.float32).T for c in range(NCORES)]
    y = np.concatenate(ys, axis=0)
    nb = np.asarray(x_prompt).shape[0] * SEQ
    y_prompt = np.ascontiguousarray(y[:nb].reshape(np.asarray(x_prompt).shape))
    y_sample = np.ascontiguousarray(y[nb:].reshape(np.asarray(x_sample).shape))
    return (y_prompt, y_sample)
```

```python
import numpy as np
import concourse.bass as bass
import concourse.mybir as mybir
from concourse.bass_utils import run_bass_kernel_spmd

F32 = mybir.dt.float32
BF16 = mybir.dt.bfloat16
AF = mybir.ActivationFunctionType
ALU = mybir.AluOpType

NCORES = 8
D = 2048
NSEQ = 3
SEQ = 8192
OWN = 3072
HALO = 1024
U = OWN + 4 * HALO
TT = 512
DFF = 5632
NF = DFF // 128
NC16 = D // 128
EPS = 1e-6
NEGM = -30000.0
GRID_W = 64
ROPE_THETA = 500000.0

REGIONS = [(0, U, HALO, U - HALO), (HALO, U - HALO, 2 * HALO, U - 2 * HALO)]

KBLKS = [("ka", i) for i in range(2)] + [("kb", i) for i in range(6)] + [("kc", i) for i in range(4)]
QBLKS = [("qa", i) for i in range(6)] + [("qb", i) for i in range(6)] + [("qc", i) for i in range(4)]
QKBLKS = KBLKS + QBLKS

SAME_ENGINE_WAITS = True


def _col_index():
    idx = {}
    n = 0
    for qb in range(HALO // 128, (U - HALO) // 128):
        for j in range(3):
            idx[("A", qb, j)] = n; n += 1
    for qb in range(HALO // 128, (U - HALO) // 128):
        for j in range(2):
            idx[("B0", qb, j)] = n; n += 1
    for m in range(1, 6):
        for qh in range(2):
            for j in range(2):
                idx[("B1", m, qh, j)] = n; n += 1
    for m in range(1, 6):
        for j in range(2):
            idx[("B2", m, j)] = n; n += 1
    for qb in range(HALO // 128, (U - HALO) // 128):
        for dt in range(7):
            for qh in range(2):
                idx[("C", qb, dt, qh)] = n; n += 1
    return idx, n


COLIDX, NCOLS = _col_index()


def _seqid(g):
    g = np.asarray(g)
    return np.where((g >= 0) & (g < NSEQ * SEQ), g // SEQ, -1)


def _build_cols(core):
    base = core * OWN - 2 * HALO
    cols = np.zeros((128, NCOLS), np.float32)
    p = np.arange(128)

    def setcol(key, ktok_u, qtok_u, extra_valid=None):
        gq = base + qtok_u
        sq = int(_seqid(gq))
        if sq < 0:
            return
        gk = base + ktok_u
        valid = (_seqid(gk) == sq)
        if extra_valid is not None:
            valid = valid & extra_valid
        cols[:len(valid), COLIDX[key]] = np.where(valid, 0.0, NEGM)

    for qb in range(HALO // 128, (U - HALO) // 128):
        u0 = qb * 128
        for j in range(3):
            setcol(("A", qb, j), u0 + 128 * (j - 1) + p, u0)
        for j in range(2):
            setcol(("B0", qb, j), u0 - 64 + 128 * j + p, u0)
        gq0 = base + u0
        for dt in range(7):
            for qh in range(2):
                rq = (gq0 % SEQ) // GRID_W + qh
                ks = min(max(rq - 4, 0), SEQ // GRID_W - 8)
                ktok = u0 + 128 * (dt - 3) + p
                gk = base + ktok
                kr = (gk % SEQ) // GRID_W
                ev = (kr >= ks) & (kr < ks + 8)
                setcol(("C", qb, dt, qh), ktok, u0, ev)
    for m in range(1, 6):
        for qh in range(2):
            J0 = 256 * m + 128 * qh
            for j in range(2):
                setcol(("B1", m, qh, j), 4 * (J0 - 64 + 128 * j + p), 4 * J0)
        J0 = 64 * m
        setcol(("B2", m, 0), 16 * (J0 - 64 + p), 16 * J0)
        setcol(("B2", m, 1), 16 * (J0 + 64 + p[:64]), 16 * J0)
    return cols


def _ctab_index():
    k = np.arange(128)[:, None]
    q = np.arange(128)[None, :]
    kro, kc = k // 64, k % 64
    qro, qc = q // 64, q % 64
    cstart = np.clip(qc - 8, 0, GRID_W - 16)
    cvalid = (kc >= cstart) & (kc < cstart + 16)
    relc = np.clip(kc - qc + 15, 0, 30)
    out = np.zeros((7, 128, 4, 128), np.int64)
    for dt in range(7):
        dr = 2 * (dt - 3) + kro - qro
        relr = np.clip(dr + 7, 0, 14)
        for h in range(4):
            ii = h * 15 * 31 + relr * 31 + relc
            out[dt, :, h, :] = np.where(cvalid, ii, 4 * 15 * 31)
    return out


class _Op:
    __slots__ = ("eng", "fn", "deps", "dma", "dkey", "sem", "val", "hasdep", "i", "persist")


class Prog:
    ENG = ("pe", "act", "dve", "pool", "sp")

    def __init__(self):
        self.ops = []
        self.lastw = {}
        self.readers = {}
        self.lastop = {}
        self.lastdma = {}

    def add(self, eng, fn, reads=(), writes=(), dkey=None, persist=False):
        op = _Op()
        op.persist = persist
        op.eng, op.fn, op.dma, op.dkey = eng, fn, dkey is not None, dkey
        op.sem = None; op.val = 0; op.hasdep = False; op.i = len(self.ops)
        deps = set()
        for r in reads:
            w = self.lastw.get(r)
            if w is not None:
                deps.add(w)
        for w_ in writes:
            lw = self.lastw.get(w_)
            if lw is not None:
                deps.add(lw)
            for rd in self.readers.get(w_, ()):
                deps.add(rd)
        for r in reads:
            self.readers.setdefault(r, []).append(op)
        for w_ in writes:
            self.lastw[w_] = op
            self.readers[w_] = []
        op.deps = deps
        for d in deps:
            d.hasdep = True
        self.ops.append(op)
        if op.dma:
            self.lastdma[dkey] = op
        else:
            self.lastop[eng] = op
        return op

    def barrier(self):
        pend = set(o for o in (set(self.lastop.values()) | set(self.lastdma.values())) if not o.persist)
        keepw = {k: v for k, v in self.lastw.items() if v.persist}
        keepd = {k: v for k, v in self.lastdma.items() if v.persist}
        for e in self.ENG:
            op = _Op()
            op.persist = False
            op.eng, op.fn, op.dma, op.dkey = e, None, False, None
            op.sem = None; op.val = 0; op.hasdep = False; op.i = len(self.ops)
            op.deps = set(pend)
            for d in pend:
                d.hasdep = True
            self.ops.append(op)
        self.lastw.clear(); self.readers.clear(); self.lastop.clear(); self.lastdma.clear()
        self.lastw.update(keepw); self.lastdma.update(keepd)

    def emit(self, nc, block):
        engs = {"pe": "tensor", "act": "scalar", "dve": "vector", "pool": "gpsimd", "sp": "sync"}
        esem = {e: nc.alloc_semaphore(f"e_{e}") for e in self.ENG}
        ecnt = {e: 0 for e in self.ENG}
        dsem = {}
        dcnt = {}
        for op in self.ops:
            if op.fn is None:
                continue
            if op.dma:
                if op.dkey not in dsem:
                    dsem[op.dkey] = nc.alloc_semaphore("d_" + str(len(dsem)))
                    dcnt[op.dkey] = 0
                dcnt[op.dkey] += 16
                op.sem, op.val = dsem[op.dkey], dcnt[op.dkey]
            elif op.hasdep:
                ecnt[op.eng] += 1
                op.sem, op.val = esem[op.eng], ecnt[op.eng]
        self.nsem = len(dsem) + 5
        per = {e: [o for o in self.ops if o.eng == e] for e in self.ENG}

        def run(e):
            def body(eng):
                waited = {}
                for op in per[e]:
                    for d in sorted(op.deps, key=lambda o: o.i):
                        if d.sem is None:
                            continue
                        if (not d.dma) and d.eng == e and (e == "pe" or not SAME_ENGINE_WAITS):
                            continue
                        k = id(d.sem)
                        if waited.get(k, 0) < d.val:
                            eng.wait_ge(d.sem, d.val)
                            waited[k] = d.val
                    if op.fn is None:
                        continue
                    ins = op.fn(eng)
                    if op.dma:
                        ins.then_inc(op.sem, 16)
                    elif op.hasdep:
                        ins.then_inc(op.sem, 1)
            return body

        for e in self.ENG:
            getattr(block, engs[e])(run(e))


class Arena:
    def __init__(self, t, n):
        self.t, self.n, self.o = t, n, 0

    def reset(self):
        self.o = 0

    def get(self, n):
        assert self.o + n <= self.n, (self.o, n, self.n)
        v = self.t[:, self.o:self.o + n]
        self.o += n
        return v


def build_program(debug=False, stop=None):
    nc = bass.Bass("TRN2", target_bir_lowering=False)
    P = Prog()

    def dram(name, shape, dt=F32, kind="ExternalInput"):
        return nc.dram_tensor(name, list(shape), dt, kind=kind).ap()

    xT = dram("xT", [D, U])
    WQK = dram("WQK", [2, 28, 128, 2048])
    WV = dram("WV", [2, 3, 128, 8192])
    WOUT = dram("WOUT", [2, 16, 128, 2048])
    WG = dram("WG", [2, NF, 128, 2048])
    WU = dram("WU", [2, NF, 128, 2048])
    WD = dram("WD", [2, 16, 128, DFF])
    GN = dram("GN", [128, 2 * 2 * 16])
    GQK = dram("GQK", [128, 2 * 6])
    SINK = dram("SINK", [128, 2 * 6])
    COSW = dram("COSW", [128, U])
    SINW = dram("SINW", [32, U])
    COLS = dram("COLS", [128, NCOLS])
    TABC = dram("TABC", [2, 128, 7 * 512])
    CONST = dram("CONST", [128, 128 * 3 + 128])
    yT = dram("yT", [D, OWN], kind="ExternalOutput")
    ik = "ExternalOutput" if debug else "Internal"
    WQKb = dram("WQKb", [2, 28, 128, 2048], BF16, "Internal")
    WVb = dram("WVb", [2, 3, 128, 8192], BF16, "Internal")
    WOUTb = dram("WOUTb", [2, 16, 128, 2048], BF16, "Internal")
    WGb = dram("WGb", [2, NF, 128, 2048], BF16, "Internal")
    WUb = dram("WUb", [2, NF, 128, 2048], BF16, "Internal")
    WDb = dram("WDb", [2, 16, 128, DFF], BF16, "Internal")
    S = {
        "qa": dram("s_qa", [6, 128, U], BF16, ik), "ka": dram("s_ka", [2, 128, U], BF16, ik),
        "qc": dram("s_qc", [4, 128, U], BF16, ik), "kc": dram("s_kc", [4, 128, U], BF16, ik),
        "qb0": dram("s_qb0", [2, 128, U], BF16, ik), "kb0": dram("s_kb0", [2, 128, U], BF16, ik),
        "qb1": dram("s_qb1", [2, 4, 128, U // 4], BF16, ik), "kb1": dram("s_kb1", [2, 4, 128, U // 4], BF16, ik),
        "qb2": dram("s_qb2", [2, 16, 128, U // 16], BF16, ik), "kb2": dram("s_kb2", [2, 16, 128, U // 16], BF16, ik),
    }
    Vs = dram("s_v", [U, 1536], BF16, ik)
    MIX = dram("s_mix", [D, U], BF16, ik)
    X1 = dram("s_x1", [D, U], F32, ik)

    import contextlib
    es = contextlib.ExitStack()
    with es:
        NB16 = 63000
        NF32 = 15000
        abf_t = es.enter_context(nc.sbuf_tensor("abf", [128, NB16], BF16))
        af_t = es.enter_context(nc.sbuf_tensor("af32", [128, NF32], F32))
        cbf_t = es.enter_context(nc.sbuf_tensor("cbf", [128, 128 * 2 + 512 * 2 + 7 * 512 + 128], BF16))
        cf_t = es.enter_context(nc.sbuf_tensor("cf", [128, NCOLS + 64 + 12 + 12 + 12 + 768 + 512], F32))
        ps = [es.enter_context(nc.psum_tensor(f"ps{i}", [128, 512], F32)) for i in range(8)]
        AB = Arena(abf_t, NB16)
        AFP = Arena(af_t, NF32)
        CB = Arena(cbf_t, 128 * 2 + 512 * 2 + 7 * 512 + 128)
        CF = Arena(cf_t, NCOLS + 64 + 12 + 12 + 12 + 768 + 512)

        ident = CB.get(128)
        ones = CB.get(128)
        triGE = CB.get(512)
        triLE = CB.get(512)
        tabc = CB.get(7 * 512)
        rotT = CB.get(128)
        cols = CF.get(NCOLS)
        gn = CF.get(64)
        gqk = CF.get(12)
        gqs = CF.get(12)
        esink = CF.get(12)
        esinkb = CF.get(768)

        AB.reset(); AFP.reset()
        cst32 = CF.get(512)
        P.add("sp", lambda e: e.dma_start(out=cst32, in_=CONST[:, :]), writes=["cst32"], dkey="cst32")
        P.add("sp", lambda e: e.dma_start(out=cols, in_=COLS[:, :]), writes=["cols"], dkey="cols")
        P.add("sp", lambda e: e.dma_start(out=gn, in_=GN[:, :]), writes=["gn"], dkey="gn")
        P.add("sp", lambda e: e.dma_start(out=gqk, in_=GQK[:, :]), writes=["gqk"], dkey="gqk")
        P.add("sp", lambda e: e.dma_start(out=esink, in_=SINK[:, :]), writes=["esink"], dkey="esink")
        P.add("dve", lambda e: e.tensor_copy(out=ident, in_=cst32[:, 0:128]), reads=["cst32"], writes=["ident"])
        P.add("dve", lambda e: e.memset(ones, 1.0), writes=["ones"])
        for r in range(4):
            P.add("dve", lambda e, r=r: e.tensor_copy(out=triGE[:, r * 128:(r + 1) * 128], in_=cst32[:, 128:256]),
                  reads=["cst32"], writes=["triGE"])
            P.add("dve", lambda e, r=r: e.tensor_copy(out=triLE[:, r * 128:(r + 1) * 128], in_=cst32[:, 256:384]),
                  reads=["cst32"], writes=["triLE"])
        P.add("act", lambda e: e.activation(out=esink, in_=esink, func=AF.Exp), reads=["esink"], writes=["esink"])
        P.add("dve", lambda e: e.tensor_copy(out=gqs, in_=gqk), reads=["gqk"], writes=["gqs"])
        for l in range(2):
            for mx in range(3):
                cc = l * 6 + mx * 2
                P.add("dve", lambda e, cc=cc: e.tensor_scalar(out=gqs[:, cc:cc + 1], in0=gqk[:, cc:cc + 1],
                                                             scalar1=float(128 ** -0.5), scalar2=None, op0=ALU.mult),
                      reads=["gqk", "gqs"], writes=["gqs"])

        convq = []

        def conv(src, dst, nblk, key, step=8):
            for b0 in range(0, nblk, step):
                b1 = min(nblk, b0 + step)
                convq.append(lambda b0=b0, b1=b1, src=src, dst=dst, key=key: P.add(
                    "pool", lambda e: e.dma_start(out=dst[b0:b1], in_=src[b0:b1]), writes=[key], dkey=key, persist=True))
        for l in range(2):
            conv(WQK[l], WQKb[l], 28, f"D_wqk{l}", 2 if l == 0 else 1)
            conv(WV[l], WVb[l], 3, f"D_wv{l}", 1)
            conv(WOUT[l], WOUTb[l], 16, f"D_wout{l}", 1)
            conv(WG[l], WGb[l], NF, f"D_wg{l}", 1)
            conv(WU[l], WUb[l], NF, f"D_wu{l}", 1)
            conv(WD[l], WDb[l], 16, f"D_wd{l}", 1)
        NJ1 = 28 + 3 + 16 + NF + NF + 16

        def conv_some(n):
            for _ in range(n):
                if convq:
                    convq.pop(0)()
        conv_some(17)
        P.barrier()

        xin = [xT, X1]
        xout = [X1, None]

        def emit_layer(l):
            kv_lo, kv_hi, q_lo, q_hi = REGIONS[l]
            AB.reset(); AFP.reset()
            t32 = AFP.get(7 * 512)
            P.add("sp", lambda e, l=l: e.dma_start(out=t32, in_=TABC[l]), writes=["t32"], dkey="t32")
            P.add("dve", lambda e: e.tensor_copy(out=tabc, in_=t32), reads=["t32"], writes=["tabc"])
            for h in range(6):
                P.add("dve", lambda e, h=h, l=l: e.tensor_scalar(
                    out=esinkb[:, h * 128:(h + 1) * 128], in0=cst32[:, 0:128], scalar1=0.0,
                    scalar2=esink[:, l * 6 + h:l * 6 + h + 1], op0=ALU.mult, op1=ALU.add),
                    reads=["cst32", "esink"], writes=["esinkb"])
            rg = [AB.get(128) for _ in range(4)]
            gcols = [l * 6 + 0, l * 6 + 1, l * 6 + 2, l * 6 + 3]
            for j in range(4):
                P.add("dve", lambda e, j=j: e.tensor_scalar(
                    out=rg[j], in0=cst32[:, 384:512], scalar1=gqs[:, gcols[j]:gcols[j] + 1], scalar2=None,
                    op0=ALU.mult), reads=["cst32", "gqs"], writes=[f"rg{j}"])
            P.barrier()
            ab_base, af_base = AB.o, 0
            AFP.o = 0

            xt = AFP.get(16 * TT).rearrange("p (c t) -> p c t", c=16)
            rstd = AFP.get(TT)
            cosb = AFP.get(TT)
            sinb = AFP.get(TT)
            cosg = [AFP.get(TT) for _ in range(4)]
            rsh = [AFP.get(TT) for _ in range(2)]
            t1 = [AFP.get(TT) for _ in range(2)]
            t2 = [AFP.get(TT) for _ in range(2)]
            sq = AB.get(16 * TT).rearrange("p (c t) -> p c t", c=16)
            xn = [AB.get(16 * TT).rearrange("p (c t) -> p c t", c=16) for _ in range(2)]
            wqk = [AB.get(2048).rearrange("p (c m) -> p c m", c=16) for _ in range(4)]
            wv = [AB.get(8192).rearrange("p (c m) -> p c m", c=16) for _ in range(2)]
            sqh = [AB.get(TT) for _ in range(3)]
            qbf = [AB.get(TT) for _ in range(3)]
            outq = [AB.get(TT) for _ in range(8)]
            vst = [AB.get(1536) for _ in range(2)]
            xin_v = xin[l].rearrange("(c p) u -> p c u", p=128)
            gmix = l * 32
            wslot = 0
            vslot = 0
            hcount = 0
            vcount = 0
            pscnt = 0
            tiles = list(range(kv_lo // TT, kv_hi // TT))
            cnt = {"w": 0, "v": 0, "h": 0, "vc": 0, "ps": 0}

            def prepA(ti):
                u0 = tiles[ti] * TT
                xs = ti % 2
                P.add("sp", lambda e: e.dma_start(out=xt, in_=xin_v[:, :, u0:u0 + TT]), writes=["xt"], dkey="xt")
                for c in range(16):
                    P.add("act", lambda e, c=c: e.activation(out=sq[:, c, :], in_=xt[:, c, :], func=AF.Square),
                          reads=["xt"], writes=[f"sq{c}"])
                for c in range(16):
                    P.add("pe", lambda e, c=c: e.matmul(ps[7][:], lhsT=ones, rhs=sq[:, c, :], start=(c == 0), stop=(c == 15)),
                          reads=[f"sq{c}"], writes=["ps7"])
                P.add("act", lambda e: e.activation(out=rstd, in_=ps[7][:], func=AF.Sqrt, bias=float(EPS), scale=1.0 / D),
                      reads=["ps7"], writes=["rstd"])
                P.add("dve", lambda e: e.reciprocal(out=rstd, in_=rstd), reads=["rstd"], writes=["rstd"])
                for c in range(16):
                    P.add("dve", lambda e, c=c: e.scalar_tensor_tensor(
                        out=xn[xs][:, c, :], in0=xt[:, c, :], scalar=gn[:, gmix + c:gmix + c + 1], in1=rstd,
                        op0=ALU.mult, op1=ALU.mult), reads=["xt", "rstd"], writes=[f"xn{xs}_{c}"])

            def prepB(ti):
                u0 = tiles[ti] * TT
                P.add("sp", lambda e: e.dma_start(out=cosb, in_=COSW[:, u0:u0 + TT]), writes=["cosb"], dkey="cosb")
                P.add("sp", lambda e: e.dma_start(out=sinb[0:32, :], in_=SINW[:, u0:u0 + TT]), writes=["sinb"], dkey="sinb")
                for j in range(4):
                    P.add("pool", lambda e, j=j: e.tensor_scalar(out=cosg[j], in0=cosb, scalar1=gqs[:, gcols[j]:gcols[j] + 1],
                                                                 scalar2=None, op0=ALU.mult),
                          reads=["cosb"], writes=[f"cosg{j}"])

            def v_section(ti, vbs=(0, 1, 2)):
                u0 = tiles[ti] * TT
                xs = ti % 2
                xnr = [f"xn{xs}_{c}" for c in range(16)]
                for vb in vbs:
                    vs_ = cnt["v"] % 2; cnt["v"] += 1
                    P.add("sp", lambda e, vs_=vs_, vb=vb: e.dma_start(out=wv[vs_], in_=WVb[l, vb].rearrange("p (c m) -> p c m", c=16)),
                          reads=[f"D_wv{l}"], writes=[f"wv{vs_}"], dkey=f"wv{vs_}")
                    for s4 in range(4):
                        pb = cnt["ps"] % 4; cnt["ps"] += 1
                        for c in range(16):
                            P.add("pe", lambda e, c=c, vs_=vs_, pb=pb, s4=s4: e.matmul(
                                ps[pb][:], lhsT=xn[xs][:, c, s4 * 128:(s4 + 1) * 128], rhs=wv[vs_][:, c, :],
                                start=(c == 0), stop=(c == 15)),
                                reads=[f"wv{vs_}", xnr[c]], writes=[f"ps{pb}"])
                        vo = s4 % 2
                        ce = "act"
                        cnt["vc"] += 1
                        if ce == "act":
                            P.add("act", lambda e, pb=pb, vo=vo, vb=vb: e.activation(out=vst[vo][:, vb * 512:(vb + 1) * 512], in_=ps[pb][:], func=AF.Copy),
                                  reads=[f"ps{pb}"], writes=[f"vst{vo}_{vb}"])
                        else:
                            P.add("dve", lambda e, pb=pb, vo=vo, vb=vb: e.tensor_copy(out=vst[vo][:, vb * 512:(vb + 1) * 512], in_=ps[pb][:]),
                                  reads=[f"ps{pb}"], writes=[f"vst{vo}_{vb}"])
                        P.add("pool", lambda e, vo=vo, vb=vb, s4=s4: e.dma_start(
                            out=Vs[u0 + s4 * 128:u0 + (s4 + 1) * 128, vb * 512:(vb + 1) * 512], in_=vst[vo][:, vb * 512:(vb + 1) * 512]),
                            reads=[f"vst{vo}_{vb}"], dkey=f"vst{vo}_{vb}")

            def head(ti, nm, hi):
                u0 = tiles[ti] * TT
                xs = ti % 2
                xnr = [f"xn{xs}_{c}" for c in range(16)]
                gb = QKBLKS.index((nm, hi))
                ws = cnt["w"] % 4; cnt["w"] += 1
                pb = cnt["ps"] % 4; cnt["ps"] += 1
                hs = cnt["h"] % 2
                h3 = cnt["h"] % 3
                os_ = cnt["h"] % 8
                cnt["h"] += 1
                isq = nm[0] == "q"
                mixer = {"a": 0, "b": 1, "c": 2}[nm[1]]
                gcol = l * 6 + mixer * 2 + (0 if isq else 1)
                sb_ = 4 + hs
                rb = 6 + hs
                if nm[1] == "b" and hi >= 2:
                    dil = 4 if hi < 4 else 16
                    ov = outq[os_].rearrange("p (r j) -> p j r", r=dil)
                else:
                    ov = outq[os_]

                def stage1():
                    P.add("sp", lambda e: e.dma_start(out=wqk[ws], in_=WQKb[l, gb].rearrange("p (c m) -> p c m", c=16)),
                          reads=[f"D_wqk{l}"], writes=[f"wqk{ws}"], dkey=f"wqk{ws}")
                    for c in range(16):
                        P.add("pe", lambda e, c=c: e.matmul(
                            ps[pb][:], lhsT=wqk[ws][:, c, :], rhs=xn[xs][:, c, :], start=(c == 0), stop=(c == 15)),
                            reads=[f"wqk{ws}", xnr[c]], writes=[f"ps{pb}"])
                    P.add("act", lambda e: e.activation(out=sqh[h3], in_=ps[pb][:], func=AF.Square),
                          reads=[f"ps{pb}"], writes=[f"sqh{h3}"])
                    if mixer != 2:
                        P.add("act", lambda e: e.activation(out=qbf[h3], in_=ps[pb][:], func=AF.Copy),
                              reads=[f"ps{pb}"], writes=[f"qbf{h3}"])

                def stage2():
                    P.add("pe", lambda e: e.matmul(ps[sb_][:], lhsT=ones, rhs=sqh[h3], start=True, stop=True),
                          reads=[f"sqh{h3}"], writes=[f"ps{sb_}"])
                    P.add("act", lambda e: e.activation(out=rsh[hs], in_=ps[sb_][:], func=AF.Identity, bias=float(EPS), scale=1.0 / 128),
                          reads=[f"ps{sb_}"], writes=[f"rsh{hs}"])
                    P.add("act", lambda e: e.activation(out=rsh[hs], in_=rsh[hs], func=AF.Ln), reads=[f"rsh{hs}"], writes=[f"rsh{hs}"])
                    P.add("act", lambda e: e.activation(out=rsh[hs], in_=rsh[hs], func=AF.Exp, scale=-0.5), reads=[f"rsh{hs}"], writes=[f"rsh{hs}"])
                    if mixer == 2:
                        P.add("dve", lambda e: e.scalar_tensor_tensor(
                            out=ov, in0=ps[pb][:], scalar=gqs[:, gcol:gcol + 1], in1=rsh[hs], op0=ALU.mult, op1=ALU.mult),
                            reads=[f"ps{pb}", f"rsh{hs}"], writes=[f"outq{os_}"])
                    else:
                        j = mixer * 2 + (0 if isq else 1)
                        P.add("pe", lambda e: e.matmul(ps[rb][0:32, :], lhsT=rg[j][:, 0:32], rhs=qbf[h3], start=True, stop=True),
                              reads=[f"qbf{h3}", f"rg{j}"], writes=[f"ps{rb}"])
                        P.add("dve", lambda e: e.tensor_tensor(out=t1[hs], in0=ps[pb][:], in1=cosg[j], op=ALU.mult),
                              reads=[f"ps{pb}", f"cosg{j}"], writes=[f"t1{hs}"])
                        P.add("dve", lambda e: e.tensor_tensor(out=t2[hs][0:32, :], in0=ps[rb][0:32, :], in1=sinb[0:32, :], op=ALU.mult),
                              reads=[f"ps{rb}", "sinb"], writes=[f"t2{hs}"])
                        P.add("pool", lambda e: e.tensor_tensor(out=t1[hs][0:32, :], in0=t1[hs][0:32, :], in1=t2[hs][0:32, :], op=ALU.add),
                              reads=[f"t1{hs}", f"t2{hs}"], writes=[f"t1{hs}"])
                        P.add("pool", lambda e: e.tensor_tensor(out=ov, in0=t1[hs], in1=rsh[hs], op=ALU.mult),
                              reads=[f"t1{hs}", f"rsh{hs}"], writes=[f"outq{os_}"])
                    if nm[1] == "b":
                        g = hi // 2
                        hh = hi % 2
                        if g == 0:
                            dst = S[nm + "0"][hh][:, u0:u0 + TT]
                            src = outq[os_]
                        else:
                            dil_ = 4 if g == 1 else 16
                            n = TT // dil_
                            dst = S[nm + str(g)][hh][:, :, (u0 // dil_):(u0 // dil_) + n].rearrange("r p j -> p r j")
                            src = outq[os_].rearrange("p (r j) -> p r j", r=dil_)
                    else:
                        dst = S[nm][hi][:, u0:u0 + TT]
                        src = outq[os_]
                    P.add("pool", lambda e: e.dma_start(out=dst, in_=src), reads=[f"outq{os_}"], dkey=f"outq{os_}")
                return stage1, stage2

            prepA(0)
            prepB(0)
            for ti, t in enumerate(tiles):
                u0 = t * TT
                need_q = (u0 >= q_lo) and (u0 < q_hi)
                far = (ti == 0) or (ti == len(tiles) - 1)
                v_section(ti, (1,) if far else (0, 1, 2))
                if l == 0:
                    conv_some(1)
                if ti + 1 < len(tiles):
                    prepA(ti + 1)
                blks = QKBLKS if need_q else KBLKS
                if far:
                    blks = [("kb", 4), ("kb", 5)]
                pend = []
                for hidx, (nm, hi) in enumerate(blks):
                    if l == 0 and hidx % 2 == 1:
                        conv_some(1)
                    s1, s2 = head(ti, nm, hi)
                    s1()
                    pend.append(s2)
                    if len(pend) > 2:
                        pend.pop(0)()
                while pend:
                    pend.pop(0)()
                if ti + 1 < len(tiles):
                    prepB(ti + 1)
            P.barrier()
            if stop == ("P", l):
                return True

            AB.o, AFP.o = ab_base, af_base
            qa_t = AB.get(6 * 512).rearrange("p (h t) -> p h t", h=6)
            ka_t = AB.get(2 * 768).rearrange("p (h t) -> p h t", h=2)
            va_t = AB.get(6 * 256).rearrange("p (j h d) -> p j h d", j=6, h=2)
            qb0_t = AB.get(2 * 1024).rearrange("p (h t) -> p h t", h=2)
            kb0_t = AB.get(2 * 1152).rearrange("p (h t) -> p h t", h=2)
            vb0_t = AB.get(9 * 256).rearrange("p (j h d) -> p j h d", j=9, h=2)
            qb1_t = AB.get(2 * 4 * 256).rearrange("p (h r t) -> p h r t", h=2, r=4)
            kb1_t = AB.get(2 * 4 * 384).rearrange("p (h r t) -> p h r t", h=2, r=4)
            vb1_t = AB.get(4 * 3 * 256).rearrange("p (r j h d) -> p r j h d", r=4, j=3, h=2)
            qb2_t = AB.get(2 * 16 * 64).rearrange("p (h r t) -> p h r t", h=2, r=16)
            kb2_t = AB.get(2 * 16 * 192).rearrange("p (h r t) -> p h r t", h=2, r=16)
            vb2a_t = AB.get(16 * 256).rearrange("p (r h d) -> p r h d", r=16, h=2)
            vb2b_t = AB.get(16 * 256).rearrange("p (r h d) -> p r h d", r=16, h=2)
            qc_t = AB.get(4 * 512).rearrange("p (h t) -> p h t", h=4)
            kc_t = AB.get(4 * 1280).rearrange("p (h t) -> p h t", h=4)
            vc_t = AB.get(10 * 512).rearrange("p (j h d) -> p j h d", j=10, h=4)
            pT = [AB.get(512) for _ in range(3)]
            oA = AB.get(6 * 512).rearrange("p (h t) -> p h t", h=6)
            oC = AB.get(4 * 512).rearrange("p (h t) -> p h t", h=4)
            oBb = AB.get(6 * 1024).rearrange("p (h t) -> p h t", h=6)
            rden = [AFP.get(512) for _ in range(2)]
            OB = AFP.get(6 * 1024).rearrange("p (g t) -> p g t", g=6)
            DB = AFP.get(6 * 1024).rearrange("p (g t) -> p g t", g=6)
            MIXv = MIX.rearrange("(h p) u -> p h u", p=128)
            scnt = [0]
            acnt = [0]

            def s_bank():
                b = scnt[0] % 4; scnt[0] += 1
                return b

            def exp_to(pt_i, b, ncol, colkey, kpart=128, view=None):
                ci = COLIDX[colkey]
                if view is None:
                    o_ap, i_ap = pT[pt_i][0:kpart, 0:ncol], ps[b][0:kpart, 0:ncol]
                else:
                    o_ap, i_ap = view(pT[pt_i][0:kpart, :]), view(ps[b][0:kpart, :])
                P.add("act", lambda e: e.activation(out=o_ap, in_=i_ap, func=AF.Exp, bias=cols[0:kpart, ci:ci + 1], scale=1.0),
                      reads=[f"ps{b}"], writes=[f"pT{pt_i}"])

            pcnt = [0]

            def p_slot():
                s = pcnt[0] % 3; pcnt[0] += 1
                return s

            LOOK = 2
            fifo = []

            def step(s_fn, pv_fn):
                s_fn()
                fifo.append(pv_fn)
                while len(fifo) > LOOK:
                    fifo.pop(0)()

            def flush():
                while fifo:
                    fifo.pop(0)()

            def tri_of(j):
                return triGE if j == 0 else triLE

            def loadB(m):
                ub = 1024 * m
                P.add("sp", lambda e, ub=ub: e.dma_start(out=qb0_t, in_=S["qb0"][:, :, ub:ub + 1024].rearrange("h p t -> p h t")),
                      writes=["qb0_t"], dkey="qb0_t")
                P.add("sp", lambda e, ub=ub: e.dma_start(out=kb0_t, in_=S["kb0"][:, :, ub - 64:ub + 1088].rearrange("h p t -> p h t")),
                      writes=["kb0_t"], dkey="kb0_t")
                P.add("sp", lambda e, ub=ub: e.dma_start(
                    out=vb0_t, in_=Vs[ub - 64:ub + 1088, 256:512].rearrange("(j p) (h d) -> p j h d", p=128, h=2)),
                    writes=["vb0_t"], dkey="vb0_t")
                J1 = 256 * m
                for hh in range(2):
                    P.add("sp", lambda e, hh=hh, J1=J1: e.dma_start(out=qb1_t[:, hh], in_=S["qb1"][hh][:, :, J1:J1 + 256].rearrange("r p t -> p r t")),
                          writes=[f"qb1_t{hh}"], dkey=f"qb1_t{hh}")
                    P.add("sp", lambda e, hh=hh, J1=J1: e.dma_start(out=kb1_t[:, hh], in_=S["kb1"][hh][:, :, J1 - 64:J1 + 320].rearrange("r p t -> p r t")),
                          writes=[f"kb1_t{hh}"], dkey=f"kb1_t{hh}")
                for r in range(4):
                    t0 = 4 * (J1 - 64) + r
                    P.add("sp", lambda e, r=r, t0=t0: e.dma_start(
                        out=vb1_t[:, r], in_=Vs[t0:t0 + 4 * 384 - 3:4, 512:768].rearrange("(j p) (h d) -> p j h d", p=128, h=2)),
                        writes=[f"vb1_t{r}"], dkey=f"vb1_t{r}")
                J2 = 64 * m
                for hh in range(2):
                    P.add("sp", lambda e, hh=hh, J2=J2: e.dma_start(out=qb2_t[:, hh], in_=S["qb2"][hh][:, :, J2:J2 + 64].rearrange("r p t -> p r t")),
                          writes=[f"qb2_t{hh}"], dkey=f"qb2_t{hh}")
                    P.add("sp", lambda e, hh=hh, J2=J2: e.dma_start(out=kb2_t[:, hh], in_=S["kb2"][hh][:, :, J2 - 64:J2 + 128].rearrange("r p t -> p r t")),
                          writes=[f"kb2_t{hh}"], dkey=f"kb2_t{hh}")
                t0 = 16 * (J2 - 64)
                P.add("sp", lambda e, t0=t0: e.dma_start(
                    out=vb2a_t, in_=Vs[t0:t0 + 2048, 768:1024].rearrange("(p r) (h d) -> p r h d", r=16, h=2)),
                    writes=["vb2a_t"], dkey="vb2a_t")
                t1_ = 16 * (J2 + 64)
                P.add("sp", lambda e, t1_=t1_: e.dma_start(
                    out=vb2b_t[0:64], in_=Vs[t1_:t1_ + 1024, 768:1024].rearrange("(p r) (h d) -> p r h d", r=16, h=2)),
                    writes=["vb2b_t"], dkey="vb2b_t")


            def loadA(m, half):
                ubh = 1024 * m + 512 * half
                P.add("sp", lambda e, ubh=ubh: e.dma_start(out=qa_t, in_=S["qa"][:, :, ubh:ubh + 512].rearrange("h p t -> p h t")),
                      writes=["qa_t"], dkey="qa_t")
                P.add("sp", lambda e, ubh=ubh: e.dma_start(out=ka_t, in_=S["ka"][:, :, ubh - 128:ubh + 640].rearrange("h p t -> p h t")),
                      writes=["ka_t"], dkey="ka_t")
                P.add("sp", lambda e, ubh=ubh: e.dma_start(
                    out=va_t, in_=Vs[ubh - 128:ubh + 640, 0:256].rearrange("(j p) (h d) -> p j h d", p=128, h=2)),
                    writes=["va_t"], dkey="va_t")

            def computeA(m, half):
                ub = 1024 * m
                ubh = ub + 512 * half
                for i in range(4):
                    qb = (ubh // 128) + i
                    for g in range(2):
                        ob = 4 + (acnt[0] % 2); db = 6 + (acnt[0] % 2); acnt[0] += 1
                        rd = acnt[0] % 2
                        for j in range(3):
                            b = s_bank(); pi = p_slot()

                            def s_fn(b=b, pi=pi, g=g, i=i, j=j, qb=qb):
                                if j != 1:
                                    tri = tri_of(j)
                                    P.add("pe", lambda e: e.matmul(ps[b][:, 0:384], lhsT=ident, rhs=tri[:, 0:384], start=True, stop=False),
                                          reads=["ident", "triGE", "triLE"], writes=[f"ps{b}"])
                                P.add("pe", lambda e: e.matmul(
                                    ps[b][:, 0:384], lhsT=ka_t[:, g, (i + j) * 128:(i + j + 1) * 128],
                                    rhs=qa_t[:, 3 * g:3 * g + 3, i * 128:(i + 1) * 128], start=(j == 1), stop=True),
                                    reads=["ka_t", "qa_t"], writes=[f"ps{b}"])
                                exp_to(pi, b, 384, ("A", qb, j))

                            def pv_fn(pi=pi, g=g, i=i, j=j, ob=ob, db=db, rd=rd, ubh=ubh):
                                P.add("pe", lambda e: e.matmul(
                                    ps[ob][:, 0:384], lhsT=va_t[:, i + j, g, :], rhs=pT[pi][:, 0:384], start=(j == 0), stop=(j == 2)),
                                    reads=["va_t", f"pT{pi}"], writes=[f"ps{ob}"])
                                P.add("pe", lambda e: e.matmul(
                                    ps[db][:, 0:384], lhsT=ones, rhs=pT[pi][:, 0:384], start=(j == 0), stop=(j == 2)),
                                    reads=["ones", f"pT{pi}"], writes=[f"ps{db}"])
                                if j == 2:
                                    P.add("dve", lambda e: e.tensor_tensor(
                                        out=rden[rd][:, 0:384], in0=ps[db][:, 0:384], in1=esinkb[:, 384 * g:384 * g + 384], op=ALU.add),
                                        reads=[f"ps{db}", "esinkb"], writes=[f"rden{rd}"])
                                    P.add("dve", lambda e: e.reciprocal(out=rden[rd][:, 0:384], in_=rden[rd][:, 0:384]),
                                          reads=[f"rden{rd}"], writes=[f"rden{rd}"])
                                    P.add("dve", lambda e: e.tensor_tensor(
                                        out=oA[:, 3 * g:3 * g + 3, i * 128:(i + 1) * 128],
                                        in0=ps[ob][:, 0:384].rearrange("p (h t) -> p h t", h=3),
                                        in1=rden[rd][:, 0:384].rearrange("p (h t) -> p h t", h=3), op=ALU.mult),
                                        reads=[f"ps{ob}", f"rden{rd}"], writes=["oA"])
                                    if i == 3 and g == 1:
                                        P.add("sp", lambda e: e.dma_start(out=MIXv[:, 0:6, ubh:ubh + 512], in_=oA), reads=["oA"], dkey="oA")
                            step(s_fn, pv_fn)

            def computeB(m):
                ub = 1024 * m
                for i in range(8):
                    qb = (ub // 128) + i
                    ob = 4 + (acnt[0] % 2); db = 6 + (acnt[0] % 2); acnt[0] += 1
                    for j in range(2):
                        b = s_bank(); pi = p_slot()

                        def s_fn(b=b, pi=pi, i=i, j=j, qb=qb):
                            tri = tri_of(j)
                            P.add("pe", lambda e: e.matmul(ps[b][:, 0:256], lhsT=ident, rhs=tri[:, 0:256], start=True, stop=False),
                                  reads=["ident", "triGE", "triLE"], writes=[f"ps{b}"])
                            for hh in range(2):
                                P.add("pe", lambda e, hh=hh: e.matmul(
                                    ps[b][:, hh * 128:(hh + 1) * 128], lhsT=kb0_t[:, hh, (i + j) * 128:(i + j + 1) * 128],
                                    rhs=qb0_t[:, hh, i * 128:(i + 1) * 128], start=False, stop=(hh == 1)),
                                    reads=["kb0_t", "qb0_t"], writes=[f"ps{b}"])
                            exp_to(pi, b, 256, ("B0", qb, j))

                        def pv_fn(pi=pi, i=i, j=j, ob=ob, db=db):
                            for hh in range(2):
                                P.add("pe", lambda e, hh=hh: e.matmul(
                                    ps[ob][:, hh * 128:(hh + 1) * 128], lhsT=vb0_t[:, i + j, hh, :], rhs=pT[pi][:, hh * 128:(hh + 1) * 128],
                                    start=(j == 0 and hh == 0), stop=(j == 1 and hh == 1)), reads=["vb0_t", f"pT{pi}"], writes=[f"ps{ob}"])
                            P.add("pe", lambda e: e.matmul(ps[db][:, 0:256], lhsT=ones, rhs=pT[pi][:, 0:256], start=(j == 0), stop=(j == 1)),
                                  reads=["ones", f"pT{pi}"], writes=[f"ps{db}"])
                            if j == 1:
                                P.add("act", lambda e: e.activation(
                                    out=OB[:, 0:2, i * 128:(i + 1) * 128], in_=ps[ob][:, 0:256].rearrange("p (h t) -> p h t", h=2), func=AF.Copy),
                                    reads=[f"ps{ob}"], writes=["OB0"])
                                P.add("dve", lambda e: e.tensor_copy(
                                    out=DB[:, 0:2, i * 128:(i + 1) * 128], in_=ps[db][:, 0:256].rearrange("p (h t) -> p h t", h=2)),
                                    reads=[f"ps{db}"], writes=["DB0"])
                        step(s_fn, pv_fn)

                for qh in range(2):
                    for r in range(4):
                        ob = 4 + (acnt[0] % 2); db = 6 + (acnt[0] % 2); acnt[0] += 1
                        o0 = 512 * qh + r
                        for j in range(2):
                            b = s_bank(); pi = p_slot()
                            k0 = 128 * qh + 128 * j

                            def s_fn(b=b, pi=pi, r=r, j=j, qh=qh, k0=k0):
                                tri = tri_of(j)
                                P.add("pe", lambda e: e.matmul(ps[b][:, 0:256], lhsT=ident, rhs=tri[:, 0:256], start=True, stop=False),
                                      reads=["ident", "triGE", "triLE"], writes=[f"ps{b}"])
                                for hh in range(2):
                                    P.add("pe", lambda e, hh=hh: e.matmul(
                                        ps[b][:, hh * 128:(hh + 1) * 128], lhsT=kb1_t[:, hh, r, k0:k0 + 128],
                                        rhs=qb1_t[:, hh, r, qh * 128:(qh + 1) * 128], start=False, stop=(hh == 1)),
                                        reads=[f"kb1_t{hh}", f"qb1_t{hh}"], writes=[f"ps{b}"])
                                exp_to(pi, b, 256, ("B1", m, qh, j))

                            def pv_fn(pi=pi, r=r, j=j, qh=qh, ob=ob, db=db, o0=o0):
                                for hh in range(2):
                                    P.add("pe", lambda e, hh=hh: e.matmul(
                                        ps[ob][:, hh * 128:(hh + 1) * 128], lhsT=vb1_t[:, r, qh + j, hh, :],
                                        rhs=pT[pi][:, hh * 128:(hh + 1) * 128], start=(j == 0 and hh == 0), stop=(j == 1 and hh == 1)),
                                        reads=[f"vb1_t{r}", f"pT{pi}"], writes=[f"ps{ob}"])
                                P.add("pe", lambda e: e.matmul(ps[db][:, 0:256], lhsT=ones, rhs=pT[pi][:, 0:256], start=(j == 0), stop=(j == 1)),
                                      reads=["ones", f"pT{pi}"], writes=[f"ps{db}"])
                                if j == 1:
                                    P.add("act", lambda e: e.activation(
                                        out=OB[:, 2:4, o0:o0 + 509:4], in_=ps[ob][:, 0:256].rearrange("p (h t) -> p h t", h=2), func=AF.Copy),
                                        reads=[f"ps{ob}"], writes=["OB1"])
                                    P.add("dve", lambda e: e.tensor_copy(
                                        out=DB[:, 2:4, o0:o0 + 509:4], in_=ps[db][:, 0:256].rearrange("p (h t) -> p h t", h=2)),
                                        reads=[f"ps{db}"], writes=["DB1"])
                            step(s_fn, pv_fn)

                for rg4 in range(4):
                    ob = 4 + (acnt[0] % 2); db = 6 + (acnt[0] % 2); acnt[0] += 1
                    for j in range(2):
                        kp = 128 if j == 0 else 64
                        b = s_bank(); pi = p_slot()

                        def s_fn(b=b, pi=pi, j=j, kp=kp, rg4=rg4):
                            tri = tri_of(j)
                            for half in range(2):
                                P.add("pe", lambda e, half=half: e.matmul(
                                    ps[b][0:kp, 256 * half:256 * half + 256].rearrange("p (a t) -> p a t", a=4),
                                    lhsT=ident[0:kp, 0:kp], rhs=tri[0:kp, :].rearrange("p (a t) -> p a t", a=4)[:, :, 0:64],
                                    start=(half == 0), stop=False),
                                    reads=["ident", "triGE", "triLE"], writes=[f"ps{b}"])
                            for rr in range(4):
                                r = rg4 * 4 + rr
                                for hh in range(2):
                                    c0 = (rr * 2 + hh) * 64
                                    P.add("pe", lambda e, hh=hh, r=r, c0=c0, rr=rr: e.matmul(
                                        ps[b][0:kp, c0:c0 + 64], lhsT=kb2_t[:, hh, r, 128 * j:128 * j + kp],
                                        rhs=qb2_t[:, hh, r, :], start=False, stop=(rr == 3 and hh == 1)),
                                        reads=[f"kb2_t{hh}", f"qb2_t{hh}"], writes=[f"ps{b}"])
                            exp_to(pi, b, 512, ("B2", m, j), kpart=kp)

                        def pv_fn(pi=pi, j=j, kp=kp, rg4=rg4, ob=ob, db=db):
                            vt = vb2a_t if j == 0 else vb2b_t
                            for rr in range(4):
                                r = rg4 * 4 + rr
                                for hh in range(2):
                                    c0 = (rr * 2 + hh) * 64
                                    P.add("pe", lambda e, hh=hh, r=r, c0=c0: e.matmul(
                                        ps[ob][:, c0:c0 + 64], lhsT=vt[0:kp, r, hh, :], rhs=pT[pi][0:kp, c0:c0 + 64],
                                        start=(j == 0 and c0 == 0), stop=(j == 1 and c0 == 448)),
                                        reads=["vb2a_t", "vb2b_t", f"pT{pi}"], writes=[f"ps{ob}"])
                            P.add("pe", lambda e: e.matmul(ps[db][:, :], lhsT=ones[0:kp, :], rhs=pT[pi][0:kp, :], start=(j == 0), stop=(j == 1)),
                                  reads=["ones", f"pT{pi}"], writes=[f"ps{db}"])
                            if j == 1:
                                for hh in range(2):
                                    src_o = ps[ob][:, :].rearrange("p (rr h t) -> p rr h t", rr=4, h=2)[:, :, hh, :]
                                    src_d = ps[db][:, :].rearrange("p (rr h t) -> p rr h t", rr=4, h=2)[:, :, hh, :]
                                    dst_o = OB[:, 4 + hh, :].rearrange("p (t r) -> p r t", r=16)[:, rg4 * 4:rg4 * 4 + 4, :]
                                    dst_d = DB[:, 4 + hh, :].rearrange("p (t r) -> p r t", r=16)[:, rg4 * 4:rg4 * 4 + 4, :]
                                    P.add("act", lambda e, src_o=src_o, dst_o=dst_o: e.activation(out=dst_o, in_=src_o, func=AF.Copy),
                                          reads=[f"ps{ob}"], writes=["OB2"])
                                    P.add("dve", lambda e, src_d=src_d, dst_d=dst_d: e.tensor_copy(out=dst_d, in_=src_d),
                                          reads=[f"ps{db}"], writes=["DB2"])
                        step(s_fn, pv_fn)

                def combine(ub=ub):
                    for hh in range(2):
                        P.add("dve", lambda e, hh=hh: e.tensor_tensor(out=DB[:, hh, :], in0=DB[:, hh, :], in1=DB[:, 2 + hh, :], op=ALU.add),
                              reads=["DB0", "DB1"], writes=["DB0"])
                        P.add("dve", lambda e, hh=hh: e.tensor_tensor(out=DB[:, hh, :], in0=DB[:, hh, :], in1=DB[:, 4 + hh, :], op=ALU.add),
                              reads=["DB0", "DB2"], writes=["DB0"])
                        P.add("dve", lambda e, hh=hh: e.reciprocal(out=DB[:, hh, :], in_=DB[:, hh, :]), reads=["DB0"], writes=["DB0"])
                        for g in range(3):
                            eng = "dve"
                            P.add(eng, lambda e, hh=hh, g=g: e.tensor_tensor(out=oBb[:, 2 * g + hh, :], in0=OB[:, 2 * g + hh, :], in1=DB[:, hh, :], op=ALU.mult),
                                  reads=["DB0", f"OB{g}"], writes=["oBb"])
                    P.add("sp", lambda e: e.dma_start(out=MIXv[:, 6:12, ub:ub + 1024], in_=oBb), reads=["oBb"], dkey="oBb")
                step(lambda: None, combine)


            def loadC(m, half):
                ubh = 1024 * m + 512 * half
                P.add("sp", lambda e, ubh=ubh: e.dma_start(out=qc_t, in_=S["qc"][:, :, ubh:ubh + 512].rearrange("h p t -> p h t")),
                      writes=["qc_t"], dkey="qc_t")
                P.add("sp", lambda e, ubh=ubh: e.dma_start(out=kc_t, in_=S["kc"][:, :, ubh - 384:ubh + 896].rearrange("h p t -> p h t")),
                      writes=["kc_t"], dkey="kc_t")
                for jj in range(2):
                    P.add("sp", lambda e, ubh=ubh, jj=jj: e.dma_start(
                        out=vc_t[:, 5 * jj:5 * jj + 5],
                        in_=Vs[ubh - 384 + 640 * jj:ubh - 384 + 640 * (jj + 1), 1024:1536].rearrange("(j p) (h d) -> p j h d", p=128, h=4)),
                        writes=[f"vc_t{jj}"], dkey=f"vc_t{jj}")

            def computeC(m, half):
                ub = 1024 * m
                ubh = ub + 512 * half
                for i in range(4):
                    qb = (ubh // 128) + i
                    ob = 4 + (acnt[0] % 2); db = 6 + (acnt[0] % 2); acnt[0] += 1
                    rd = acnt[0] % 2
                    for dt in range(7):
                        b = s_bank(); pi = p_slot()
                        kt = i + dt

                        def s_fn(b=b, pi=pi, i=i, dt=dt, kt=kt, qb=qb):
                            P.add("pe", lambda e: e.matmul(ps[b][:, :], lhsT=ident, rhs=tabc[:, dt * 512:(dt + 1) * 512], start=True, stop=False),
                                  reads=["ident", "tabc"], writes=[f"ps{b}"])
                            for h in range(4):
                                P.add("pe", lambda e, h=h: e.matmul(
                                    ps[b][:, h * 128:(h + 1) * 128], lhsT=kc_t[:, h, kt * 128:(kt + 1) * 128],
                                    rhs=qc_t[:, h, i * 128:(i + 1) * 128], start=False, stop=(h == 3)),
                                    reads=["kc_t", "qc_t"], writes=[f"ps{b}"])
                            for qh in range(2):
                                exp_to(pi, b, 0, ("C", qb, dt, qh),
                                       view=lambda a, qh=qh: a.rearrange("p (h t) -> p h t", h=4)[:, :, qh * 64:(qh + 1) * 64])

                        def pv_fn(pi=pi, i=i, dt=dt, kt=kt, ob=ob, db=db, rd=rd, ubh=ubh):
                            for h in range(4):
                                P.add("pe", lambda e, h=h: e.matmul(
                                    ps[ob][:, h * 128:(h + 1) * 128], lhsT=vc_t[:, kt, h, :], rhs=pT[pi][:, h * 128:(h + 1) * 128],
                                    start=(dt == 0 and h == 0), stop=(dt == 6 and h == 3)),
                                    reads=["vc_t0", "vc_t1", f"pT{pi}"], writes=[f"ps{ob}"])
                            P.add("pe", lambda e: e.matmul(ps[db][:, :], lhsT=ones, rhs=pT[pi][:, :], start=(dt == 0), stop=(dt == 6)),
                                  reads=["ones", f"pT{pi}"], writes=[f"ps{db}"])
                            if dt == 6:
                                P.add("dve", lambda e: e.reciprocal(out=rden[rd], in_=ps[db][:, :]), reads=[f"ps{db}"], writes=[f"rden{rd}"])
                                P.add("dve", lambda e: e.tensor_tensor(
                                    out=oC[:, :, i * 128:(i + 1) * 128], in0=ps[ob][:, :].rearrange("p (h t) -> p h t", h=4),
                                    in1=rden[rd].rearrange("p (h t) -> p h t", h=4), op=ALU.mult),
                                    reads=[f"ps{ob}", f"rden{rd}"], writes=["oC"])
                                if i == 3:
                                    P.add("sp", lambda e: e.dma_start(out=MIXv[:, 12:16, ubh:ubh + 512], in_=oC), reads=["oC"], dkey="oC")
                        step(s_fn, pv_fn)

            ms = list(range(q_lo // 1024, q_hi // 1024))
            loadB(ms[0]); loadA(ms[0], 0)
            computeA(ms[0], 0)
            for idx, m in enumerate(ms):
                nxt = ms[idx + 1] if idx + 1 < len(ms) else None
                flush(); loadA(m, 1)
                if idx > 0:
                    computeC(ms[idx - 1], 1)
                flush(); loadC(m, 0)
                computeB(m)
                computeA(m, 1)
                flush()
                if nxt is not None:
                    loadB(nxt); loadA(nxt, 0)
                computeC(m, 0)
                flush(); loadC(m, 1)
                if nxt is not None:
                    computeA(nxt, 0)
            computeC(ms[-1], 1)
            flush()
            P.barrier()
            if stop == ("T", l):
                return True

            while len(convq) > (NJ1 if l == 0 else 0):
                conv_some(1)
            AB.o, AFP.o = ab_base, af_base
            x1 = AFP.get(16 * TT).rearrange("p (c t) -> p c t", c=16)
            rstd2 = AFP.get(TT)
            xs_t = [AFP.get(TT) for _ in range(3)]
            ost = [AFP.get(TT) for _ in range(3)]
            sg = [AFP.get(TT) for _ in range(3)]
            mix_t = AB.get(16 * TT).rearrange("p (c t) -> p c t", c=16)
            xn2 = AB.get(16 * TT).rearrange("p (c t) -> p c t", c=16)
            h_t = AB.get(NF * TT).rearrange("p (f t) -> p f t", f=NF)
            sq2 = h_t
            wo_t = [AB.get(2048).rearrange("p (c m) -> p c m", c=16) for _ in range(2)]
            wgu_t = [AB.get(2048).rearrange("p (c m) -> p c m", c=16) for _ in range(4)]
            wd_t = [AB.get(DFF).rearrange("p (f m) -> p f m", f=NF) for _ in range(2)]
            xres = xin[l].rearrange("(c p) u -> p c u", p=128)
            gffn = l * 32 + 16
            woc = 0; wgc = 0; wdc = 0; pc = 0; xc = 0; oc = 0; sc = 0
            for t in range(q_lo // TT, q_hi // TT):
                u0 = t * TT
                P.add("sp", lambda e, u0=u0: e.dma_start(out=mix_t, in_=MIXv[:, :, u0:u0 + TT]), writes=["mix_t"], dkey="mix_t")
                for o in range(16):
                    ws = woc % 2; woc += 1
                    P.add("sp", lambda e, ws=ws, o=o: e.dma_start(out=wo_t[ws], in_=WOUTb[l, o].rearrange("p (c m) -> p c m", c=16)),
                          reads=[f"D_wout{l}"], writes=[f"wo{ws}"], dkey=f"wo{ws}")
                    xi = xc % 3; xc += 1
                    P.add("sp", lambda e, xi=xi, o=o, u0=u0: e.dma_start(out=xs_t[xi], in_=xres[:, o, u0:u0 + TT]),
                          writes=[f"xs{xi}"], dkey=f"xs{xi}")
                    pb = pc % 8; pc += 1
                    for c in range(16):
                        P.add("pe", lambda e, c=c, ws=ws, pb=pb: e.matmul(ps[pb][:], lhsT=wo_t[ws][:, c, :], rhs=mix_t[:, c, :],
                                                                          start=(c == 0), stop=(c == 15)),
                              reads=[f"wo{ws}", "mix_t"], writes=[f"ps{pb}"])
                    P.add("dve", lambda e, pb=pb, xi=xi, o=o: e.tensor_tensor(out=x1[:, o, :], in0=ps[pb][:], in1=xs_t[xi], op=ALU.add),
                          reads=[f"ps{pb}", f"xs{xi}"], writes=[f"x1_{o}"])
                    P.add("act", lambda e, o=o: e.activation(out=sq2[:, o, :], in_=x1[:, o, :], func=AF.Square),
                          reads=[f"x1_{o}"], writes=[f"h{o}"])
                pb = pc % 8; pc += 1
                for o in range(16):
                    P.add("pe", lambda e, o=o, pb=pb: e.matmul(ps[pb][:], lhsT=ones, rhs=sq2[:, o, :], start=(o == 0), stop=(o == 15)),
                          reads=[f"h{o}", "ones"], writes=[f"ps{pb}"])
                P.add("act", lambda e, pb=pb: e.activation(out=rstd2, in_=ps[pb][:], func=AF.Sqrt, bias=float(EPS), scale=1.0 / D),
                      reads=[f"ps{pb}"], writes=["rstd2"])
                P.add("dve", lambda e: e.reciprocal(out=rstd2, in_=rstd2), reads=["rstd2"], writes=["rstd2"])
                for c in range(16):
                    eng = "dve"
                    P.add(eng, lambda e, c=c: e.scalar_tensor_tensor(
                        out=xn2[:, c, :], in0=x1[:, c, :], scalar=gn[:, gffn + c:gffn + c + 1], in1=rstd2,
                        op0=ALU.mult, op1=ALU.mult), reads=[f"x1_{c}", "rstd2"], writes=[f"xn2_{c}"])
                for f in range(NF):
                    if f % 3 == 1:
                        conv_some(1)
                    pbs = []
                    for wi, Wsrc in enumerate((WGb, WUb)):
                        ws = wgc % 4; wgc += 1
                        dk_ = f"D_wg{l}" if wi == 0 else f"D_wu{l}"
                        P.add("sp", lambda e, ws=ws, f=f, Wsrc=Wsrc: e.dma_start(out=wgu_t[ws], in_=Wsrc[l, f].rearrange("p (c m) -> p c m", c=16)),
                              reads=[dk_], writes=[f"wgu{ws}"], dkey=f"wgu{ws}")
                        pb = pc % 8; pc += 1
                        pbs.append(pb)
                        for c in range(16):
                            P.add("pe", lambda e, c=c, ws=ws, pb=pb: e.matmul(ps[pb][:], lhsT=wgu_t[ws][:, c, :], rhs=xn2[:, c, :],
                                                                              start=(c == 0), stop=(c == 15)),
                                  reads=[f"wgu{ws}", f"xn2_{c}"], writes=[f"ps{pb}"])
                    si = sc % 3; sc += 1
                    P.add("act", lambda e, si=si, pb=pbs[0]: e.activation(out=sg[si], in_=ps[pb][:], func=AF.Silu),
                          reads=[f"ps{pbs[0]}"], writes=[f"sg{si}"])
                    P.add("dve", lambda e, si=si, pb=pbs[1], f=f: e.tensor_tensor(out=h_t[:, f, :], in0=ps[pb][:], in1=sg[si], op=ALU.mult),
                          reads=[f"ps{pbs[1]}", f"sg{si}"], writes=[f"h{f}"])
                for o in range(16):
                    ws = wdc % 2; wdc += 1
                    P.add("sp", lambda e, ws=ws, o=o: e.dma_start(out=wd_t[ws], in_=WDb[l, o].rearrange("p (f m) -> p f m", f=NF)),
                          reads=[f"D_wd{l}"], writes=[f"wd{ws}"], dkey=f"wd{ws}")
                    pb = pc % 8; pc += 1
                    for f in range(NF):
                        P.add("pe", lambda e, f=f, ws=ws, pb=pb: e.matmul(ps[pb][:], lhsT=wd_t[ws][:, f, :], rhs=h_t[:, f, :],
                                                                          start=(f == 0), stop=(f == NF - 1)),
                              reads=[f"wd{ws}", f"h{f}"], writes=[f"ps{pb}"])
                    oi = oc % 3; oc += 1
                    P.add("dve", lambda e, pb=pb, oi=oi, o=o: e.tensor_tensor(out=ost[oi], in0=ps[pb][:], in1=x1[:, o, :], op=ALU.add),
                          reads=[f"ps{pb}", f"x1_{o}"], writes=[f"ost{oi}"])
                    if l == 0:
                        dst = X1.rearrange("(c p) u -> p c u", p=128)[:, o, u0:u0 + TT]
                    else:
                        dst = yT.rearrange("(c p) u -> p c u", p=128)[:, o, u0 - 2 * HALO:u0 - 2 * HALO + TT]
                    P.add("pool", lambda e, oi=oi, dst=dst: e.dma_start(out=dst, in_=ost[oi]), reads=[f"ost{oi}"], dkey=f"ost{oi}")
            P.barrier()
            return stop == ("F", l)

        for l_ in range(2):
            if emit_layer(l_):
                break

        with nc.Block() as block:
            P.emit(nc, block)
    return nc


def _blk(w, cols):
    K = w.shape[0]
    return np.ascontiguousarray(w.reshape(K // 128, 128, -1).transpose(1, 0, 2).reshape(128, -1))


def prepare_inputs(x_prompt, x_sample, norm_mix, w_in, qk_norm, sink_a, rpb_c, w_out, norm_ffn, w_gate, w_up, w_down):
    f32 = np.float32
    xs = np.concatenate([np.asarray(x_prompt, f32).reshape(-1, D), np.asarray(x_sample, f32).reshape(-1, D)], axis=0)
    NTOK = xs.shape[0]
    xpad = np.zeros((NTOK + 4 * HALO, D), f32)
    xpad[2 * HALO:2 * HALO + NTOK] = xs
    w_in = np.asarray(w_in, f32); w_out = np.asarray(w_out, f32)
    w_gate = np.asarray(w_gate, f32); w_up = np.asarray(w_up, f32); w_down = np.asarray(w_down, f32)
    cg = {"qa": 0, "ka": 768, "va": 1024, "qb": 1280, "kb": 2048, "vb": 2816, "qc": 3584, "kc": 4096, "vc": 4608}
    WQK = np.zeros((2, 28, 128, 2048), f32)
    WV = np.zeros((2, 3, 128, 8192), f32)
    WOUT = np.zeros((2, 16, 128, 2048), f32)
    WG = np.zeros((2, NF, 128, 2048), f32)
    WU = np.zeros((2, NF, 128, 2048), f32)
    WD = np.zeros((2, 16, 128, DFF), f32)
    for l in range(2):
        for b, (nm, hi) in enumerate(QKBLKS):
            c0 = cg[nm] + hi * 128
            WQK[l, b] = _blk(w_in[l][:, c0:c0 + 128], 128)
        wvv = np.concatenate([w_in[l][:, 1024:1280], w_in[l][:, 2816:3584], w_in[l][:, 4608:5120]], axis=1)
        for b in range(3):
            WV[l, b] = _blk(wvv[:, b * 512:(b + 1) * 512], 512)
        for o in range(16):
            WOUT[l, o] = _blk(w_out[l][:, o * 128:(o + 1) * 128], 128)
            WD[l, o] = _blk(w_down[l][:, o * 128:(o + 1) * 128], 128)
        for f in range(NF):
            WG[l, f] = _blk(w_gate[l][:, f * 128:(f + 1) * 128], 128)
            WU[l, f] = _blk(w_up[l][:, f * 128:(f + 1) * 128], 128)
    GN = np.zeros((128, 64), f32)
    for l in range(2):
        GN[:, l * 32:l * 32 + 16] = np.asarray(norm_mix, f32)[l].reshape(16, 128).T
        GN[:, l * 32 + 16:l * 32 + 32] = np.asarray(norm_ffn, f32)[l].reshape(16, 128).T
    GQK = np.ascontiguousarray(np.asarray(qk_norm, f32).reshape(12, 128).T)
    SINK = np.ascontiguousarray(np.broadcast_to(np.asarray(sink_a, f32).reshape(1, 12), (128, 12)))
    cidx = _ctab_index()
    TABC = np.zeros((2, 128, 7 * 512), f32)
    for l in range(2):
        ext = np.concatenate([np.asarray(rpb_c, f32)[l].reshape(-1), np.array([NEGM], f32)])
        tab = ext[cidx]
        TABC[l] = tab.transpose(1, 0, 2, 3).reshape(128, 7 * 512)
    k = np.arange(128)[:, None]; q = np.arange(128)[None, :]
    CONST = np.zeros((128, 512), f32)
    CONST[:, 0:128] = np.eye(128, dtype=f32)
    CONST[:, 128:256] = np.where(k >= q, 0.0, NEGM)
    CONST[:, 256:384] = np.where(k <= q, 0.0, NEGM)
    for m_ in range(16):
        CONST[m_ + 16, 384 + m_] = -1.0
        CONST[m_, 384 + 16 + m_] = 1.0
    inv = (ROPE_THETA ** (-np.arange(0, 32, 2, dtype=np.float32) / 32)).astype(np.float32)
    pos = np.arange(SEQ, dtype=np.float32)
    ang = pos[:, None] * inv[None, :]
    cos_t = np.cos(ang).astype(f32).T
    sin_t = np.sin(ang).astype(f32).T
    common = dict(WQK=WQK, WV=WV, WOUT=WOUT, WG=WG, WU=WU, WD=WD, GN=GN, GQK=GQK, SINK=SINK, TABC=TABC, CONST=CONST)
    in_maps = []
    for c in range(NCORES):
        g0 = c * OWN
        win = xpad[g0:g0 + U]
        m = dict(common)
        m["xT"] = np.ascontiguousarray(win.T)
        gpos = (np.arange(U) + g0 - 2 * HALO) % SEQ
        COSW = np.ones((128, U), f32)
        COSW[0:16] = cos_t[:, gpos]; COSW[16:32] = cos_t[:, gpos]
        SINW = np.zeros((32, U), f32)
        SINW[0:16] = sin_t[:, gpos]; SINW[16:32] = sin_t[:, gpos]
        m["COSW"] = COSW; m["SINW"] = SINW
        m["COLS"] = _build_cols(c)
        in_maps.append(m)
    return in_maps


_NC_CACHE = {}


def kernel(x_prompt, x_sample, norm_mix, w_in, qk_norm, sink_a, rpb_c, w_out, norm_ffn, w_gate, w_up, w_down):
    in_maps = prepare_inputs(x_prompt, x_sample, norm_mix, w_in, qk_norm, sink_a, rpb_c, w_out, norm_ffn, w_gate, w_up, w_down)
    nc = build_program()
    res = run_bass_kernel_spmd(nc, in_maps, core_ids=list(range(NCORES)))
    ys = [np.asarray(res.results[c]["yT"], np.float32).T for c in range(NCORES)]
    y = np.concatenate(ys, axis=0)
    nb = np.asarray(x_prompt).shape[0] * SEQ
    y_prompt = np.ascontiguousarray(y[:nb].reshape(np.asarray(x_prompt).shape))
    y_sample = np.ascontiguousarray(y[nb:].reshape(np.asarray(x_sample).shape))
    return (y_prompt, y_sample)
```

```python
import numpy as np
import concourse.bass as bass
import concourse.mybir as mybir
from concourse.bass_utils import run_bass_kernel_spmd

F32 = mybir.dt.float32
BF16 = mybir.dt.bfloat16
AF = mybir.ActivationFunctionType
ALU = mybir.AluOpType

NCORES = 8
D = 2048
NSEQ = 3
SEQ = 8192
OWN = 3072
HALO = 1024
U = OWN + 4 * HALO
TT = 512
DFF = 5632
NF = DFF // 128
NC16 = D // 128
EPS = 1e-6
NEGM = -30000.0
GRID_W = 64
ROPE_THETA = 500000.0

REGIONS = [(0, U, HALO, U - HALO), (HALO, U - HALO, 2 * HALO, U - 2 * HALO)]

KBLKS = [("ka", i) for i in range(2)] + [("kb", i) for i in range(6)] + [("kc", i) for i in range(4)]
QBLKS = [("qa", i) for i in range(6)] + [("qb", i) for i in range(6)] + [("qc", i) for i in range(4)]
QKBLKS = KBLKS + QBLKS

SAME_ENGINE_WAITS = False


def _col_index():
    idx = {}
    n = 0
    for qb in range(HALO // 128, (U - HALO) // 128):
        for j in range(3):
            idx[("A", qb, j)] = n; n += 1
    for qb in range(HALO // 128, (U - HALO) // 128):
        for j in range(2):
            idx[("B0", qb, j)] = n; n += 1
    for m in range(1, 6):
        for qh in range(2):
            for j in range(2):
                idx[("B1", m, qh, j)] = n; n += 1
    for m in range(1, 6):
        for j in range(2):
            idx[("B2", m, j)] = n; n += 1
    for qb in range(HALO // 128, (U - HALO) // 128):
        for dt in range(7):
            for qh in range(2):
                idx[("C", qb, dt, qh)] = n; n += 1
    return idx, n


COLIDX, NCOLS = _col_index()


def _seqid(g):
    g = np.asarray(g)
    return np.where((g >= 0) & (g < NSEQ * SEQ), g // SEQ, -1)


def _build_cols(core):
    base = core * OWN - 2 * HALO
    cols = np.zeros((128, NCOLS), np.float32)
    p = np.arange(128)

    def setcol(key, ktok_u, qtok_u, extra_valid=None):
        gq = base + qtok_u
        sq = int(_seqid(gq))
        if sq < 0:
            return
        gk = base + ktok_u
        valid = (_seqid(gk) == sq)
        if extra_valid is not None:
            valid = valid & extra_valid
        cols[:len(valid), COLIDX[key]] = np.where(valid, 0.0, NEGM)

    for qb in range(HALO // 128, (U - HALO) // 128):
        u0 = qb * 128
        for j in range(3):
            setcol(("A", qb, j), u0 + 128 * (j - 1) + p, u0)
        for j in range(2):
            setcol(("B0", qb, j), u0 - 64 + 128 * j + p, u0)
        gq0 = base + u0
        for dt in range(7):
            for qh in range(2):
                rq = (gq0 % SEQ) // GRID_W + qh
                ks = min(max(rq - 4, 0), SEQ // GRID_W - 8)
                ktok = u0 + 128 * (dt - 3) + p
                gk = base + ktok
                kr = (gk % SEQ) // GRID_W
                ev = (kr >= ks) & (kr < ks + 8)
                setcol(("C", qb, dt, qh), ktok, u0, ev)
    for m in range(1, 6):
        for qh in range(2):
            J0 = 256 * m + 128 * qh
            for j in range(2):
                setcol(("B1", m, qh, j), 4 * (J0 - 64 + 128 * j + p), 4 * J0)
        J0 = 64 * m
        setcol(("B2", m, 0), 16 * (J0 - 64 + p), 16 * J0)
        setcol(("B2", m, 1), 16 * (J0 + 64 + p[:64]), 16 * J0)
    return cols


def _ctab_index():
    k = np.arange(128)[:, None]
    q = np.arange(128)[None, :]
    kro, kc = k // 64, k % 64
    qro, qc = q // 64, q % 64
    cstart = np.clip(qc - 8, 0, GRID_W - 16)
    cvalid = (kc >= cstart) & (kc < cstart + 16)
    relc = np.clip(kc - qc + 15, 0, 30)
    out = np.zeros((7, 128, 4, 128), np.int64)
    for dt in range(7):
        dr = 2 * (dt - 3) + kro - qro
        relr = np.clip(dr + 7, 0, 14)
        for h in range(4):
            ii = h * 15 * 31 + relr * 31 + relc
            out[dt, :, h, :] = np.where(cvalid, ii, 4 * 15 * 31)
    return out


class _Op:
    __slots__ = ("eng", "fn", "deps", "dma", "dkey", "sem", "val", "hasdep", "i", "persist")


class Prog:
    ENG = ("pe", "act", "dve", "pool", "sp")

    def __init__(self):
        self.ops = []
        self.lastw = {}
        self.readers = {}
        self.lastop = {}
        self.lastdma = {}

    def add(self, eng, fn, reads=(), writes=(), dkey=None, persist=False):
        op = _Op()
        op.persist = persist
        op.eng, op.fn, op.dma, op.dkey = eng, fn, dkey is not None, dkey
        op.sem = None; op.val = 0; op.hasdep = False; op.i = len(self.ops)
        deps = set()
        for r in reads:
            w = self.lastw.get(r)
            if w is not None:
                deps.add(w)
        for w_ in writes:
            lw = self.lastw.get(w_)
            if lw is not None:
                deps.add(lw)
            for rd in self.readers.get(w_, ()):
                deps.add(rd)
        for r in reads:
            self.readers.setdefault(r, []).append(op)
        for w_ in writes:
            self.lastw[w_] = op
            self.readers[w_] = []
        op.deps = deps
        for d in deps:
            d.hasdep = True
        self.ops.append(op)
        if op.dma:
            self.lastdma[dkey] = op
        else:
            self.lastop[eng] = op
        return op

    def barrier(self):
        pend = set(o for o in (set(self.lastop.values()) | set(self.lastdma.values())) if not o.persist)
        keepw = {k: v for k, v in self.lastw.items() if v.persist}
        keepd = {k: v for k, v in self.lastdma.items() if v.persist}
        for e in self.ENG:
            op = _Op()
            op.persist = False
            op.eng, op.fn, op.dma, op.dkey = e, None, False, None
            op.sem = None; op.val = 0; op.hasdep = False; op.i = len(self.ops)
            op.deps = set(pend)
            for d in pend:
                d.hasdep = True
            self.ops.append(op)
        self.lastw.clear(); self.readers.clear(); self.lastop.clear(); self.lastdma.clear()
        self.lastw.update(keepw); self.lastdma.update(keepd)

    def emit(self, nc, block):
        engs = {"pe": "tensor", "act": "scalar", "dve": "vector", "pool": "gpsimd", "sp": "sync"}
        esem = {e: nc.alloc_semaphore(f"e_{e}") for e in self.ENG}
        ecnt = {e: 0 for e in self.ENG}
        dsem = {}
        dcnt = {}
        for op in self.ops:
            if op.fn is None:
                continue
            if op.dma:
                if op.dkey not in dsem:
                    dsem[op.dkey] = nc.alloc_semaphore("d_" + str(len(dsem)))
                    dcnt[op.dkey] = 0
                dcnt[op.dkey] += 16
                op.sem, op.val = dsem[op.dkey], dcnt[op.dkey]
            elif op.hasdep:
                ecnt[op.eng] += 1
                op.sem, op.val = esem[op.eng], ecnt[op.eng]
        self.nsem = len(dsem) + 5
        per = {e: [o for o in self.ops if o.eng == e] for e in self.ENG}

        def run(e):
            def body(eng):
                waited = {}
                for op in per[e]:
                    for d in sorted(op.deps, key=lambda o: o.i):
                        if d.sem is None:
                            continue
                        if (not d.dma) and d.eng == e and (e == "pe" or not SAME_ENGINE_WAITS):
                            continue
                        k = id(d.sem)
                        if waited.get(k, 0) < d.val:
                            eng.wait_ge(d.sem, d.val)
                            waited[k] = d.val
                    if op.fn is None:
                        continue
                    ins = op.fn(eng)
                    if op.dma:
                        ins.then_inc(op.sem, 16)
                    elif op.hasdep:
                        ins.then_inc(op.sem, 1)
            return body

        for e in self.ENG:
            getattr(block, engs[e])(run(e))


class Arena:
    def __init__(self, t, n):
        self.t, self.n, self.o = t, n, 0

    def reset(self):
        self.o = 0

    def get(self, n):
        assert self.o + n <= self.n, (self.o, n, self.n)
        v = self.t[:, self.o:self.o + n]
        self.o += n
        return v


def build_program(debug=False, stop=None):
    nc = bass.Bass("TRN2", target_bir_lowering=False)
    P = Prog()

    def dram(name, shape, dt=F32, kind="ExternalInput"):
        return nc.dram_tensor(name, list(shape), dt, kind=kind).ap()

    xT = dram("xT", [D, U])
    WQK = dram("WQK", [2, 28, 128, 2048])
    WV = dram("WV", [2, 3, 128, 8192])
    WOUT = dram("WOUT", [2, 16, 128, 2048])
    WG = dram("WG", [2, NF, 128, 2048])
    WU = dram("WU", [2, NF, 128, 2048])
    WD = dram("WD", [2, 16, 128, DFF])
    GN = dram("GN", [128, 2 * 2 * 16])
    GQK = dram("GQK", [128, 2 * 6])
    SINK = dram("SINK", [128, 2 * 6])
    COSW = dram("COSW", [128, U])
    SINW = dram("SINW", [32, U])
    COLS = dram("COLS", [128, NCOLS])
    TABC = dram("TABC", [2, 128, 7 * 512])
    CONST = dram("CONST", [128, 128 * 3 + 128])
    yT = dram("yT", [D, OWN], kind="ExternalOutput")
    ik = "ExternalOutput" if debug else "Internal"
    WQKb = dram("WQKb", [2, 28, 128, 2048], BF16, "Internal")
    WVb = dram("WVb", [2, 3, 128, 8192], BF16, "Internal")
    WOUTb = dram("WOUTb", [2, 16, 128, 2048], BF16, "Internal")
    WGb = dram("WGb", [2, NF, 128, 2048], BF16, "Internal")
    WUb = dram("WUb", [2, NF, 128, 2048], BF16, "Internal")
    WDb = dram("WDb", [2, 16, 128, DFF], BF16, "Internal")
    S = {
        "qa": dram("s_qa", [6, 128, U], BF16, ik), "ka": dram("s_ka", [2, 128, U], BF16, ik),
        "qc": dram("s_qc", [4, 128, U], BF16, ik), "kc": dram("s_kc", [4, 128, U], BF16, ik),
        "qb0": dram("s_qb0", [2, 128, U], BF16, ik), "kb0": dram("s_kb0", [2, 128, U], BF16, ik),
        "qb1": dram("s_qb1", [2, 4, 128, U // 4], BF16, ik), "kb1": dram("s_kb1", [2, 4, 128, U // 4], BF16, ik),
        "qb2": dram("s_qb2", [2, 16, 128, U // 16], BF16, ik), "kb2": dram("s_kb2", [2, 16, 128, U // 16], BF16, ik),
    }
    Vs = dram("s_v", [U, 1536], BF16, ik)
    MIX = dram("s_mix", [D, U], BF16, ik)
    X1 = dram("s_x1", [D, U], F32, ik)

    import contextlib
    es = contextlib.ExitStack()
    with es:
        NB16 = 63000
        NF32 = 15000
        abf_t = es.enter_context(nc.sbuf_tensor("abf", [128, NB16], BF16))
        af_t = es.enter_context(nc.sbuf_tensor("af32", [128, NF32], F32))
        cbf_t = es.enter_context(nc.sbuf_tensor("cbf", [128, 128 * 2 + 512 * 2 + 7 * 512 + 128], BF16))
        cf_t = es.enter_context(nc.sbuf_tensor("cf", [128, NCOLS + 64 + 12 + 12 + 12 + 768 + 512], F32))
        ps = [es.enter_context(nc.psum_tensor(f"ps{i}", [128, 512], F32)) for i in range(8)]
        AB = Arena(abf_t, NB16)
        AFP = Arena(af_t, NF32)
        CB = Arena(cbf_t, 128 * 2 + 512 * 2 + 7 * 512 + 128)
        CF = Arena(cf_t, NCOLS + 64 + 12 + 12 + 12 + 768 + 512)

        ident = CB.get(128)
        ones = CB.get(128)
        triGE = CB.get(512)
        triLE = CB.get(512)
        tabc = CB.get(7 * 512)
        rotT = CB.get(128)
        cols = CF.get(NCOLS)
        gn = CF.get(64)
        gqk = CF.get(12)
        gqs = CF.get(12)
        esink = CF.get(12)
        esinkb = CF.get(768)

        AB.reset(); AFP.reset()
        cst32 = CF.get(512)
        P.add("sp", lambda e: e.dma_start(out=cst32, in_=CONST[:, :]), writes=["cst32"], dkey="cst32")
        P.add("sp", lambda e: e.dma_start(out=cols, in_=COLS[:, :]), writes=["cols"], dkey="cols")
        P.add("sp", lambda e: e.dma_start(out=gn, in_=GN[:, :]), writes=["gn"], dkey="gn")
        P.add("sp", lambda e: e.dma_start(out=gqk, in_=GQK[:, :]), writes=["gqk"], dkey="gqk")
        P.add("sp", lambda e: e.dma_start(out=esink, in_=SINK[:, :]), writes=["esink"], dkey="esink")
        P.add("dve", lambda e: e.tensor_copy(out=ident, in_=cst32[:, 0:128]), reads=["cst32"], writes=["ident"])
        P.add("dve", lambda e: e.memset(ones, 1.0), writes=["ones"])
        for r in range(4):
            P.add("dve", lambda e, r=r: e.tensor_copy(out=triGE[:, r * 128:(r + 1) * 128], in_=cst32[:, 128:256]),
                  reads=["cst32"], writes=["triGE"])
            P.add("dve", lambda e, r=r: e.tensor_copy(out=triLE[:, r * 128:(r + 1) * 128], in_=cst32[:, 256:384]),
                  reads=["cst32"], writes=["triLE"])
        P.add("act", lambda e: e.activation(out=esink, in_=esink, func=AF.Exp), reads=["esink"], writes=["esink"])
        P.add("dve", lambda e: e.tensor_copy(out=gqs, in_=gqk), reads=["gqk"], writes=["gqs"])
        for l in range(2):
            for mx in range(3):
                cc = l * 6 + mx * 2
                P.add("dve", lambda e, cc=cc: e.tensor_scalar(out=gqs[:, cc:cc + 1], in0=gqk[:, cc:cc + 1],
                                                             scalar1=float(128 ** -0.5), scalar2=None, op0=ALU.mult),
                      reads=["gqk", "gqs"], writes=["gqs"])

        convq = []

        def conv(src, dst, nblk, key, step=8):
            for b0 in range(0, nblk, step):
                b1 = min(nblk, b0 + step)
                convq.append(lambda b0=b0, b1=b1, src=src, dst=dst, key=key: P.add(
                    "pool", lambda e: e.dma_start(out=dst[b0:b1], in_=src[b0:b1]), writes=[key], dkey=key, persist=True))
        for l in range(2):
            conv(WQK[l], WQKb[l], 28, f"D_wqk{l}", 2 if l == 0 else 1)
            conv(WV[l], WVb[l], 3, f"D_wv{l}", 1)
            conv(WOUT[l], WOUTb[l], 16, f"D_wout{l}", 1)
            conv(WG[l], WGb[l], NF, f"D_wg{l}", 1)
            conv(WU[l], WUb[l], NF, f"D_wu{l}", 1)
            conv(WD[l], WDb[l], 16, f"D_wd{l}", 1)
        NJ1 = 28 + 3 + 16 + NF + NF + 16

        def conv_some(n):
            for _ in range(n):
                if convq:
                    convq.pop(0)()
        conv_some(17)
        P.barrier()

        xin = [xT, X1]
        xout = [X1, None]

        def emit_layer(l):
            kv_lo, kv_hi, q_lo, q_hi = REGIONS[l]
            AB.reset(); AFP.reset()
            t32 = AFP.get(7 * 512)
            P.add("sp", lambda e, l=l: e.dma_start(out=t32, in_=TABC[l]), writes=["t32"], dkey="t32")
            P.add("dve", lambda e: e.tensor_copy(out=tabc, in_=t32), reads=["t32"], writes=["tabc"])
            for h in range(6):
                P.add("dve", lambda e, h=h, l=l: e.tensor_scalar(
                    out=esinkb[:, h * 128:(h + 1) * 128], in0=cst32[:, 0:128], scalar1=0.0,
                    scalar2=esink[:, l * 6 + h:l * 6 + h + 1], op0=ALU.mult, op1=ALU.add),
                    reads=["cst32", "esink"], writes=["esinkb"])
            rg = [AB.get(128) for _ in range(4)]
            gcols = [l * 6 + 0, l * 6 + 1, l * 6 + 2, l * 6 + 3]
            for j in range(4):
                P.add("dve", lambda e, j=j: e.tensor_scalar(
                    out=rg[j], in0=cst32[:, 384:512], scalar1=gqs[:, gcols[j]:gcols[j] + 1], scalar2=None,
                    op0=ALU.mult), reads=["cst32", "gqs"], writes=[f"rg{j}"])
            P.barrier()
            ab_base, af_base = AB.o, 0
            AFP.o = 0

            xt = AFP.get(16 * TT).rearrange("p (c t) -> p c t", c=16)
            rstd = AFP.get(TT)
            cosb = AFP.get(TT)
            sinb = AFP.get(TT)
            cosg = [AFP.get(TT) for _ in range(4)]
            rsh = [AFP.get(TT) for _ in range(2)]
            t1 = [AFP.get(TT) for _ in range(2)]
            t2 = [AFP.get(TT) for _ in range(2)]
            sq = AB.get(16 * TT).rearrange("p (c t) -> p c t", c=16)
            xn = [AB.get(16 * TT).rearrange("p (c t) -> p c t", c=16) for _ in range(2)]
            wqk = [AB.get(2048).rearrange("p (c m) -> p c m", c=16) for _ in range(4)]
            wv = [AB.get(8192).rearrange("p (c m) -> p c m", c=16) for _ in range(2)]
            sqh = [AB.get(TT) for _ in range(3)]
            qbf = [AB.get(TT) for _ in range(3)]
            outq = [AB.get(TT) for _ in range(4)]
            vst = [AB.get(1536) for _ in range(2)]
            xin_v = xin[l].rearrange("(c p) u -> p c u", p=128)
            gmix = l * 32
            wslot = 0
            vslot = 0
            hcount = 0
            vcount = 0
            pscnt = 0
            tiles = list(range(kv_lo // TT, kv_hi // TT))
            cnt = {"w": 0, "v": 0, "h": 0, "vc": 0, "ps": 0}

            def prepA(ti):
                u0 = tiles[ti] * TT
                xs = ti % 2
                P.add("sp", lambda e: e.dma_start(out=xt, in_=xin_v[:, :, u0:u0 + TT]), writes=["xt"], dkey="xt")
                for c in range(16):
                    P.add("act", lambda e, c=c: e.activation(out=sq[:, c, :], in_=xt[:, c, :], func=AF.Square),
                          reads=["xt"], writes=[f"sq{c}"])
                for c in range(16):
                    P.add("pe", lambda e, c=c: e.matmul(ps[7][:], lhsT=ones, rhs=sq[:, c, :], start=(c == 0), stop=(c == 15)),
                          reads=[f"sq{c}"], writes=["ps7"])
                P.add("act", lambda e: e.activation(out=rstd, in_=ps[7][:], func=AF.Sqrt, bias=float(EPS), scale=1.0 / D),
                      reads=["ps7"], writes=["rstd"])
                P.add("dve", lambda e: e.reciprocal(out=rstd, in_=rstd), reads=["rstd"], writes=["rstd"])
                for c in range(16):
                    P.add("dve", lambda e, c=c: e.scalar_tensor_tensor(
                        out=xn[xs][:, c, :], in0=xt[:, c, :], scalar=gn[:, gmix + c:gmix + c + 1], in1=rstd,
                        op0=ALU.mult, op1=ALU.mult), reads=["xt", "rstd"], writes=[f"xn{xs}_{c}"])

            def prepB(ti):
                u0 = tiles[ti] * TT
                P.add("sp", lambda e: e.dma_start(out=cosb, in_=COSW[:, u0:u0 + TT]), writes=["cosb"], dkey="cosb")
                P.add("sp", lambda e: e.dma_start(out=sinb[0:32, :], in_=SINW[:, u0:u0 + TT]), writes=["sinb"], dkey="sinb")
                for j in range(4):
                    P.add("pool", lambda e, j=j: e.tensor_scalar(out=cosg[j], in0=cosb, scalar1=gqs[:, gcols[j]:gcols[j] + 1],
                                                                 scalar2=None, op0=ALU.mult),
                          reads=["cosb"], writes=[f"cosg{j}"])

            def v_section(ti, vbs=(0, 1, 2)):
                u0 = tiles[ti] * TT
                xs = ti % 2
                xnr = [f"xn{xs}_{c}" for c in range(16)]
                for vb in vbs:
                    vs_ = cnt["v"] % 2; cnt["v"] += 1
                    P.add("sp", lambda e, vs_=vs_, vb=vb: e.dma_start(out=wv[vs_], in_=WVb[l, vb].rearrange("p (c m) -> p c m", c=16)),
                          reads=[f"D_wv{l}"], writes=[f"wv{vs_}"], dkey=f"wv{vs_}")
                    for s4 in range(4):
                        pb = cnt["ps"] % 4; cnt["ps"] += 1
                        for c in range(16):
                            P.add("pe", lambda e, c=c, vs_=vs_, pb=pb, s4=s4: e.matmul(
                                ps[pb][:], lhsT=xn[xs][:, c, s4 * 128:(s4 + 1) * 128], rhs=wv[vs_][:, c, :],
                                start=(c == 0), stop=(c == 15)),
                                reads=[f"wv{vs_}", xnr[c]], writes=[f"ps{pb}"])
                        vo = s4 % 2
                        ce = "act"
                        cnt["vc"] += 1
                        if ce == "act":
                            P.add("act", lambda e, pb=pb, vo=vo, vb=vb: e.activation(out=vst[vo][:, vb * 512:(vb + 1) * 512], in_=ps[pb][:], func=AF.Copy),
                                  reads=[f"ps{pb}"], writes=[f"vst{vo}_{vb}"])
                        else:
                            P.add("dve", lambda e, pb=pb, vo=vo, vb=vb: e.tensor_copy(out=vst[vo][:, vb * 512:(vb + 1) * 512], in_=ps[pb][:]),
                                  reads=[f"ps{pb}"], writes=[f"vst{vo}_{vb}"])
                        P.add("pool", lambda e, vo=vo, vb=vb, s4=s4: e.dma_start(
                            out=Vs[u0 + s4 * 128:u0 + (s4 + 1) * 128, vb * 512:(vb + 1) * 512], in_=vst[vo][:, vb * 512:(vb + 1) * 512]),
                            reads=[f"vst{vo}_{vb}"], dkey=f"vst{vo}_{vb}")

            def head(ti, nm, hi):
                u0 = tiles[ti] * TT
                xs = ti % 2
                xnr = [f"xn{xs}_{c}" for c in range(16)]
                gb = QKBLKS.index((nm, hi))
                ws = cnt["w"] % 4; cnt["w"] += 1
                pb = cnt["ps"] % 4; cnt["ps"] += 1
                hs = cnt["h"] % 2
                h3 = cnt["h"] % 3
                os_ = cnt["h"] % 4
                cnt["h"] += 1
                isq = nm[0] == "q"
                mixer = {"a": 0, "b": 1, "c": 2}[nm[1]]
                gcol = l * 6 + mixer * 2 + (0 if isq else 1)
                sb_ = 4 + hs
                rb = 6 + hs
                if nm[1] == "b" and hi >= 2:
                    dil = 4 if hi < 4 else 16
                    ov = outq[os_].rearrange("p (r j) -> p j r", r=dil)
                else:
                    ov = outq[os_]

                def stage1():
                    P.add("sp", lambda e: e.dma_start(out=wqk[ws], in_=WQKb[l, gb].rearrange("p (c m) -> p c m", c=16)),
                          reads=[f"D_wqk{l}"], writes=[f"wqk{ws}"], dkey=f"wqk{ws}")
                    for c in range(16):
                        P.add("pe", lambda e, c=c: e.matmul(
                            ps[pb][:], lhsT=wqk[ws][:, c, :], rhs=xn[xs][:, c, :], start=(c == 0), stop=(c == 15)),
                            reads=[f"wqk{ws}", xnr[c]], writes=[f"ps{pb}"])
                    P.add("act", lambda e: e.activation(out=sqh[h3], in_=ps[pb][:], func=AF.Square),
                          reads=[f"ps{pb}"], writes=[f"sqh{h3}"])
                    if mixer != 2:
                        P.add("act", lambda e: e.activation(out=qbf[h3], in_=ps[pb][:], func=AF.Copy),
                              reads=[f"ps{pb}"], writes=[f"qbf{h3}"])

                def stage2():
                    P.add("pe", lambda e: e.matmul(ps[sb_][:], lhsT=ones, rhs=sqh[h3], start=True, stop=True),
                          reads=[f"sqh{h3}"], writes=[f"ps{sb_}"])
                    P.add("act", lambda e: e.activation(out=rsh[hs], in_=ps[sb_][:], func=AF.Identity, bias=float(EPS), scale=1.0 / 128),
                          reads=[f"ps{sb_}"], writes=[f"rsh{hs}"])
                    P.add("act", lambda e: e.activation(out=rsh[hs], in_=rsh[hs], func=AF.Ln), reads=[f"rsh{hs}"], writes=[f"rsh{hs}"])
                    P.add("act", lambda e: e.activation(out=rsh[hs], in_=rsh[hs], func=AF.Exp, scale=-0.5), reads=[f"rsh{hs}"], writes=[f"rsh{hs}"])
                    if mixer == 2:
                        P.add("dve", lambda e: e.scalar_tensor_tensor(
                            out=ov, in0=ps[pb][:], scalar=gqs[:, gcol:gcol + 1], in1=rsh[hs], op0=ALU.mult, op1=ALU.mult),
                            reads=[f"ps{pb}", f"rsh{hs}"], writes=[f"outq{os_}"])
                    else:
                        j = mixer * 2 + (0 if isq else 1)
                        P.add("pe", lambda e: e.matmul(ps[rb][0:32, :], lhsT=rg[j][:, 0:32], rhs=qbf[h3], start=True, stop=True),
                              reads=[f"qbf{h3}", f"rg{j}"], writes=[f"ps{rb}"])
                        P.add("dve", lambda e: e.tensor_tensor(out=t1[hs], in0=ps[pb][:], in1=cosg[j], op=ALU.mult),
                              reads=[f"ps{pb}", f"cosg{j}"], writes=[f"t1{hs}"])
                        P.add("dve", lambda e: e.tensor_tensor(out=t2[hs][0:32, :], in0=ps[rb][0:32, :], in1=sinb[0:32, :], op=ALU.mult),
                              reads=[f"ps{rb}", "sinb"], writes=[f"t2{hs}"])
                        P.add("pool", lambda e: e.tensor_tensor(out=t1[hs][0:32, :], in0=t1[hs][0:32, :], in1=t2[hs][0:32, :], op=ALU.add),
                              reads=[f"t1{hs}", f"t2{hs}"], writes=[f"t1{hs}"])
                        P.add("pool", lambda e: e.tensor_tensor(out=ov, in0=t1[hs], in1=rsh[hs], op=ALU.mult),
                              reads=[f"t1{hs}", f"rsh{hs}"], writes=[f"outq{os_}"])
                    if nm[1] == "b":
                        g = hi // 2
                        hh = hi % 2
                        if g == 0:
                            dst = S[nm + "0"][hh][:, u0:u0 + TT]
                            src = outq[os_]
                        else:
                            dil_ = 4 if g == 1 else 16
                            n = TT // dil_
                            dst = S[nm + str(g)][hh][:, :, (u0 // dil_):(u0 // dil_) + n].rearrange("r p j -> p r j")
                            src = outq[os_].rearrange("p (r j) -> p r j", r=dil_)
                    else:
                        dst = S[nm][hi][:, u0:u0 + TT]
                        src = outq[os_]
                    P.add("pool", lambda e: e.dma_start(out=dst, in_=src), reads=[f"outq{os_}"], dkey=f"outq{os_}")
                return stage1, stage2

            prepA(0)
            prepB(0)
            for ti, t in enumerate(tiles):
                u0 = t * TT
                need_q = (u0 >= q_lo) and (u0 < q_hi)
                far = (ti == 0) or (ti == len(tiles) - 1)
                v_section(ti, (1,) if far else (0, 1, 2))
                if l == 0:
                    conv_some(1)
                if ti + 1 < len(tiles):
                    prepA(ti + 1)
                blks = QKBLKS if need_q else KBLKS
                if far:
                    blks = [("kb", 4), ("kb", 5)]
                pend = []
                for hidx, (nm, hi) in enumerate(blks):
                    if l == 0 and hidx % 2 == 1:
                        conv_some(1)
                    s1, s2 = head(ti, nm, hi)
                    s1()
                    pend.append(s2)
                    if len(pend) > 2:
                        pend.pop(0)()
                while pend:
                    pend.pop(0)()
                if ti + 1 < len(tiles):
                    prepB(ti + 1)
            P.barrier()
            if stop == ("P", l):
                return True

            AB.o, AFP.o = ab_base, af_base
            qa_t = AB.get(6 * 512).rearrange("p (h t) -> p h t", h=6)
            ka_t = AB.get(2 * 768).rearrange("p (h t) -> p h t", h=2)
            va_t = AB.get(6 * 256).rearrange("p (j h d) -> p j h d", j=6, h=2)
            qb0_t = AB.get(2 * 1024).rearrange("p (h t) -> p h t", h=2)
            kb0_t = AB.get(2 * 1152).rearrange("p (h t) -> p h t", h=2)
            vb0_t = AB.get(9 * 256).rearrange("p (j h d) -> p j h d", j=9, h=2)
            qb1_t = AB.get(2 * 4 * 256).rearrange("p (h r t) -> p h r t", h=2, r=4)
            kb1_t = AB.get(2 * 4 * 384).rearrange("p (h r t) -> p h r t", h=2, r=4)
            vb1_t = AB.get(4 * 3 * 256).rearrange("p (r j h d) -> p r j h d", r=4, j=3, h=2)
            qb2_t = AB.get(2 * 16 * 64).rearrange("p (h r t) -> p h r t", h=2, r=16)
            kb2_t = AB.get(2 * 16 * 192).rearrange("p (h r t) -> p h r t", h=2, r=16)
            vb2a_t = AB.get(16 * 256).rearrange("p (r h d) -> p r h d", r=16, h=2)
            vb2b_t = AB.get(16 * 256).rearrange("p (r h d) -> p r h d", r=16, h=2)
            qc_t = AB.get(4 * 512).rearrange("p (h t) -> p h t", h=4)
            kc_t = AB.get(4 * 1280).rearrange("p (h t) -> p h t", h=4)
            vc_t = AB.get(10 * 512).rearrange("p (j h d) -> p j h d", j=10, h=4)
            pT = [AB.get(512) for _ in range(3)]
            oA = AB.get(6 * 512).rearrange("p (h t) -> p h t", h=6)
            oC = AB.get(4 * 512).rearrange("p (h t) -> p h t", h=4)
            oBb = AB.get(6 * 1024).rearrange("p (h t) -> p h t", h=6)
            rden = [AFP.get(512) for _ in range(2)]
            OB = AFP.get(6 * 1024).rearrange("p (g t) -> p g t", g=6)
            DB = AFP.get(6 * 1024).rearrange("p (g t) -> p g t", g=6)
            MIXv = MIX.rearrange("(h p) u -> p h u", p=128)
            scnt = [0]
            acnt = [0]

            def s_bank():
                b = scnt[0] % 4; scnt[0] += 1
                return b

            def exp_to(pt_i, b, ncol, colkey, kpart=128, view=None):
                ci = COLIDX[colkey]
                if view is None:
                    o_ap, i_ap = pT[pt_i][0:kpart, 0:ncol], ps[b][0:kpart, 0:ncol]
                else:
                    o_ap, i_ap = view(pT[pt_i][0:kpart, :]), view(ps[b][0:kpart, :])
                P.add("act", lambda e: e.activation(out=o_ap, in_=i_ap, func=AF.Exp, bias=cols[0:kpart, ci:ci + 1], scale=1.0),
                      reads=[f"ps{b}"], writes=[f"pT{pt_i}"])

            pcnt = [0]

            def p_slot():
                s = pcnt[0] % 3; pcnt[0] += 1
                return s

            LOOK = 2
            fifo = []

            def step(s_fn, pv_fn):
                s_fn()
                fifo.append(pv_fn)
                while len(fifo) > LOOK:
                    fifo.pop(0)()

            def flush():
                while fifo:
                    fifo.pop(0)()

            def tri_of(j):
                return triGE if j == 0 else triLE

            def loadB(m):
                ub = 1024 * m
                P.add("sp", lambda e, ub=ub: e.dma_start(out=qb0_t, in_=S["qb0"][:, :, ub:ub + 1024].rearrange("h p t -> p h t")),
                      writes=["qb0_t"], dkey="qb0_t")
                P.add("sp", lambda e, ub=ub: e.dma_start(out=kb0_t, in_=S["kb0"][:, :, ub - 64:ub + 1088].rearrange("h p t -> p h t")),
                      writes=["kb0_t"], dkey="kb0_t")
                P.add("sp", lambda e, ub=ub: e.dma_start(
                    out=vb0_t, in_=Vs[ub - 64:ub + 1088, 256:512].rearrange("(j p) (h d) -> p j h d", p=128, h=2)),
                    writes=["vb0_t"], dkey="vb0_t")
                J1 = 256 * m
                for hh in range(2):
                    P.add("sp", lambda e, hh=hh, J1=J1: e.dma_start(out=qb1_t[:, hh], in_=S["qb1"][hh][:, :, J1:J1 + 256].rearrange("r p t -> p r t")),
                          writes=[f"qb1_t{hh}"], dkey=f"qb1_t{hh}")
                    P.add("sp", lambda e, hh=hh, J1=J1: e.dma_start(out=kb1_t[:, hh], in_=S["kb1"][hh][:, :, J1 - 64:J1 + 320].rearrange("r p t -> p r t")),
                          writes=[f"kb1_t{hh}"], dkey=f"kb1_t{hh}")
                for r in range(4):
                    t0 = 4 * (J1 - 64) + r
                    P.add("sp", lambda e, r=r, t0=t0: e.dma_start(
                        out=vb1_t[:, r], in_=Vs[t0:t0 + 4 * 384 - 3:4, 512:768].rearrange("(j p) (h d) -> p j h d", p=128, h=2)),
                        writes=[f"vb1_t{r}"], dkey=f"vb1_t{r}")
                J2 = 64 * m
                for hh in range(2):
                    P.add("sp", lambda e, hh=hh, J2=J2: e.dma_start(out=qb2_t[:, hh], in_=S["qb2"][hh][:, :, J2:J2 + 64].rearrange("r p t -> p r t")),
                          writes=[f"qb2_t{hh}"], dkey=f"qb2_t{hh}")
                    P.add("sp", lambda e, hh=hh, J2=J2: e.dma_start(out=kb2_t[:, hh], in_=S["kb2"][hh][:, :, J2 - 64:J2 + 128].rearrange("r p t -> p r t")),
                          writes=[f"kb2_t{hh}"], dkey=f"kb2_t{hh}")
                t0 = 16 * (J2 - 64)
                P.add("sp", lambda e, t0=t0: e.dma_start(
                    out=vb2a_t, in_=Vs[t0:t0 + 2048, 768:1024].rearrange("(p r) (h d) -> p r h d", r=16, h=2)),
                    writes=["vb2a_t"], dkey="vb2a_t")
                t1_ = 16 * (J2 + 64)
                P.add("sp", lambda e, t1_=t1_: e.dma_start(
                    out=vb2b_t[0:64], in_=Vs[t1_:t1_ + 1024, 768:1024].rearrange("(p r) (h d) -> p r h d", r=16, h=2)),
                    writes=["vb2b_t"], dkey="vb2b_t")


            def loadA(m, half):
                ubh = 1024 * m + 512 * half
                P.add("sp", lambda e, ubh=ubh: e.dma_start(out=qa_t, in_=S["qa"][:, :, ubh:ubh + 512].rearrange("h p t -> p h t")),
                      writes=["qa_t"], dkey="qa_t")
                P.add("sp", lambda e, ubh=ubh: e.dma_start(out=ka_t, in_=S["ka"][:, :, ubh - 128:ubh + 640].rearrange("h p t -> p h t")),
                      writes=["ka_t"], dkey="ka_t")
                P.add("sp", lambda e, ubh=ubh: e.dma_start(
                    out=va_t, in_=Vs[ubh - 128:ubh + 640, 0:256].rearrange("(j p) (h d) -> p j h d", p=128, h=2)),
                    writes=["va_t"], dkey="va_t")

            def computeA(m, half):
                ub = 1024 * m
                ubh = ub + 512 * half
                for i in range(4):
                    qb = (ubh // 128) + i
                    for g in range(2):
                        ob = 4 + (acnt[0] % 2); db = 6 + (acnt[0] % 2); acnt[0] += 1
                        rd = acnt[0] % 2
                        for j in range(3):
                            b = s_bank(); pi = p_slot()

                            def s_fn(b=b, pi=pi, g=g, i=i, j=j, qb=qb):
                                if j != 1:
                                    tri = tri_of(j)
                                    P.add("pe", lambda e: e.matmul(ps[b][:, 0:384], lhsT=ident, rhs=tri[:, 0:384], start=True, stop=False),
                                          reads=["ident", "triGE", "triLE"], writes=[f"ps{b}"])
                                P.add("pe", lambda e: e.matmul(
                                    ps[b][:, 0:384], lhsT=ka_t[:, g, (i + j) * 128:(i + j + 1) * 128],
                                    rhs=qa_t[:, 3 * g:3 * g + 3, i * 128:(i + 1) * 128], start=(j == 1), stop=True),
                                    reads=["ka_t", "qa_t"], writes=[f"ps{b}"])
                                exp_to(pi, b, 384, ("A", qb, j))

                            def pv_fn(pi=pi, g=g, i=i, j=j, ob=ob, db=db, rd=rd, ubh=ubh):
                                P.add("pe", lambda e: e.matmul(
                                    ps[ob][:, 0:384], lhsT=va_t[:, i + j, g, :], rhs=pT[pi][:, 0:384], start=(j == 0), stop=(j == 2)),
                                    reads=["va_t", f"pT{pi}"], writes=[f"ps{ob}"])
                                P.add("pe", lambda e: e.matmul(
                                    ps[db][:, 0:384], lhsT=ones, rhs=pT[pi][:, 0:384], start=(j == 0), stop=(j == 2)),
                                    reads=["ones", f"pT{pi}"], writes=[f"ps{db}"])
                                if j == 2:
                                    P.add("dve", lambda e: e.tensor_tensor(
                                        out=rden[rd][:, 0:384], in0=ps[db][:, 0:384], in1=esinkb[:, 384 * g:384 * g + 384], op=ALU.add),
                                        reads=[f"ps{db}", "esinkb"], writes=[f"rden{rd}"])
                                    P.add("dve", lambda e: e.reciprocal(out=rden[rd][:, 0:384], in_=rden[rd][:, 0:384]),
                                          reads=[f"rden{rd}"], writes=[f"rden{rd}"])
                                    P.add("dve", lambda e: e.tensor_tensor(
                                        out=oA[:, 3 * g:3 * g + 3, i * 128:(i + 1) * 128],
                                        in0=ps[ob][:, 0:384].rearrange("p (h t) -> p h t", h=3),
                                        in1=rden[rd][:, 0:384].rearrange("p (h t) -> p h t", h=3), op=ALU.mult),
                                        reads=[f"ps{ob}", f"rden{rd}"], writes=["oA"])
                                    if i == 3 and g == 1:
                                        P.add("sp", lambda e: e.dma_start(out=MIXv[:, 0:6, ubh:ubh + 512], in_=oA), reads=["oA"], dkey="oA")
                            step(s_fn, pv_fn)

            def computeB(m):
                ub = 1024 * m
                for i in range(8):
                    qb = (ub // 128) + i
                    ob = 4 + (acnt[0] % 2); db = 6 + (acnt[0] % 2); acnt[0] += 1
                    for j in range(2):
                        b = s_bank(); pi = p_slot()

                        def s_fn(b=b, pi=pi, i=i, j=j, qb=qb):
                            tri = tri_of(j)
                            P.add("pe", lambda e: e.matmul(ps[b][:, 0:256], lhsT=ident, rhs=tri[:, 0:256], start=True, stop=False),
                                  reads=["ident", "triGE", "triLE"], writes=[f"ps{b}"])
                            for hh in range(2):
                                P.add("pe", lambda e, hh=hh: e.matmul(
                                    ps[b][:, hh * 128:(hh + 1) * 128], lhsT=kb0_t[:, hh, (i + j) * 128:(i + j + 1) * 128],
                                    rhs=qb0_t[:, hh, i * 128:(i + 1) * 128], start=False, stop=(hh == 1)),
                                    reads=["kb0_t", "qb0_t"], writes=[f"ps{b}"])
                            exp_to(pi, b, 256, ("B0", qb, j))

                        def pv_fn(pi=pi, i=i, j=j, ob=ob, db=db):
                            for hh in range(2):
                                P.add("pe", lambda e, hh=hh: e.matmul(
                                    ps[ob][:, hh * 128:(hh + 1) * 128], lhsT=vb0_t[:, i + j, hh, :], rhs=pT[pi][:, hh * 128:(hh + 1) * 128],
                                    start=(j == 0 and hh == 0), stop=(j == 1 and hh == 1)), reads=["vb0_t", f"pT{pi}"], writes=[f"ps{ob}"])
                            P.add("pe", lambda e: e.matmul(ps[db][:, 0:256], lhsT=ones, rhs=pT[pi][:, 0:256], start=(j == 0), stop=(j == 1)),
                                  reads=["ones", f"pT{pi}"], writes=[f"ps{db}"])
                            if j == 1:
                                P.add("act", lambda e: e.activation(
                                    out=OB[:, 0:2, i * 128:(i + 1) * 128], in_=ps[ob][:, 0:256].rearrange("p (h t) -> p h t", h=2), func=AF.Copy),
                                    reads=[f"ps{ob}"], writes=["OB0"])
                                P.add("dve", lambda e: e.tensor_copy(
                                    out=DB[:, 0:2, i * 128:(i + 1) * 128], in_=ps[db][:, 0:256].rearrange("p (h t) -> p h t", h=2)),
                                    reads=[f"ps{db}"], writes=["DB0"])
                        step(s_fn, pv_fn)

                for qh in range(2):
                    for r in range(4):
                        ob = 4 + (acnt[0] % 2); db = 6 + (acnt[0] % 2); acnt[0] += 1
                        o0 = 512 * qh + r
                        for j in range(2):
                            b = s_bank(); pi = p_slot()
                            k0 = 128 * qh + 128 * j

                            def s_fn(b=b, pi=pi, r=r, j=j, qh=qh, k0=k0):
                                tri = tri_of(j)
                                P.add("pe", lambda e: e.matmul(ps[b][:, 0:256], lhsT=ident, rhs=tri[:, 0:256], start=True, stop=False),
                                      reads=["ident", "triGE", "triLE"], writes=[f"ps{b}"])
                                for hh in range(2):
                                    P.add("pe", lambda e, hh=hh: e.matmul(
                                        ps[b][:, hh * 128:(hh + 1) * 128], lhsT=kb1_t[:, hh, r, k0:k0 + 128],
                                        rhs=qb1_t[:, hh, r, qh * 128:(qh + 1) * 128], start=False, stop=(hh == 1)),
                                        reads=[f"kb1_t{hh}", f"qb1_t{hh}"], writes=[f"ps{b}"])
                                exp_to(pi, b, 256, ("B1", m, qh, j))

                            def pv_fn(pi=pi, r=r, j=j, qh=qh, ob=ob, db=db, o0=o0):
                                for hh in range(2):
                                    P.add("pe", lambda e, hh=hh: e.matmul(
                                        ps[ob][:, hh * 128:(hh + 1) * 128], lhsT=vb1_t[:, r, qh + j, hh, :],
                                        rhs=pT[pi][:, hh * 128:(hh + 1) * 128], start=(j == 0 and hh == 0), stop=(j == 1 and hh == 1)),
                                        reads=[f"vb1_t{r}", f"pT{pi}"], writes=[f"ps{ob}"])
                                P.add("pe", lambda e: e.matmul(ps[db][:, 0:256], lhsT=ones, rhs=pT[pi][:, 0:256], start=(j == 0), stop=(j == 1)),
                                      reads=["ones", f"pT{pi}"], writes=[f"ps{db}"])
                                if j == 1:
                                    P.add("act", lambda e: e.activation(
                                        out=OB[:, 2:4, o0:o0 + 509:4], in_=ps[ob][:, 0:256].rearrange("p (h t) -> p h t", h=2), func=AF.Copy),
                                        reads=[f"ps{ob}"], writes=["OB1"])
                                    P.add("dve", lambda e: e.tensor_copy(
                                        out=DB[:, 2:4, o0:o0 + 509:4], in_=ps[db][:, 0:256].rearrange("p (h t) -> p h t", h=2)),
                                        reads=[f"ps{db}"], writes=["DB1"])
                            step(s_fn, pv_fn)

                for rg4 in range(4):
                    ob = 4 + (acnt[0] % 2); db = 6 + (acnt[0] % 2); acnt[0] += 1
                    for j in range(2):
                        kp = 128 if j == 0 else 64
                        b = s_bank(); pi = p_slot()

                        def s_fn(b=b, pi=pi, j=j, kp=kp, rg4=rg4):
                            tri = tri_of(j)
                            for half in range(2):
                                P.add("pe", lambda e, half=half: e.matmul(
                                    ps[b][0:kp, 256 * half:256 * half + 256].rearrange("p (a t) -> p a t", a=4),
                                    lhsT=ident[0:kp, 0:kp], rhs=tri[0:kp, :].rearrange("p (a t) -> p a t", a=4)[:, :, 0:64],
                                    start=(half == 0), stop=False),
                                    reads=["ident", "triGE", "triLE"], writes=[f"ps{b}"])
                            for rr in range(4):
                                r = rg4 * 4 + rr
                                for hh in range(2):
                                    c0 = (rr * 2 + hh) * 64
                                    P.add("pe", lambda e, hh=hh, r=r, c0=c0, rr=rr: e.matmul(
                                        ps[b][0:kp, c0:c0 + 64], lhsT=kb2_t[:, hh, r, 128 * j:128 * j + kp],
                                        rhs=qb2_t[:, hh, r, :], start=False, stop=(rr == 3 and hh == 1)),
                                        reads=[f"kb2_t{hh}", f"qb2_t{hh}"], writes=[f"ps{b}"])
                            exp_to(pi, b, 512, ("B2", m, j), kpart=kp)

                        def pv_fn(pi=pi, j=j, kp=kp, rg4=rg4, ob=ob, db=db):
                            vt = vb2a_t if j == 0 else vb2b_t
                            for rr in range(4):
                                r = rg4 * 4 + rr
                                for hh in range(2):
                                    c0 = (rr * 2 + hh) * 64
                                    P.add("pe", lambda e, hh=hh, r=r, c0=c0: e.matmul(
                                        ps[ob][:, c0:c0 + 64], lhsT=vt[0:kp, r, hh, :], rhs=pT[pi][0:kp, c0:c0 + 64],
                                        start=(j == 0 and c0 == 0), stop=(j == 1 and c0 == 448)),
                                        reads=["vb2a_t", "vb2b_t", f"pT{pi}"], writes=[f"ps{ob}"])
                            P.add("pe", lambda e: e.matmul(ps[db][:, :], lhsT=ones[0:kp, :], rhs=pT[pi][0:kp, :], start=(j == 0), stop=(j == 1)),
                                  reads=["ones", f"pT{pi}"], writes=[f"ps{db}"])
                            if j == 1:
                                for hh in range(2):
                                    src_o = ps[ob][:, :].rearrange("p (rr h t) -> p rr h t", rr=4, h=2)[:, :, hh, :]
                                    src_d = ps[db][:, :].rearrange("p (rr h t) -> p rr h t", rr=4, h=2)[:, :, hh, :]
                                    dst_o = OB[:, 4 + hh, :].rearrange("p (t r) -> p r t", r=16)[:, rg4 * 4:rg4 * 4 + 4, :]
                                    dst_d = DB[:, 4 + hh, :].rearrange("p (t r) -> p r t", r=16)[:, rg4 * 4:rg4 * 4 + 4, :]
                                    P.add("act", lambda e, src_o=src_o, dst_o=dst_o: e.activation(out=dst_o, in_=src_o, func=AF.Copy),
                                          reads=[f"ps{ob}"], writes=["OB2"])
                                    P.add("dve", lambda e, src_d=src_d, dst_d=dst_d: e.tensor_copy(out=dst_d, in_=src_d),
                                          reads=[f"ps{db}"], writes=["DB2"])
                        step(s_fn, pv_fn)

                def combine(ub=ub):
                    for hh in range(2):
                        P.add("dve", lambda e, hh=hh: e.tensor_tensor(out=DB[:, hh, :], in0=DB[:, hh, :], in1=DB[:, 2 + hh, :], op=ALU.add),
                              reads=["DB0", "DB1"], writes=["DB0"])
                        P.add("dve", lambda e, hh=hh: e.tensor_tensor(out=DB[:, hh, :], in0=DB[:, hh, :], in1=DB[:, 4 + hh, :], op=ALU.add),
                              reads=["DB0", "DB2"], writes=["DB0"])
                        P.add("dve", lambda e, hh=hh: e.reciprocal(out=DB[:, hh, :], in_=DB[:, hh, :]), reads=["DB0"], writes=["DB0"])
                        for g in range(3):
                            eng = "dve"
                            P.add(eng, lambda e, hh=hh, g=g: e.tensor_tensor(out=oBb[:, 2 * g + hh, :], in0=OB[:, 2 * g + hh, :], in1=DB[:, hh, :], op=ALU.mult),
                                  reads=["DB0", f"OB{g}"], writes=["oBb"])
                    P.add("sp", lambda e: e.dma_start(out=MIXv[:, 6:12, ub:ub + 1024], in_=oBb), reads=["oBb"], dkey="oBb")
                step(lambda: None, combine)


            def loadC(m, half):
                ubh = 1024 * m + 512 * half
                P.add("sp", lambda e, ubh=ubh: e.dma_start(out=qc_t, in_=S["qc"][:, :, ubh:ubh + 512].rearrange("h p t -> p h t")),
                      writes=["qc_t"], dkey="qc_t")
                P.add("sp", lambda e, ubh=ubh: e.dma_start(out=kc_t, in_=S["kc"][:, :, ubh - 384:ubh + 896].rearrange("h p t -> p h t")),
                      writes=["kc_t"], dkey="kc_t")
                for jj in range(2):
                    P.add("sp", lambda e, ubh=ubh, jj=jj: e.dma_start(
                        out=vc_t[:, 5 * jj:5 * jj + 5],
                        in_=Vs[ubh - 384 + 640 * jj:ubh - 384 + 640 * (jj + 1), 1024:1536].rearrange("(j p) (h d) -> p j h d", p=128, h=4)),
                        writes=[f"vc_t{jj}"], dkey=f"vc_t{jj}")

            def computeC(m, half):
                ub = 1024 * m
                ubh = ub + 512 * half
                for i in range(4):
                    qb = (ubh // 128) + i
                    ob = 4 + (acnt[0] % 2); db = 6 + (acnt[0] % 2); acnt[0] += 1
                    rd = acnt[0] % 2
                    for dt in range(7):
                        b = s_bank(); pi = p_slot()
                        kt = i + dt

                        def s_fn(b=b, pi=pi, i=i, dt=dt, kt=kt, qb=qb):
                            P.add("pe", lambda e: e.matmul(ps[b][:, :], lhsT=ident, rhs=tabc[:, dt * 512:(dt + 1) * 512], start=True, stop=False),
                                  reads=["ident", "tabc"], writes=[f"ps{b}"])
                            for h in range(4):
                                P.add("pe", lambda e, h=h: e.matmul(
                                    ps[b][:, h * 128:(h + 1) * 128], lhsT=kc_t[:, h, kt * 128:(kt + 1) * 128],
                                    rhs=qc_t[:, h, i * 128:(i + 1) * 128], start=False, stop=(h == 3)),
                                    reads=["kc_t", "qc_t"], writes=[f"ps{b}"])
                            for qh in range(2):
                                exp_to(pi, b, 0, ("C", qb, dt, qh),
                                       view=lambda a, qh=qh: a.rearrange("p (h t) -> p h t", h=4)[:, :, qh * 64:(qh + 1) * 64])

                        def pv_fn(pi=pi, i=i, dt=dt, kt=kt, ob=ob, db=db, rd=rd, ubh=ubh):
                            for h in range(4):
                                P.add("pe", lambda e, h=h: e.matmul(
                                    ps[ob][:, h * 128:(h + 1) * 128], lhsT=vc_t[:, kt, h, :], rhs=pT[pi][:, h * 128:(h + 1) * 128],
                                    start=(dt == 0 and h == 0), stop=(dt == 6 and h == 3)),
                                    reads=["vc_t0", "vc_t1", f"pT{pi}"], writes=[f"ps{ob}"])
                            P.add("pe", lambda e: e.matmul(ps[db][:, :], lhsT=ones, rhs=pT[pi][:, :], start=(dt == 0), stop=(dt == 6)),
                                  reads=["ones", f"pT{pi}"], writes=[f"ps{db}"])
                            if dt == 6:
                                P.add("dve", lambda e: e.reciprocal(out=rden[rd], in_=ps[db][:, :]), reads=[f"ps{db}"], writes=[f"rden{rd}"])
                                P.add("dve", lambda e: e.tensor_tensor(
                                    out=oC[:, :, i * 128:(i + 1) * 128], in0=ps[ob][:, :].rearrange("p (h t) -> p h t", h=4),
                                    in1=rden[rd].rearrange("p (h t) -> p h t", h=4), op=ALU.mult),
                                    reads=[f"ps{ob}", f"rden{rd}"], writes=["oC"])
                                if i == 3:
                                    P.add("sp", lambda e: e.dma_start(out=MIXv[:, 12:16, ubh:ubh + 512], in_=oC), reads=["oC"], dkey="oC")
                        step(s_fn, pv_fn)

            ms = list(range(q_lo // 1024, q_hi // 1024))
            loadB(ms[0]); loadA(ms[0], 0)
            computeA(ms[0], 0)
            for idx, m in enumerate(ms):
                nxt = ms[idx + 1] if idx + 1 < len(ms) else None
                flush(); loadA(m, 1)
                if idx > 0:
                    computeC(ms[idx - 1], 1)
                flush(); loadC(m, 0)
                computeB(m)
                computeA(m, 1)
                flush()
                if nxt is not None:
                    loadB(nxt); loadA(nxt, 0)
                computeC(m, 0)
                flush(); loadC(m, 1)
                if nxt is not None:
                    computeA(nxt, 0)
            computeC(ms[-1], 1)
            flush()
            P.barrier()
            if stop == ("T", l):
                return True

            while len(convq) > (NJ1 if l == 0 else 0):
                conv_some(1)
            AB.o, AFP.o = ab_base, af_base
            x1 = AFP.get(16 * TT).rearrange("p (c t) -> p c t", c=16)
            rstd2 = AFP.get(TT)
            xs_t = [AFP.get(TT) for _ in range(3)]
            ost = [AFP.get(TT) for _ in range(3)]
            sg = [AFP.get(TT) for _ in range(3)]
            mix_t = AB.get(16 * TT).rearrange("p (c t) -> p c t", c=16)
            xn2 = AB.get(16 * TT).rearrange("p (c t) -> p c t", c=16)
            h_t = AB.get(NF * TT).rearrange("p (f t) -> p f t", f=NF)
            sq2 = h_t
            wo_t = [AB.get(2048).rearrange("p (c m) -> p c m", c=16) for _ in range(2)]
            wgu_t = [AB.get(2048).rearrange("p (c m) -> p c m", c=16) for _ in range(4)]
            wd_t = [AB.get(DFF).rearrange("p (f m) -> p f m", f=NF) for _ in range(2)]
            xres = xin[l].rearrange("(c p) u -> p c u", p=128)
            gffn = l * 32 + 16
            woc = 0; wgc = 0; wdc = 0; pc = 0; xc = 0; oc = 0; sc = 0
            for t in range(q_lo // TT, q_hi // TT):
                u0 = t * TT
                P.add("sp", lambda e, u0=u0: e.dma_start(out=mix_t, in_=MIXv[:, :, u0:u0 + TT]), writes=["mix_t"], dkey="mix_t")
                for o in range(16):
                    ws = woc % 2; woc += 1
                    P.add("sp", lambda e, ws=ws, o=o: e.dma_start(out=wo_t[ws], in_=WOUTb[l, o].rearrange("p (c m) -> p c m", c=16)),
                          reads=[f"D_wout{l}"], writes=[f"wo{ws}"], dkey=f"wo{ws}")
                    xi = xc % 3; xc += 1
                    P.add("sp", lambda e, xi=xi, o=o, u0=u0: e.dma_start(out=xs_t[xi], in_=xres[:, o, u0:u0 + TT]),
                          writes=[f"xs{xi}"], dkey=f"xs{xi}")
                    pb = pc % 8; pc += 1
                    for c in range(16):
                        P.add("pe", lambda e, c=c, ws=ws, pb=pb: e.matmul(ps[pb][:], lhsT=wo_t[ws][:, c, :], rhs=mix_t[:, c, :],
                                                                          start=(c == 0), stop=(c == 15)),
                              reads=[f"wo{ws}", "mix_t"], writes=[f"ps{pb}"])
                    P.add("dve", lambda e, pb=pb, xi=xi, o=o: e.tensor_tensor(out=x1[:, o, :], in0=ps[pb][:], in1=xs_t[xi], op=ALU.add),
                          reads=[f"ps{pb}", f"xs{xi}"], writes=[f"x1_{o}"])
                    P.add("act", lambda e, o=o: e.activation(out=sq2[:, o, :], in_=x1[:, o, :], func=AF.Square),
                          reads=[f"x1_{o}"], writes=[f"h{o}"])
                pb = pc % 8; pc += 1
                for o in range(16):
                    P.add("pe", lambda e, o=o, pb=pb: e.matmul(ps[pb][:], lhsT=ones, rhs=sq2[:, o, :], start=(o == 0), stop=(o == 15)),
                          reads=[f"h{o}", "ones"], writes=[f"ps{pb}"])
                P.add("act", lambda e, pb=pb: e.activation(out=rstd2, in_=ps[pb][:], func=AF.Sqrt, bias=float(EPS), scale=1.0 / D),
                      reads=[f"ps{pb}"], writes=["rstd2"])
                P.add("dve", lambda e: e.reciprocal(out=rstd2, in_=rstd2), reads=["rstd2"], writes=["rstd2"])
                for c in range(16):
                    eng = "dve"
                    P.add(eng, lambda e, c=c: e.scalar_tensor_tensor(
                        out=xn2[:, c, :], in0=x1[:, c, :], scalar=gn[:, gffn + c:gffn + c + 1], in1=rstd2,
                        op0=ALU.mult, op1=ALU.mult), reads=[f"x1_{c}", "rstd2"], writes=[f"xn2_{c}"])
                for f in range(NF):
                    if f % 3 == 1:
                        conv_some(1)
                    pbs = []
                    for wi, Wsrc in enumerate((WGb, WUb)):
                        ws = wgc % 4; wgc += 1
                        dk_ = f"D_wg{l}" if wi == 0 else f"D_wu{l}"
                        P.add("sp", lambda e, ws=ws, f=f, Wsrc=Wsrc: e.dma_start(out=wgu_t[ws], in_=Wsrc[l, f].rearrange("p (c m) -> p c m", c=16)),
                              reads=[dk_], writes=[f"wgu{ws}"], dkey=f"wgu{ws}")
                        pb = pc % 8; pc += 1
                        pbs.append(pb)
                        for c in range(16):
                            P.add("pe", lambda e, c=c, ws=ws, pb=pb: e.matmul(ps[pb][:], lhsT=wgu_t[ws][:, c, :], rhs=xn2[:, c, :],
                                                                              start=(c == 0), stop=(c == 15)),
                                  reads=[f"wgu{ws}", f"xn2_{c}"], writes=[f"ps{pb}"])
                    si = sc % 3; sc += 1
                    P.add("act", lambda e, si=si, pb=pbs[0]: e.activation(out=sg[si], in_=ps[pb][:], func=AF.Silu),
                          reads=[f"ps{pbs[0]}"], writes=[f"sg{si}"])
                    P.add("dve", lambda e, si=si, pb=pbs[1], f=f: e.tensor_tensor(out=h_t[:, f, :], in0=ps[pb][:], in1=sg[si], op=ALU.mult),
                          reads=[f"ps{pbs[1]}", f"sg{si}"], writes=[f"h{f}"])
                for o in range(16):
                    ws = wdc % 2; wdc += 1
                    P.add("sp", lambda e, ws=ws, o=o: e.dma_start(out=wd_t[ws], in_=WDb[l, o].rearrange("p (f m) -> p f m", f=NF)),
                          reads=[f"D_wd{l}"], writes=[f"wd{ws}"], dkey=f"wd{ws}")
                    pb = pc % 8; pc += 1
                    for f in range(NF):
                        P.add("pe", lambda e, f=f, ws=ws, pb=pb: e.matmul(ps[pb][:], lhsT=wd_t[ws][:, f, :], rhs=h_t[:, f, :],
                                                                          start=(f == 0), stop=(f == NF - 1)),
                              reads=[f"wd{ws}", f"h{f}"], writes=[f"ps{pb}"])
                    oi = oc % 3; oc += 1
                    P.add("dve", lambda e, pb=pb, oi=oi, o=o: e.tensor_tensor(out=ost[oi], in0=ps[pb][:], in1=x1[:, o, :], op=ALU.add),
                          reads=[f"ps{pb}", f"x1_{o}"], writes=[f"ost{oi}"])
                    if l == 0:
                        dst = X1.rearrange("(c p) u -> p c u", p=128)[:, o, u0:u0 + TT]
                    else:
                        dst = yT.rearrange("(c p) u -> p c u", p=128)[:, o, u0 - 2 * HALO:u0 - 2 * HALO + TT]
                    P.add("pool", lambda e, oi=oi, dst=dst: e.dma_start(out=dst, in_=ost[oi]), reads=[f"ost{oi}"], dkey=f"ost{oi}")
            P.barrier()
            return stop == ("F", l)

        for l_ in range(2):
            if emit_layer(l_):
                break

        with nc.Block() as block:
            P.emit(nc, block)
    return nc


def _blk(w, cols):
    K = w.shape[0]
    return np.ascontiguousarray(w.reshape(K // 128, 128, -1).transpose(1, 0, 2).reshape(128, -1))


def prepare_inputs(x_prompt, x_sample, norm_mix, w_in, qk_norm, sink_a, rpb_c, w_out, norm_ffn, w_gate, w_up, w_down):
    f32 = np.float32
    xs = np.concatenate([np.asarray(x_prompt, f32).reshape(-1, D), np.asarray(x_sample, f32).reshape(-1, D)], axis=0)
    NTOK = xs.shape[0]
    xpad = np.zeros((NTOK + 4 * HALO, D), f32)
    xpad[2 * HALO:2 * HALO + NTOK] = xs
    w_in = np.asarray(w_in, f32); w_out = np.asarray(w_out, f32)
    w_gate = np.asarray(w_gate, f32); w_up = np.asarray(w_up, f32); w_down = np.asarray(w_down, f32)
    cg = {"qa": 0, "ka": 768, "va": 1024, "qb": 1280, "kb": 2048, "vb": 2816, "qc": 3584, "kc": 4096, "vc": 4608}
    WQK = np.zeros((2, 28, 128, 2048), f32)
    WV = np.zeros((2, 3, 128, 8192), f32)
    WOUT = np.zeros((2, 16, 128, 2048), f32)
    WG = np.zeros((2, NF, 128, 2048), f32)
    WU = np.zeros((2, NF, 128, 2048), f32)
    WD = np.zeros((2, 16, 128, DFF), f32)
    for l in range(2):
        for b, (nm, hi) in enumerate(QKBLKS):
            c0 = cg[nm] + hi * 128
            WQK[l, b] = _blk(w_in[l][:, c0:c0 + 128], 128)
        wvv = np.concatenate([w_in[l][:, 1024:1280], w_in[l][:, 2816:3584], w_in[l][:, 4608:5120]], axis=1)
        for b in range(3):
            WV[l, b] = _blk(wvv[:, b * 512:(b + 1) * 512], 512)
        for o in range(16):
            WOUT[l, o] = _blk(w_out[l][:, o * 128:(o + 1) * 128], 128)
            WD[l, o] = _blk(w_down[l][:, o * 128:(o + 1) * 128], 128)
        for f in range(NF):
            WG[l, f] = _blk(w_gate[l][:, f * 128:(f + 1) * 128], 128)
            WU[l, f] = _blk(w_up[l][:, f * 128:(f + 1) * 128], 128)
    GN = np.zeros((128, 64), f32)
    for l in range(2):
        GN[:, l * 32:l * 32 + 16] = np.asarray(norm_mix, f32)[l].reshape(16, 128).T
        GN[:, l * 32 + 16:l * 32 + 32] = np.asarray(norm_ffn, f32)[l].reshape(16, 128).T
    GQK = np.ascontiguousarray(np.asarray(qk_norm, f32).reshape(12, 128).T)
    SINK = np.ascontiguousarray(np.broadcast_to(np.asarray(sink_a, f32).reshape(1, 12), (128, 12)))
    cidx = _ctab_index()
    TABC = np.zeros((2, 128, 7 * 512), f32)
    for l in range(2):
        ext = np.concatenate([np.asarray(rpb_c, f32)[l].reshape(-1), np.array([NEGM], f32)])
        tab = ext[cidx]
        TABC[l] = tab.transpose(1, 0, 2, 3).reshape(128, 7 * 512)
    k = np.arange(128)[:, None]; q = np.arange(128)[None, :]
    CONST = np.zeros((128, 512), f32)
    CONST[:, 0:128] = np.eye(128, dtype=f32)
    CONST[:, 128:256] = np.where(k >= q, 0.0, NEGM)
    CONST[:, 256:384] = np.where(k <= q, 0.0, NEGM)
    for m_ in range(16):
        CONST[m_ + 16, 384 + m_] = -1.0
        CONST[m_, 384 + 16 + m_] = 1.0
    inv = (ROPE_THETA ** (-np.arange(0, 32, 2, dtype=np.float32) / 32)).astype(np.float32)
    pos = np.arange(SEQ, dtype=np.float32)
    ang = pos[:, None] * inv[None, :]
    cos_t = np.cos(ang).astype(f32).T
    sin_t = np.sin(ang).astype(f32).T
    common = dict(WQK=WQK, WV=WV, WOUT=WOUT, WG=WG, WU=WU, WD=WD, GN=GN, GQK=GQK, SINK=SINK, TABC=TABC, CONST=CONST)
    in_maps = []
    for c in range(NCORES):
        g0 = c * OWN
        win = xpad[g0:g0 + U]
        m = dict(common)
        m["xT"] = np.ascontiguousarray(win.T)
        gpos = (np.arange(U) + g0 - 2 * HALO) % SEQ
        COSW = np.ones((128, U), f32)
        COSW[0:16] = cos_t[:, gpos]; COSW[16:32] = cos_t[:, gpos]
        SINW = np.zeros((32, U), f32)
        SINW[0:16] = sin_t[:, gpos]; SINW[16:32] = sin_t[:, gpos]
        m["COSW"] = COSW; m["SINW"] = SINW
        m["COLS"] = _build_cols(c)
        in_maps.append(m)
    return in_maps


_NC_CACHE = {}


def kernel(x_prompt, x_sample, norm_mix, w_in, qk_norm, sink_a, rpb_c, w_out, norm_ffn, w_gate, w_up, w_down):
    in_maps = prepare_inputs(x_prompt, x_sample, norm_mix, w_in, qk_norm, sink_a, rpb_c, w_out, norm_ffn, w_gate, w_up, w_down)
    nc = build_program()
    res = run_bass_kernel_spmd(nc, in_maps, core_ids=list(range(NCORES)))
    ys = [np.asarray(res.results[c]["yT"], np.float32).T for c in range(NCORES)]
    y = np.concatenate(ys, axis=0)
    nb = np.asarray(x_prompt).shape[0] * SEQ
    y_prompt = np.ascontiguousarray(y[:nb].reshape(np.asarray(x_prompt).shape))
    y_sample = np.ascontiguousarray(y[nb:].reshape(np.asarray(x_sample).shape))
    return (y_prompt, y_sample)
```

```python
import numpy as np
import concourse.bass as bass
import concourse.mybir as mybir
from concourse.bass_utils import run_bass_kernel_spmd

F32 = mybir.dt.float32
BF16 = mybir.dt.bfloat16
AF = mybir.ActivationFunctionType
ALU = mybir.AluOpType

NCORES = 8
D = 2048
NSEQ = 3
SEQ = 8192
OWN = 3072
HALO = 1024
U = OWN + 4 * HALO
TT = 512
DFF = 5632
NF = DFF // 128
NC16 = D // 128
EPS = 1e-6
NEGM = -30000.0
GRID_W = 64
ROPE_THETA = 500000.0

REGIONS = [(0, U, HALO, U - HALO), (HALO, U - HALO, 2 * HALO, U - 2 * HALO)]

KBLKS = [("ka", i) for i in range(2)] + [("kb", i) for i in range(6)] + [("kc", i) for i in range(4)]
QBLKS = [("qa", i) for i in range(6)] + [("qb", i) for i in range(6)] + [("qc", i) for i in range(4)]
QKBLKS = KBLKS + QBLKS

SAME_ENGINE_WAITS = False


def _col_index():
    idx = {}
    n = 0
    for qb in range(HALO // 128, (U - HALO) // 128):
        for j in range(3):
            idx[("A", qb, j)] = n; n += 1
    for qb in range(HALO // 128, (U - HALO) // 128):
        for j in range(2):
            idx[("B0", qb, j)] = n; n += 1
    for m in range(1, 6):
        for qh in range(2):
            for j in range(2):
                idx[("B1", m, qh, j)] = n; n += 1
    for m in range(1, 6):
        for j in range(2):
            idx[("B2", m, j)] = n; n += 1
    for qb in range(HALO // 128, (U - HALO) // 128):
        for dt in range(7):
            for qh in range(2):
                idx[("C", qb, dt, qh)] = n; n += 1
    return idx, n


COLIDX, NCOLS = _col_index()


def _seqid(g):
    g = np.asarray(g)
    return np.where((g >= 0) & (g < NSEQ * SEQ), g // SEQ, -1)


def _build_cols(core):
    base = core * OWN - 2 * HALO
    cols = np.zeros((128, NCOLS), np.float32)
    p = np.arange(128)

    def setcol(key, ktok_u, qtok_u, extra_valid=None):
        gq = base + qtok_u
        sq = int(_seqid(gq))
        if sq < 0:
            return
        gk = base + ktok_u
        valid = (_seqid(gk) == sq)
        if extra_valid is not None:
            valid = valid & extra_valid
        cols[:len(valid), COLIDX[key]] = np.where(valid, 0.0, NEGM)

    for qb in range(HALO // 128, (U - HALO) // 128):
        u0 = qb * 128
        for j in range(3):
            setcol(("A", qb, j), u0 + 128 * (j - 1) + p, u0)
        for j in range(2):
            setcol(("B0", qb, j), u0 - 64 + 128 * j + p, u0)
        gq0 = base + u0
        for dt in range(7):
            for qh in range(2):
                rq = (gq0 % SEQ) // GRID_W + qh
                ks = min(max(rq - 4, 0), SEQ // GRID_W - 8)
                ktok = u0 + 128 * (dt - 3) + p
                gk = base + ktok
                kr = (gk % SEQ) // GRID_W
                ev = (kr >= ks) & (kr < ks + 8)
                setcol(("C", qb, dt, qh), ktok, u0, ev)
    for m in range(1, 6):
        for qh in range(2):
            J0 = 256 * m + 128 * qh
            for j in range(2):
                setcol(("B1", m, qh, j), 4 * (J0 - 64 + 128 * j + p), 4 * J0)
        J0 = 64 * m
        setcol(("B2", m, 0), 16 * (J0 - 64 + p), 16 * J0)
        setcol(("B2", m, 1), 16 * (J0 + 64 + p[:64]), 16 * J0)
    return cols


def _ctab_index():
    k = np.arange(128)[:, None]
    q = np.arange(128)[None, :]
    kro, kc = k // 64, k % 64
    qro, qc = q // 64, q % 64
    cstart = np.clip(qc - 8, 0, GRID_W - 16)
    cvalid = (kc >= cstart) & (kc < cstart + 16)
    relc = np.clip(kc - qc + 15, 0, 30)
    out = np.zeros((7, 128, 4, 128), np.int64)
    for dt in range(7):
        dr = 2 * (dt - 3) + kro - qro
        relr = np.clip(dr + 7, 0, 14)
        for h in range(4):
            ii = h * 15 * 31 + relr * 31 + relc
            out[dt, :, h, :] = np.where(cvalid, ii, 4 * 15 * 31)
    return out


class _Op:
    __slots__ = ("eng", "fn", "deps", "dma", "dkey", "sem", "val", "hasdep", "i", "persist")


class Prog:
    ENG = ("pe", "act", "dve", "pool", "sp")

    def __init__(self):
        self.ops = []
        self.lastw = {}
        self.readers = {}
        self.lastop = {}
        self.lastdma = {}

    def add(self, eng, fn, reads=(), writes=(), dkey=None, persist=False):
        op = _Op()
        op.persist = persist
        op.eng, op.fn, op.dma, op.dkey = eng, fn, dkey is not None, dkey
        op.sem = None; op.val = 0; op.hasdep = False; op.i = len(self.ops)
        deps = set()
        for r in reads:
            w = self.lastw.get(r)
            if w is not None:
                deps.add(w)
        for w_ in writes:
            lw = self.lastw.get(w_)
            if lw is not None:
                deps.add(lw)
            for rd in self.readers.get(w_, ()):
                deps.add(rd)
        for r in reads:
            self.readers.setdefault(r, []).append(op)
        for w_ in writes:
            self.lastw[w_] = op
            self.readers[w_] = []
        op.deps = deps
        for d in deps:
            d.hasdep = True
        self.ops.append(op)
        if op.dma:
            self.lastdma[dkey] = op
        else:
            self.lastop[eng] = op
        return op

    def barrier(self):
        pend = set(o for o in (set(self.lastop.values()) | set(self.lastdma.values())) if not o.persist)
        keepw = {k: v for k, v in self.lastw.items() if v.persist}
        keepd = {k: v for k, v in self.lastdma.items() if v.persist}
        for e in self.ENG:
            op = _Op()
            op.persist = False
            op.eng, op.fn, op.dma, op.dkey = e, None, False, None
            op.sem = None; op.val = 0; op.hasdep = False; op.i = len(self.ops)
            op.deps = set(pend)
            for d in pend:
                d.hasdep = True
            self.ops.append(op)
        self.lastw.clear(); self.readers.clear(); self.lastop.clear(); self.lastdma.clear()
        self.lastw.update(keepw); self.lastdma.update(keepd)

    def emit(self, nc, block):
        engs = {"pe": "tensor", "act": "scalar", "dve": "vector", "pool": "gpsimd", "sp": "sync"}
        esem = {e: nc.alloc_semaphore(f"e_{e}") for e in self.ENG}
        ecnt = {e: 0 for e in self.ENG}
        dsem = {}
        dcnt = {}
        for op in self.ops:
            if op.fn is None:
                continue
            if op.dma:
                if op.dkey not in dsem:
                    dsem[op.dkey] = nc.alloc_semaphore("d_" + str(len(dsem)))
                    dcnt[op.dkey] = 0
                dcnt[op.dkey] += 16
                op.sem, op.val = dsem[op.dkey], dcnt[op.dkey]
            elif op.hasdep:
                ecnt[op.eng] += 1
                op.sem, op.val = esem[op.eng], ecnt[op.eng]
        self.nsem = len(dsem) + 5
        per = {e: [o for o in self.ops if o.eng == e] for e in self.ENG}

        def run(e):
            def body(eng):
                waited = {}
                for op in per[e]:
                    for d in sorted(op.deps, key=lambda o: o.i):
                        if d.sem is None:
                            continue
                        if (not d.dma) and d.eng == e and (e == "pe" or not SAME_ENGINE_WAITS):
                            continue
                        k = id(d.sem)
                        if waited.get(k, 0) < d.val:
                            eng.wait_ge(d.sem, d.val)
                            waited[k] = d.val
                    if op.fn is None:
                        continue
                    ins = op.fn(eng)
                    if op.dma:
                        ins.then_inc(op.sem, 16)
                    elif op.hasdep:
                        ins.then_inc(op.sem, 1)
            return body

        for e in self.ENG:
            getattr(block, engs[e])(run(e))


class Arena:
    def __init__(self, t, n):
        self.t, self.n, self.o = t, n, 0

    def reset(self):
        self.o = 0

    def get(self, n):
        assert self.o + n <= self.n, (self.o, n, self.n)
        v = self.t[:, self.o:self.o + n]
        self.o += n
        return v


def build_program(debug=False, stop=None):
    nc = bass.Bass("TRN2", target_bir_lowering=False)
    P = Prog()

    def dram(name, shape, dt=F32, kind="ExternalInput"):
        return nc.dram_tensor(name, list(shape), dt, kind=kind).ap()

    xT = dram("xT", [D, U])
    WQK = dram("WQK", [2, 28, 128, 2048])
    WV = dram("WV", [2, 3, 128, 8192])
    WOUT = dram("WOUT", [2, 16, 128, 2048])
    WG = dram("WG", [2, NF, 128, 2048])
    WU = dram("WU", [2, NF, 128, 2048])
    WD = dram("WD", [2, 16, 128, DFF])
    GN = dram("GN", [128, 2 * 2 * 16])
    GQK = dram("GQK", [128, 2 * 6])
    SINK = dram("SINK", [128, 2 * 6])
    COSW = dram("COSW", [128, U])
    SINW = dram("SINW", [32, U])
    COLS = dram("COLS", [128, NCOLS])
    TABC = dram("TABC", [2, 128, 7 * 512])
    CONST = dram("CONST", [128, 128 * 3 + 128])
    yT = dram("yT", [D, OWN], kind="ExternalOutput")
    ik = "ExternalOutput" if debug else "Internal"
    WQKb = dram("WQKb", [2, 28, 128, 2048], BF16, "Internal")
    WVb = dram("WVb", [2, 3, 128, 8192], BF16, "Internal")
    WOUTb = dram("WOUTb", [2, 16, 128, 2048], BF16, "Internal")
    WGb = dram("WGb", [2, NF, 128, 2048], BF16, "Internal")
    WUb = dram("WUb", [2, NF, 128, 2048], BF16, "Internal")
    WDb = dram("WDb", [2, 16, 128, DFF], BF16, "Internal")
    S = {
        "qa": dram("s_qa", [6, 128, U], BF16, ik), "ka": dram("s_ka", [2, 128, U], BF16, ik),
        "qc": dram("s_qc", [4, 128, U], BF16, ik), "kc": dram("s_kc", [4, 128, U], BF16, ik),
        "qb0": dram("s_qb0", [2, 128, U], BF16, ik), "kb0": dram("s_kb0", [2, 128, U], BF16, ik),
        "qb1": dram("s_qb1", [2, 4, 128, U // 4], BF16, ik), "kb1": dram("s_kb1", [2, 4, 128, U // 4], BF16, ik),
        "qb2": dram("s_qb2", [2, 16, 128, U // 16], BF16, ik), "kb2": dram("s_kb2", [2, 16, 128, U // 16], BF16, ik),
    }
    Vs = dram("s_v", [U, 1536], BF16, ik)
    MIX = dram("s_mix", [D, U], BF16, ik)
    X1 = dram("s_x1", [D, U], F32, ik)

    import contextlib
    es = contextlib.ExitStack()
    with es:
        NB16 = 63000
        NF32 = 15000
        abf_t = es.enter_context(nc.sbuf_tensor("abf", [128, NB16], BF16))
        af_t = es.enter_context(nc.sbuf_tensor("af32", [128, NF32], F32))
        cbf_t = es.enter_context(nc.sbuf_tensor("cbf", [128, 128 * 2 + 512 * 2 + 7 * 512 + 128], BF16))
        cf_t = es.enter_context(nc.sbuf_tensor("cf", [128, NCOLS + 64 + 12 + 12 + 12 + 768 + 512], F32))
        ps = [es.enter_context(nc.psum_tensor(f"ps{i}", [128, 512], F32)) for i in range(8)]
        AB = Arena(abf_t, NB16)
        AFP = Arena(af_t, NF32)
        CB = Arena(cbf_t, 128 * 2 + 512 * 2 + 7 * 512 + 128)
        CF = Arena(cf_t, NCOLS + 64 + 12 + 12 + 12 + 768 + 512)

        ident = CB.get(128)
        ones = CB.get(128)
        triGE = CB.get(512)
        triLE = CB.get(512)
        tabc = CB.get(7 * 512)
        rotT = CB.get(128)
        cols = CF.get(NCOLS)
        gn = CF.get(64)
        gqk = CF.get(12)
        gqs = CF.get(12)
        esink = CF.get(12)
        esinkb = CF.get(768)

        AB.reset(); AFP.reset()
        cst32 = CF.get(512)
        P.add("sp", lambda e: e.dma_start(out=cst32, in_=CONST[:, :]), writes=["cst32"], dkey="cst32")
        P.add("sp", lambda e: e.dma_start(out=cols, in_=COLS[:, :]), writes=["cols"], dkey="cols")
        P.add("sp", lambda e: e.dma_start(out=gn, in_=GN[:, :]), writes=["gn"], dkey="gn")
        P.add("sp", lambda e: e.dma_start(out=gqk, in_=GQK[:, :]), writes=["gqk"], dkey="gqk")
        P.add("sp", lambda e: e.dma_start(out=esink, in_=SINK[:, :]), writes=["esink"], dkey="esink")
        P.add("dve", lambda e: e.tensor_copy(out=ident, in_=cst32[:, 0:128]), reads=["cst32"], writes=["ident"])
        P.add("dve", lambda e: e.memset(ones, 1.0), writes=["ones"])
        for r in range(4):
            P.add("dve", lambda e, r=r: e.tensor_copy(out=triGE[:, r * 128:(r + 1) * 128], in_=cst32[:, 128:256]),
                  reads=["cst32"], writes=["triGE"])
            P.add("dve", lambda e, r=r: e.tensor_copy(out=triLE[:, r * 128:(r + 1) * 128], in_=cst32[:, 256:384]),
                  reads=["cst32"], writes=["triLE"])
        P.add("act", lambda e: e.activation(out=esink, in_=esink, func=AF.Exp), reads=["esink"], writes=["esink"])
        P.add("dve", lambda e: e.tensor_copy(out=gqs, in_=gqk), reads=["gqk"], writes=["gqs"])
        for l in range(2):
            for mx in range(3):
                cc = l * 6 + mx * 2
                P.add("dve", lambda e, cc=cc: e.tensor_scalar(out=gqs[:, cc:cc + 1], in0=gqk[:, cc:cc + 1],
                                                             scalar1=float(128 ** -0.5), scalar2=None, op0=ALU.mult),
                      reads=["gqk", "gqs"], writes=["gqs"])

        convq = []

        def conv(src, dst, nblk, key, step=8):
            for b0 in range(0, nblk, step):
                b1 = min(nblk, b0 + step)
                convq.append(lambda b0=b0, b1=b1, src=src, dst=dst, key=key: P.add(
                    "pool", lambda e: e.dma_start(out=dst[b0:b1], in_=src[b0:b1]), writes=[key], dkey=key, persist=True))
        for l in range(2):
            conv(WQK[l], WQKb[l], 28, f"D_wqk{l}", 2 if l == 0 else 1)
            conv(WV[l], WVb[l], 3, f"D_wv{l}", 1)
            conv(WOUT[l], WOUTb[l], 16, f"D_wout{l}", 1)
            conv(WG[l], WGb[l], NF, f"D_wg{l}", 1)
            conv(WU[l], WUb[l], NF, f"D_wu{l}", 1)
            conv(WD[l], WDb[l], 16, f"D_wd{l}", 1)
        NJ1 = 28 + 3 + 16 + NF + NF + 16

        def conv_some(n):
            for _ in range(n):
                if convq:
                    convq.pop(0)()
        conv_some(17)
        P.barrier()

        xin = [xT, X1]
        xout = [X1, None]

        def emit_layer(l):
            kv_lo, kv_hi, q_lo, q_hi = REGIONS[l]
            AB.reset(); AFP.reset()
            t32 = AFP.get(7 * 512)
            P.add("sp", lambda e, l=l: e.dma_start(out=t32, in_=TABC[l]), writes=["t32"], dkey="t32")
            P.add("dve", lambda e: e.tensor_copy(out=tabc, in_=t32), reads=["t32"], writes=["tabc"])
            for h in range(6):
                P.add("dve", lambda e, h=h, l=l: e.tensor_scalar(
                    out=esinkb[:, h * 128:(h + 1) * 128], in0=cst32[:, 0:128], scalar1=0.0,
                    scalar2=esink[:, l * 6 + h:l * 6 + h + 1], op0=ALU.mult, op1=ALU.add),
                    reads=["cst32", "esink"], writes=["esinkb"])
            rg = [AB.get(128) for _ in range(4)]
            gcols = [l * 6 + 0, l * 6 + 1, l * 6 + 2, l * 6 + 3]
            for j in range(4):
                P.add("dve", lambda e, j=j: e.tensor_scalar(
                    out=rg[j], in0=cst32[:, 384:512], scalar1=gqs[:, gcols[j]:gcols[j] + 1], scalar2=None,
                    op0=ALU.mult), reads=["cst32", "gqs"], writes=[f"rg{j}"])
            P.barrier()
            ab_base, af_base = AB.o, 0
            AFP.o = 0

            xt = AFP.get(16 * TT).rearrange("p (c t) -> p c t", c=16)
            rstd = AFP.get(TT)
            cosb = AFP.get(TT)
            sinb = AFP.get(TT)
            cosg = [AFP.get(TT) for _ in range(4)]
            rsh = [AFP.get(TT) for _ in range(2)]
            t1 = [AFP.get(TT) for _ in range(2)]
            t2 = [AFP.get(TT) for _ in range(2)]
            sq = AB.get(16 * TT).rearrange("p (c t) -> p c t", c=16)
            xn = [AB.get(16 * TT).rearrange("p (c t) -> p c t", c=16) for _ in range(2)]
            wqk = [AB.get(2048).rearrange("p (c m) -> p c m", c=16) for _ in range(4)]
            wv = [AB.get(8192).rearrange("p (c m) -> p c m", c=16) for _ in range(2)]
            sqh = [AB.get(TT) for _ in range(3)]
            qbf = [AB.get(TT) for _ in range(3)]
            outq = [AB.get(TT) for _ in range(8)]
            vst = [AB.get(1536) for _ in range(2)]
            xin_v = xin[l].rearrange("(c p) u -> p c u", p=128)
            gmix = l * 32
            wslot = 0
            vslot = 0
            hcount = 0
            vcount = 0
            pscnt = 0
            tiles = list(range(kv_lo // TT, kv_hi // TT))
            cnt = {"w": 0, "v": 0, "h": 0, "vc": 0, "ps": 0}

            def prepA(ti):
                u0 = tiles[ti] * TT
                xs = ti % 2
                P.add("sp", lambda e: e.dma_start(out=xt, in_=xin_v[:, :, u0:u0 + TT]), writes=["xt"], dkey="xt")
                for c in range(16):
                    P.add("act", lambda e, c=c: e.activation(out=sq[:, c, :], in_=xt[:, c, :], func=AF.Square),
                          reads=["xt"], writes=[f"sq{c}"])
                for c in range(16):
                    P.add("pe", lambda e, c=c: e.matmul(ps[7][:], lhsT=ones, rhs=sq[:, c, :], start=(c == 0), stop=(c == 15)),
                          reads=[f"sq{c}"], writes=["ps7"])
                P.add("act", lambda e: e.activation(out=rstd, in_=ps[7][:], func=AF.Sqrt, bias=float(EPS), scale=1.0 / D),
                      reads=["ps7"], writes=["rstd"])
                P.add("dve", lambda e: e.reciprocal(out=rstd, in_=rstd), reads=["rstd"], writes=["rstd"])
                for c in range(16):
                    P.add("dve", lambda e, c=c: e.scalar_tensor_tensor(
                        out=xn[xs][:, c, :], in0=xt[:, c, :], scalar=gn[:, gmix + c:gmix + c + 1], in1=rstd,
                        op0=ALU.mult, op1=ALU.mult), reads=["xt", "rstd"], writes=[f"xn{xs}_{c}"])

            def prepB(ti):
                u0 = tiles[ti] * TT
                P.add("sp", lambda e: e.dma_start(out=cosb, in_=COSW[:, u0:u0 + TT]), writes=["cosb"], dkey="cosb")
                P.add("sp", lambda e: e.dma_start(out=sinb[0:32, :], in_=SINW[:, u0:u0 + TT]), writes=["sinb"], dkey="sinb")
                for j in range(4):
                    P.add("pool", lambda e, j=j: e.tensor_scalar(out=cosg[j], in0=cosb, scalar1=gqs[:, gcols[j]:gcols[j] + 1],
                                                                 scalar2=None, op0=ALU.mult),
                          reads=["cosb"], writes=[f"cosg{j}"])

            def v_section(ti, vbs=(0, 1, 2)):
                u0 = tiles[ti] * TT
                xs = ti % 2
                xnr = [f"xn{xs}_{c}" for c in range(16)]
                for vb in vbs:
                    vs_ = cnt["v"] % 2; cnt["v"] += 1
                    P.add("sp", lambda e, vs_=vs_, vb=vb: e.dma_start(out=wv[vs_], in_=WVb[l, vb].rearrange("p (c m) -> p c m", c=16)),
                          reads=[f"D_wv{l}"], writes=[f"wv{vs_}"], dkey=f"wv{vs_}")
                    for s4 in range(4):
                        pb = cnt["ps"] % 4; cnt["ps"] += 1
                        for c in range(16):
                            P.add("pe", lambda e, c=c, vs_=vs_, pb=pb, s4=s4: e.matmul(
                                ps[pb][:], lhsT=xn[xs][:, c, s4 * 128:(s4 + 1) * 128], rhs=wv[vs_][:, c, :],
                                start=(c == 0), stop=(c == 15)),
                                reads=[f"wv{vs_}", xnr[c]], writes=[f"ps{pb}"])
                        vo = s4 % 2
                        ce = "act"
                        cnt["vc"] += 1
                        if ce == "act":
                            P.add("act", lambda e, pb=pb, vo=vo, vb=vb: e.activation(out=vst[vo][:, vb * 512:(vb + 1) * 512], in_=ps[pb][:], func=AF.Copy),
                                  reads=[f"ps{pb}"], writes=[f"vst{vo}_{vb}"])
                        else:
                            P.add("dve", lambda e, pb=pb, vo=vo, vb=vb: e.tensor_copy(out=vst[vo][:, vb * 512:(vb + 1) * 512], in_=ps[pb][:]),
                                  reads=[f"ps{pb}"], writes=[f"vst{vo}_{vb}"])
                        P.add("pool", lambda e, vo=vo, vb=vb, s4=s4: e.dma_start(
                            out=Vs[u0 + s4 * 128:u0 + (s4 + 1) * 128, vb * 512:(vb + 1) * 512], in_=vst[vo][:, vb * 512:(vb + 1) * 512]),
                            reads=[f"vst{vo}_{vb}"], dkey=f"vst{vo}_{vb}")

            def head(ti, nm, hi):
                u0 = tiles[ti] * TT
                xs = ti % 2
                xnr = [f"xn{xs}_{c}" for c in range(16)]
                gb = QKBLKS.index((nm, hi))
                ws = cnt["w"] % 4; cnt["w"] += 1
                pb = cnt["ps"] % 4; cnt["ps"] += 1
                hs = cnt["h"] % 2
                h3 = cnt["h"] % 3
                os_ = cnt["h"] % 8
                cnt["h"] += 1
                isq = nm[0] == "q"
                mixer = {"a": 0, "b": 1, "c": 2}[nm[1]]
                gcol = l * 6 + mixer * 2 + (0 if isq else 1)
                sb_ = 4 + hs
                rb = 6 + hs
                if nm[1] == "b" and hi >= 2:
                    dil = 4 if hi < 4 else 16
                    ov = outq[os_].rearrange("p (r j) -> p j r", r=dil)
                else:
                    ov = outq[os_]

                def stage1():
                    P.add("sp", lambda e: e.dma_start(out=wqk[ws], in_=WQKb[l, gb].rearrange("p (c m) -> p c m", c=16)),
                          reads=[f"D_wqk{l}"], writes=[f"wqk{ws}"], dkey=f"wqk{ws}")
                    for c in range(16):
                        P.add("pe", lambda e, c=c: e.matmul(
                            ps[pb][:], lhsT=wqk[ws][:, c, :], rhs=xn[xs][:, c, :], start=(c == 0), stop=(c == 15)),
                            reads=[f"wqk{ws}", xnr[c]], writes=[f"ps{pb}"])
                    P.add("act", lambda e: e.activation(out=sqh[h3], in_=ps[pb][:], func=AF.Square),
                          reads=[f"ps{pb}"], writes=[f"sqh{h3}"])
                    if mixer != 2:
                        P.add("act", lambda e: e.activation(out=qbf[h3], in_=ps[pb][:], func=AF.Copy),
                              reads=[f"ps{pb}"], writes=[f"qbf{h3}"])

                def stage2():
                    P.add("pe", lambda e: e.matmul(ps[sb_][:], lhsT=ones, rhs=sqh[h3], start=True, stop=True),
                          reads=[f"sqh{h3}"], writes=[f"ps{sb_}"])
                    P.add("act", lambda e: e.activation(out=rsh[hs], in_=ps[sb_][:], func=AF.Identity, bias=float(EPS), scale=1.0 / 128),
                          reads=[f"ps{sb_}"], writes=[f"rsh{hs}"])
                    P.add("act", lambda e: e.activation(out=rsh[hs], in_=rsh[hs], func=AF.Ln), reads=[f"rsh{hs}"], writes=[f"rsh{hs}"])
                    P.add("act", lambda e: e.activation(out=rsh[hs], in_=rsh[hs], func=AF.Exp, scale=-0.5), reads=[f"rsh{hs}"], writes=[f"rsh{hs}"])
                    if mixer == 2:
                        P.add("dve", lambda e: e.scalar_tensor_tensor(
                            out=ov, in0=ps[pb][:], scalar=gqs[:, gcol:gcol + 1], in1=rsh[hs], op0=ALU.mult, op1=ALU.mult),
                            reads=[f"ps{pb}", f"rsh{hs}"], writes=[f"outq{os_}"])
                    else:
                        j = mixer * 2 + (0 if isq else 1)
                        P.add("pe", lambda e: e.matmul(ps[rb][0:32, :], lhsT=rg[j][:, 0:32], rhs=qbf[h3], start=True, stop=True),
                              reads=[f"qbf{h3}", f"rg{j}"], writes=[f"ps{rb}"])
                        P.add("dve", lambda e: e.tensor_tensor(out=t1[hs], in0=ps[pb][:], in1=cosg[j], op=ALU.mult),
                              reads=[f"ps{pb}", f"cosg{j}"], writes=[f"t1{hs}"])
                        P.add("dve", lambda e: e.tensor_tensor(out=t2[hs][0:32, :], in0=ps[rb][0:32, :], in1=sinb[0:32, :], op=ALU.mult),
                              reads=[f"ps{rb}", "sinb"], writes=[f"t2{hs}"])
                        P.add("pool", lambda e: e.tensor_tensor(out=t1[hs][0:32, :], in0=t1[hs][0:32, :], in1=t2[hs][0:32, :], op=ALU.add),
                              reads=[f"t1{hs}", f"t2{hs}"], writes=[f"t1{hs}"])
                        P.add("pool", lambda e: e.tensor_tensor(out=ov, in0=t1[hs], in1=rsh[hs], op=ALU.mult),
                              reads=[f"t1{hs}", f"rsh{hs}"], writes=[f"outq{os_}"])
                    if nm[1] == "b":
                        g = hi // 2
                        hh = hi % 2
                        if g == 0:
                            dst = S[nm + "0"][hh][:, u0:u0 + TT]
                            src = outq[os_]
                        else:
                            dil_ = 4 if g == 1 else 16
                            n = TT // dil_
                            dst = S[nm + str(g)][hh][:, :, (u0 // dil_):(u0 // dil_) + n].rearrange("r p j -> p r j")
                            src = outq[os_].rearrange("p (r j) -> p r j", r=dil_)
                    else:
                        dst = S[nm][hi][:, u0:u0 + TT]
                        src = outq[os_]
                    P.add("pool", lambda e: e.dma_start(out=dst, in_=src), reads=[f"outq{os_}"], dkey=f"outq{os_}")
                return stage1, stage2

            prepA(0)
            prepB(0)
            for ti, t in enumerate(tiles):
                u0 = t * TT
                need_q = (u0 >= q_lo) and (u0 < q_hi)
                far = (ti == 0) or (ti == len(tiles) - 1)
                v_section(ti, (1,) if far else (0, 1, 2))
                if l == 0:
                    conv_some(1)
                if ti + 1 < len(tiles):
                    prepA(ti + 1)
                blks = QKBLKS if need_q else KBLKS
                if far:
                    blks = [("kb", 4), ("kb", 5)]
                pend = []
                for hidx, (nm, hi) in enumerate(blks):
                    if l == 0 and hidx % 2 == 1:
                        conv_some(1)
                    s1, s2 = head(ti, nm, hi)
                    s1()
                    pend.append(s2)
                    if len(pend) > 2:
                        pend.pop(0)()
                while pend:
                    pend.pop(0)()
                if ti + 1 < len(tiles):
                    prepB(ti + 1)
            P.barrier()
            if stop == ("P", l):
                return True

            AB.o, AFP.o = ab_base, af_base
            qa_t = AB.get(6 * 512).rearrange("p (h t) -> p h t", h=6)
            ka_t = AB.get(2 * 768).rearrange("p (h t) -> p h t", h=2)
            va_t = AB.get(6 * 256).rearrange("p (j h d) -> p j h d", j=6, h=2)
            qb0_t = AB.get(2 * 1024).rearrange("p (h t) -> p h t", h=2)
            kb0_t = AB.get(2 * 1152).rearrange("p (h t) -> p h t", h=2)
            vb0_t = AB.get(9 * 256).rearrange("p (j h d) -> p j h d", j=9, h=2)
            qb1_t = AB.get(2 * 4 * 256).rearrange("p (h r t) -> p h r t", h=2, r=4)
            kb1_t = AB.get(2 * 4 * 384).rearrange("p (h r t) -> p h r t", h=2, r=4)
            vb1_t = AB.get(4 * 3 * 256).rearrange("p (r j h d) -> p r j h d", r=4, j=3, h=2)
            qb2_t = AB.get(2 * 16 * 64).rearrange("p (h r t) -> p h r t", h=2, r=16)
            kb2_t = AB.get(2 * 16 * 192).rearrange("p (h r t) -> p h r t", h=2, r=16)
            vb2a_t = AB.get(16 * 256).rearrange("p (r h d) -> p r h d", r=16, h=2)
            vb2b_t = AB.get(16 * 256).rearrange("p (r h d) -> p r h d", r=16, h=2)
            qc_t = AB.get(4 * 512).rearrange("p (h t) -> p h t", h=4)
            kc_t = AB.get(4 * 1280).rearrange("p (h t) -> p h t", h=4)
            vc_t = AB.get(10 * 512).rearrange("p (j h d) -> p j h d", j=10, h=4)
            pT = [AB.get(512) for _ in range(3)]
            oA = AB.get(6 * 512).rearrange("p (h t) -> p h t", h=6)
            oC = AB.get(4 * 512).rearrange("p (h t) -> p h t", h=4)
            oBb = AB.get(6 * 1024).rearrange("p (h t) -> p h t", h=6)
            rden = [AFP.get(512) for _ in range(2)]
            OB = AFP.get(6 * 1024).rearrange("p (g t) -> p g t", g=6)
            DB = AFP.get(6 * 1024).rearrange("p (g t) -> p g t", g=6)
            MIXv = MIX.rearrange("(h p) u -> p h u", p=128)
            scnt = [0]
            acnt = [0]

            def s_bank():
                b = scnt[0] % 4; scnt[0] += 1
                return b

            def exp_to(pt_i, b, ncol, colkey, kpart=128, view=None):
                ci = COLIDX[colkey]
                if view is None:
                    o_ap, i_ap = pT[pt_i][0:kpart, 0:ncol], ps[b][0:kpart, 0:ncol]
                else:
                    o_ap, i_ap = view(pT[pt_i][0:kpart, :]), view(ps[b][0:kpart, :])
                P.add("act", lambda e: e.activation(out=o_ap, in_=i_ap, func=AF.Exp, bias=cols[0:kpart, ci:ci + 1], scale=1.0),
                      reads=[f"ps{b}"], writes=[f"pT{pt_i}"])

            pcnt = [0]

            def p_slot():
                s = pcnt[0] % 3; pcnt[0] += 1
                return s

            LOOK = 2
            fifo = []

            def step(s_fn, pv_fn):
                s_fn()
                fifo.append(pv_fn)
                while len(fifo) > LOOK:
                    fifo.pop(0)()

            def flush():
                while fifo:
                    fifo.pop(0)()

            def tri_of(j):
                return triGE if j == 0 else triLE

            def loadB(m):
                ub = 1024 * m
                P.add("sp", lambda e, ub=ub: e.dma_start(out=qb0_t, in_=S["qb0"][:, :, ub:ub + 1024].rearrange("h p t -> p h t")),
                      writes=["qb0_t"], dkey="qb0_t")
                P.add("sp", lambda e, ub=ub: e.dma_start(out=kb0_t, in_=S["kb0"][:, :, ub - 64:ub + 1088].rearrange("h p t -> p h t")),
                      writes=["kb0_t"], dkey="kb0_t")
                P.add("sp", lambda e, ub=ub: e.dma_start(
                    out=vb0_t, in_=Vs[ub - 64:ub + 1088, 256:512].rearrange("(j p) (h d) -> p j h d", p=128, h=2)),
                    writes=["vb0_t"], dkey="vb0_t")
                J1 = 256 * m
                for hh in range(2):
                    P.add("sp", lambda e, hh=hh, J1=J1: e.dma_start(out=qb1_t[:, hh], in_=S["qb1"][hh][:, :, J1:J1 + 256].rearrange("r p t -> p r t")),
                          writes=[f"qb1_t{hh}"], dkey=f"qb1_t{hh}")
                    P.add("sp", lambda e, hh=hh, J1=J1: e.dma_start(out=kb1_t[:, hh], in_=S["kb1"][hh][:, :, J1 - 64:J1 + 320].rearrange("r p t -> p r t")),
                          writes=[f"kb1_t{hh}"], dkey=f"kb1_t{hh}")
                for r in range(4):
                    t0 = 4 * (J1 - 64) + r
                    P.add("sp", lambda e, r=r, t0=t0: e.dma_start(
                        out=vb1_t[:, r], in_=Vs[t0:t0 + 4 * 384 - 3:4, 512:768].rearrange("(j p) (h d) -> p j h d", p=128, h=2)),
                        writes=[f"vb1_t{r}"], dkey=f"vb1_t{r}")
                J2 = 64 * m
                for hh in range(2):
                    P.add("sp", lambda e, hh=hh, J2=J2: e.dma_start(out=qb2_t[:, hh], in_=S["qb2"][hh][:, :, J2:J2 + 64].rearrange("r p t -> p r t")),
                          writes=[f"qb2_t{hh}"], dkey=f"qb2_t{hh}")
                    P.add("sp", lambda e, hh=hh, J2=J2: e.dma_start(out=kb2_t[:, hh], in_=S["kb2"][hh][:, :, J2 - 64:J2 + 128].rearrange("r p t -> p r t")),
                          writes=[f"kb2_t{hh}"], dkey=f"kb2_t{hh}")
                t0 = 16 * (J2 - 64)
                P.add("sp", lambda e, t0=t0: e.dma_start(
                    out=vb2a_t, in_=Vs[t0:t0 + 2048, 768:1024].rearrange("(p r) (h d) -> p r h d", r=16, h=2)),
                    writes=["vb2a_t"], dkey="vb2a_t")
                t1_ = 16 * (J2 + 64)
                P.add("sp", lambda e, t1_=t1_: e.dma_start(
                    out=vb2b_t[0:64], in_=Vs[t1_:t1_ + 1024, 768:1024].rearrange("(p r) (h d) -> p r h d", r=16, h=2)),
                    writes=["vb2b_t"], dkey="vb2b_t")


            def loadA(m, half):
                ubh = 1024 * m + 512 * half
                P.add("sp", lambda e, ubh=ubh: e.dma_start(out=qa_t, in_=S["qa"][:, :, ubh:ubh + 512].rearrange("h p t -> p h t")),
                      writes=["qa_t"], dkey="qa_t")
                P.add("sp", lambda e, ubh=ubh: e.dma_start(out=ka_t, in_=S["ka"][:, :, ubh - 128:ubh + 640].rearrange("h p t -> p h t")),
                      writes=["ka_t"], dkey="ka_t")
                P.add("sp", lambda e, ubh=ubh: e.dma_start(
                    out=va_t, in_=Vs[ubh - 128:ubh + 640, 0:256].rearrange("(j p) (h d) -> p j h d", p=128, h=2)),
                    writes=["va_t"], dkey="va_t")

            def computeA(m, half):
                ub = 1024 * m
                ubh = ub + 512 * half
                for i in range(4):
                    qb = (ubh // 128) + i
                    for g in range(2):
                        ob = 4 + (acnt[0] % 2); db = 6 + (acnt[0] % 2); acnt[0] += 1
                        rd = acnt[0] % 2
                        for j in range(3):
                            b = s_bank(); pi = p_slot()

                            def s_fn(b=b, pi=pi, g=g, i=i, j=j, qb=qb):
                                if j != 1:
                                    tri = tri_of(j)
                                    P.add("pe", lambda e: e.matmul(ps[b][:, 0:384], lhsT=ident, rhs=tri[:, 0:384], start=True, stop=False),
                                          reads=["ident", "triGE", "triLE"], writes=[f"ps{b}"])
                                P.add("pe", lambda e: e.matmul(
                                    ps[b][:, 0:384], lhsT=ka_t[:, g, (i + j) * 128:(i + j + 1) * 128],
                                    rhs=qa_t[:, 3 * g:3 * g + 3, i * 128:(i + 1) * 128], start=(j == 1), stop=True),
                                    reads=["ka_t", "qa_t"], writes=[f"ps{b}"])
                                exp_to(pi, b, 384, ("A", qb, j))

                            def pv_fn(pi=pi, g=g, i=i, j=j, ob=ob, db=db, rd=rd, ubh=ubh):
                                P.add("pe", lambda e: e.matmul(
                                    ps[ob][:, 0:384], lhsT=va_t[:, i + j, g, :], rhs=pT[pi][:, 0:384], start=(j == 0), stop=(j == 2)),
                                    reads=["va_t", f"pT{pi}"], writes=[f"ps{ob}"])
                                P.add("pe", lambda e: e.matmul(
                                    ps[db][:, 0:384], lhsT=ones, rhs=pT[pi][:, 0:384], start=(j == 0), stop=(j == 2)),
                                    reads=["ones", f"pT{pi}"], writes=[f"ps{db}"])
                                if j == 2:
                                    P.add("dve", lambda e: e.tensor_tensor(
                                        out=rden[rd][:, 0:384], in0=ps[db][:, 0:384], in1=esinkb[:, 384 * g:384 * g + 384], op=ALU.add),
                                        reads=[f"ps{db}", "esinkb"], writes=[f"rden{rd}"])
                                    P.add("dve", lambda e: e.reciprocal(out=rden[rd][:, 0:384], in_=rden[rd][:, 0:384]),
                                          reads=[f"rden{rd}"], writes=[f"rden{rd}"])
                                    P.add("dve", lambda e: e.tensor_tensor(
                                        out=oA[:, 3 * g:3 * g + 3, i * 128:(i + 1) * 128],
                                        in0=ps[ob][:, 0:384].rearrange("p (h t) -> p h t", h=3),
                                        in1=rden[rd][:, 0:384].rearrange("p (h t) -> p h t", h=3), op=ALU.mult),
                                        reads=[f"ps{ob}", f"rden{rd}"], writes=["oA"])
                                    if i == 3 and g == 1:
                                        P.add("sp", lambda e: e.dma_start(out=MIXv[:, 0:6, ubh:ubh + 512], in_=oA), reads=["oA"], dkey="oA")
                            step(s_fn, pv_fn)

            def computeB(m):
                ub = 1024 * m
                for i in range(8):
                    qb = (ub // 128) + i
                    ob = 4 + (acnt[0] % 2); db = 6 + (acnt[0] % 2); acnt[0] += 1
                    for j in range(2):
                        b = s_bank(); pi = p_slot()

                        def s_fn(b=b, pi=pi, i=i, j=j, qb=qb):
                            tri = tri_of(j)
                            P.add("pe", lambda e: e.matmul(ps[b][:, 0:256], lhsT=ident, rhs=tri[:, 0:256], start=True, stop=False),
                                  reads=["ident", "triGE", "triLE"], writes=[f"ps{b}"])
                            for hh in range(2):
                                P.add("pe", lambda e, hh=hh: e.matmul(
                                    ps[b][:, hh * 128:(hh + 1) * 128], lhsT=kb0_t[:, hh, (i + j) * 128:(i + j + 1) * 128],
                                    rhs=qb0_t[:, hh, i * 128:(i + 1) * 128], start=False, stop=(hh == 1)),
                                    reads=["kb0_t", "qb0_t"], writes=[f"ps{b}"])
                            exp_to(pi, b, 256, ("B0", qb, j))

                        def pv_fn(pi=pi, i=i, j=j, ob=ob, db=db):
                            for hh in range(2):
                                P.add("pe", lambda e, hh=hh: e.matmul(
                                    ps[ob][:, hh * 128:(hh + 1) * 128], lhsT=vb0_t[:, i + j, hh, :], rhs=pT[pi][:, hh * 128:(hh + 1) * 128],
                                    start=(j == 0 and hh == 0), stop=(j == 1 and hh == 1)), reads=["vb0_t", f"pT{pi}"], writes=[f"ps{ob}"])
                            P.add("pe", lambda e: e.matmul(ps[db][:, 0:256], lhsT=ones, rhs=pT[pi][:, 0:256], start=(j == 0), stop=(j == 1)),
                                  reads=["ones", f"pT{pi}"], writes=[f"ps{db}"])
                            if j == 1:
                                P.add("act", lambda e: e.activation(
                                    out=OB[:, 0:2, i * 128:(i + 1) * 128], in_=ps[ob][:, 0:256].rearrange("p (h t) -> p h t", h=2), func=AF.Copy),
                                    reads=[f"ps{ob}"], writes=["OB0"])
                                P.add("dve", lambda e: e.tensor_copy(
                                    out=DB[:, 0:2, i * 128:(i + 1) * 128], in_=ps[db][:, 0:256].rearrange("p (h t) -> p h t", h=2)),
                                    reads=[f"ps{db}"], writes=["DB0"])
                        step(s_fn, pv_fn)

                for qh in range(2):
                    for r in range(4):
                        ob = 4 + (acnt[0] % 2); db = 6 + (acnt[0] % 2); acnt[0] += 1
                        o0 = 512 * qh + r
                        for j in range(2):
                            b = s_bank(); pi = p_slot()
                            k0 = 128 * qh + 128 * j

                            def s_fn(b=b, pi=pi, r=r, j=j, qh=qh, k0=k0):
                                tri = tri_of(j)
                                P.add("pe", lambda e: e.matmul(ps[b][:, 0:256], lhsT=ident, rhs=tri[:, 0:256], start=True, stop=False),
                                      reads=["ident", "triGE", "triLE"], writes=[f"ps{b}"])
                                for hh in range(2):
                                    P.add("pe", lambda e, hh=hh: e.matmul(
                                        ps[b][:, hh * 128:(hh + 1) * 128], lhsT=kb1_t[:, hh, r, k0:k0 + 128],
                                        rhs=qb1_t[:, hh, r, qh * 128:(qh + 1) * 128], start=False, stop=(hh == 1)),
                                        reads=[f"kb1_t{hh}", f"qb1_t{hh}"], writes=[f"ps{b}"])
                                exp_to(pi, b, 256, ("B1", m, qh, j))

                            def pv_fn(pi=pi, r=r, j=j, qh=qh, ob=ob, db=db, o0=o0):
                                for hh in range(2):
                                    P.add("pe", lambda e, hh=hh: e.matmul(
                                        ps[ob][:, hh * 128:(hh + 1) * 128], lhsT=vb1_t[:, r, qh + j, hh, :],
                                        rhs=pT[pi][:, hh * 128:(hh + 1) * 128], start=(j == 0 and hh == 0), stop=(j == 1 and hh == 1)),
                                        reads=[f"vb1_t{r}", f"pT{pi}"], writes=[f"ps{ob}"])
                                P.add("pe", lambda e: e.matmul(ps[db][:, 0:256], lhsT=ones, rhs=pT[pi][:, 0:256], start=(j == 0), stop=(j == 1)),
                                      reads=["ones", f"pT{pi}"], writes=[f"ps{db}"])
                                if j == 1:
                                    P.add("act", lambda e: e.activation(
                                        out=OB[:, 2:4, o0:o0 + 509:4], in_=ps[ob][:, 0:256].rearrange("p (h t) -> p h t", h=2), func=AF.Copy),
                                        reads=[f"ps{ob}"], writes=["OB1"])
                                    P.add("dve", lambda e: e.tensor_copy(
                                        out=DB[:, 2:4, o0:o0 + 509:4], in_=ps[db][:, 0:256].rearrange("p (h t) -> p h t", h=2)),
                                        reads=[f"ps{db}"], writes=["DB1"])
                            step(s_fn, pv_fn)

                for rg4 in range(4):
                    ob = 4 + (acnt[0] % 2); db = 6 + (acnt[0] % 2); acnt[0] += 1
                    for j in range(2):
                        kp = 128 if j == 0 else 64
                        b = s_bank(); pi = p_slot()

                        def s_fn(b=b, pi=pi, j=j, kp=kp, rg4=rg4):
                            tri = tri_of(j)
                            for half in range(2):
                                P.add("pe", lambda e, half=half: e.matmul(
                                    ps[b][0:kp, 256 * half:256 * half + 256].rearrange("p (a t) -> p a t", a=4),
                                    lhsT=ident[0:kp, 0:kp], rhs=tri[0:kp, :].rearrange("p (a t) -> p a t", a=4)[:, :, 0:64],
                                    start=(half == 0), stop=False),
                                    reads=["ident", "triGE", "triLE"], writes=[f"ps{b}"])
                            for rr in range(4):
                                r = rg4 * 4 + rr
                                for hh in range(2):
                                    c0 = (rr * 2 + hh) * 64
                                    P.add("pe", lambda e, hh=hh, r=r, c0=c0, rr=rr: e.matmul(
                                        ps[b][0:kp, c0:c0 + 64], lhsT=kb2_t[:, hh, r, 128 * j:128 * j + kp],
                                        rhs=qb2_t[:, hh, r, :], start=False, stop=(rr == 3 and hh == 1)),
                                        reads=[f"kb2_t{hh}", f"qb2_t{hh}"], writes=[f"ps{b}"])
                            exp_to(pi, b, 512, ("B2", m, j), kpart=kp)

                        def pv_fn(pi=pi, j=j, kp=kp, rg4=rg4, ob=ob, db=db):
                            vt = vb2a_t if j == 0 else vb2b_t
                            for rr in range(4):
                                r = rg4 * 4 + rr
                                for hh in range(2):
                                    c0 = (rr * 2 + hh) * 64
                                    P.add("pe", lambda e, hh=hh, r=r, c0=c0: e.matmul(
                                        ps[ob][:, c0:c0 + 64], lhsT=vt[0:kp, r, hh, :], rhs=pT[pi][0:kp, c0:c0 + 64],
                                        start=(j == 0 and c0 == 0), stop=(j == 1 and c0 == 448)),
                                        reads=["vb2a_t", "vb2b_t", f"pT{pi}"], writes=[f"ps{ob}"])
                            P.add("pe", lambda e: e.matmul(ps[db][:, :], lhsT=ones[0:kp, :], rhs=pT[pi][0:kp, :], start=(j == 0), stop=(j == 1)),
                                  reads=["ones", f"pT{pi}"], writes=[f"ps{db}"])
                            if j == 1:
                                for hh in range(2):
                                    src_o = ps[ob][:, :].rearrange("p (rr h t) -> p rr h t", rr=4, h=2)[:, :, hh, :]
                                    src_d = ps[db][:, :].rearrange("p (rr h t) -> p rr h t", rr=4, h=2)[:, :, hh, :]
                                    dst_o = OB[:, 4 + hh, :].rearrange("p (t r) -> p r t", r=16)[:, rg4 * 4:rg4 * 4 + 4, :]
                                    dst_d = DB[:, 4 + hh, :].rearrange("p (t r) -> p r t", r=16)[:, rg4 * 4:rg4 * 4 + 4, :]
                                    P.add("act", lambda e, src_o=src_o, dst_o=dst_o: e.activation(out=dst_o, in_=src_o, func=AF.Copy),
                                          reads=[f"ps{ob}"], writes=["OB2"])
                                    P.add("dve", lambda e, src_d=src_d, dst_d=dst_d: e.tensor_copy(out=dst_d, in_=src_d),
                                          reads=[f"ps{db}"], writes=["DB2"])
                        step(s_fn, pv_fn)

                def combine(ub=ub):
                    for hh in range(2):
                        P.add("dve", lambda e, hh=hh: e.tensor_tensor(out=DB[:, hh, :], in0=DB[:, hh, :], in1=DB[:, 2 + hh, :], op=ALU.add),
                              reads=["DB0", "DB1"], writes=["DB0"])
                        P.add("dve", lambda e, hh=hh: e.tensor_tensor(out=DB[:, hh, :], in0=DB[:, hh, :], in1=DB[:, 4 + hh, :], op=ALU.add),
                              reads=["DB0", "DB2"], writes=["DB0"])
                        P.add("dve", lambda e, hh=hh: e.reciprocal(out=DB[:, hh, :], in_=DB[:, hh, :]), reads=["DB0"], writes=["DB0"])
                        for g in range(3):
                            eng = "dve"
                            P.add(eng, lambda e, hh=hh, g=g: e.tensor_tensor(out=oBb[:, 2 * g + hh, :], in0=OB[:, 2 * g + hh, :], in1=DB[:, hh, :], op=ALU.mult),
                                  reads=["DB0", f"OB{g}"], writes=["oBb"])
                    P.add("sp", lambda e: e.dma_start(out=MIXv[:, 6:12, ub:ub + 1024], in_=oBb), reads=["oBb"], dkey="oBb")
                step(lambda: None, combine)


            def loadC(m, half):
                ubh = 1024 * m + 512 * half
                P.add("sp", lambda e, ubh=ubh: e.dma_start(out=qc_t, in_=S["qc"][:, :, ubh:ubh + 512].rearrange("h p t -> p h t")),
                      writes=["qc_t"], dkey="qc_t")
                P.add("sp", lambda e, ubh=ubh: e.dma_start(out=kc_t, in_=S["kc"][:, :, ubh - 384:ubh + 896].rearrange("h p t -> p h t")),
                      writes=["kc_t"], dkey="kc_t")
                for jj in range(2):
                    P.add("sp", lambda e, ubh=ubh, jj=jj: e.dma_start(
                        out=vc_t[:, 5 * jj:5 * jj + 5],
                        in_=Vs[ubh - 384 + 640 * jj:ubh - 384 + 640 * (jj + 1), 1024:1536].rearrange("(j p) (h d) -> p j h d", p=128, h=4)),
                        writes=[f"vc_t{jj}"], dkey=f"vc_t{jj}")

            def computeC(m, half):
                ub = 1024 * m
                ubh = ub + 512 * half
                for i in range(4):
                    qb = (ubh // 128) + i
                    ob = 4 + (acnt[0] % 2); db = 6 + (acnt[0] % 2); acnt[0] += 1
                    rd = acnt[0] % 2
                    for dt in range(7):
                        b = s_bank(); pi = p_slot()
                        kt = i + dt

                        def s_fn(b=b, pi=pi, i=i, dt=dt, kt=kt, qb=qb):
                            P.add("pe", lambda e: e.matmul(ps[b][:, :], lhsT=ident, rhs=tabc[:, dt * 512:(dt + 1) * 512], start=True, stop=False),
                                  reads=["ident", "tabc"], writes=[f"ps{b}"])
                            for h in range(4):
                                P.add("pe", lambda e, h=h: e.matmul(
                                    ps[b][:, h * 128:(h + 1) * 128], lhsT=kc_t[:, h, kt * 128:(kt + 1) * 128],
                                    rhs=qc_t[:, h, i * 128:(i + 1) * 128], start=False, stop=(h == 3)),
                                    reads=["kc_t", "qc_t"], writes=[f"ps{b}"])
                            for qh in range(2):
                                exp_to(pi, b, 0, ("C", qb, dt, qh),
                                       view=lambda a, qh=qh: a.rearrange("p (h t) -> p h t", h=4)[:, :, qh * 64:(qh + 1) * 64])

                        def pv_fn(pi=pi, i=i, dt=dt, kt=kt, ob=ob, db=db, rd=rd, ubh=ubh):
                            for h in range(4):
                                P.add("pe", lambda e, h=h: e.matmul(
                                    ps[ob][:, h * 128:(h + 1) * 128], lhsT=vc_t[:, kt, h, :], rhs=pT[pi][:, h * 128:(h + 1) * 128],
                                    start=(dt == 0 and h == 0), stop=(dt == 6 and h == 3)),
                                    reads=["vc_t0", "vc_t1", f"pT{pi}"], writes=[f"ps{ob}"])
                            P.add("pe", lambda e: e.matmul(ps[db][:, :], lhsT=ones, rhs=pT[pi][:, :], start=(dt == 0), stop=(dt == 6)),
                                  reads=["ones", f"pT{pi}"], writes=[f"ps{db}"])
                            if dt == 6:
                                P.add("dve", lambda e: e.reciprocal(out=rden[rd], in_=ps[db][:, :]), reads=[f"ps{db}"], writes=[f"rden{rd}"])
                                P.add("dve", lambda e: e.tensor_tensor(
                                    out=oC[:, :, i * 128:(i + 1) * 128], in0=ps[ob][:, :].rearrange("p (h t) -> p h t", h=4),
                                    in1=rden[rd].rearrange("p (h t) -> p h t", h=4), op=ALU.mult),
                                    reads=[f"ps{ob}", f"rden{rd}"], writes=["oC"])
                                if i == 3:
                                    P.add("sp", lambda e: e.dma_start(out=MIXv[:, 12:16, ubh:ubh + 512], in_=oC), reads=["oC"], dkey="oC")
                        step(s_fn, pv_fn)

            ms = list(range(q_lo // 1024, q_hi // 1024))
            loadB(ms[0]); loadA(ms[0], 0)
            computeA(ms[0], 0)
            for idx, m in enumerate(ms):
                nxt = ms[idx + 1] if idx + 1 < len(ms) else None
                flush(); loadA(m, 1)
                if idx > 0:
                    computeC(ms[idx - 1], 1)
                flush(); loadC(m, 0)
                computeB(m)
                computeA(m, 1)
                flush()
                if nxt is not None:
                    loadB(nxt); loadA(nxt, 0)
                computeC(m, 0)
                flush(); loadC(m, 1)
                if nxt is not None:
                    computeA(nxt, 0)
            computeC(ms[-1], 1)
            flush()
            P.barrier()
            if stop == ("T", l):
                return True

            while len(convq) > (NJ1 if l == 0 else 0):
                conv_some(1)
            AB.o, AFP.o = ab_base, af_base
            x1 = AFP.get(16 * TT).rearrange("p (c t) -> p c t", c=16)
            rstd2 = AFP.get(TT)
            xs_t = [AFP.get(TT) for _ in range(3)]
            ost = [AFP.get(TT) for _ in range(3)]
            sg = [AFP.get(TT) for _ in range(3)]
            mix_t = AB.get(16 * TT).rearrange("p (c t) -> p c t", c=16)
            xn2 = AB.get(16 * TT).rearrange("p (c t) -> p c t", c=16)
            h_t = AB.get(NF * TT).rearrange("p (f t) -> p f t", f=NF)
            sq2 = h_t
            wo_t = [AB.get(2048).rearrange("p (c m) -> p c m", c=16) for _ in range(2)]
            wgu_t = [AB.get(2048).rearrange("p (c m) -> p c m", c=16) for _ in range(4)]
            wd_t = [AB.get(DFF).rearrange("p (f m) -> p f m", f=NF) for _ in range(2)]
            xres = xin[l].rearrange("(c p) u -> p c u", p=128)
            gffn = l * 32 + 16
            woc = 0; wgc = 0; wdc = 0; pc = 0; xc = 0; oc = 0; sc = 0
            for t in range(q_lo // TT, q_hi // TT):
                u0 = t * TT
                P.add("sp", lambda e, u0=u0: e.dma_start(out=mix_t, in_=MIXv[:, :, u0:u0 + TT]), writes=["mix_t"], dkey="mix_t")
                for o in range(16):
                    ws = woc % 2; woc += 1
                    P.add("sp", lambda e, ws=ws, o=o: e.dma_start(out=wo_t[ws], in_=WOUTb[l, o].rearrange("p (c m) -> p c m", c=16)),
                          reads=[f"D_wout{l}"], writes=[f"wo{ws}"], dkey=f"wo{ws}")
                    xi = xc % 3; xc += 1
                    P.add("sp", lambda e, xi=xi, o=o, u0=u0: e.dma_start(out=xs_t[xi], in_=xres[:, o, u0:u0 + TT]),
                          writes=[f"xs{xi}"], dkey=f"xs{xi}")
                    pb = pc % 8; pc += 1
                    for c in range(16):
                        P.add("pe", lambda e, c=c, ws=ws, pb=pb: e.matmul(ps[pb][:], lhsT=wo_t[ws][:, c, :], rhs=mix_t[:, c, :],
                                                                          start=(c == 0), stop=(c == 15)),
                              reads=[f"wo{ws}", "mix_t"], writes=[f"ps{pb}"])
                    P.add("dve", lambda e, pb=pb, xi=xi, o=o: e.tensor_tensor(out=x1[:, o, :], in0=ps[pb][:], in1=xs_t[xi], op=ALU.add),
                          reads=[f"ps{pb}", f"xs{xi}"], writes=[f"x1_{o}"])
                    P.add("act", lambda e, o=o: e.activation(out=sq2[:, o, :], in_=x1[:, o, :], func=AF.Square),
                          reads=[f"x1_{o}"], writes=[f"h{o}"])
                pb = pc % 8; pc += 1
                for o in range(16):
                    P.add("pe", lambda e, o=o, pb=pb: e.matmul(ps[pb][:], lhsT=ones, rhs=sq2[:, o, :], start=(o == 0), stop=(o == 15)),
                          reads=[f"h{o}", "ones"], writes=[f"ps{pb}"])
                P.add("act", lambda e, pb=pb: e.activation(out=rstd2, in_=ps[pb][:], func=AF.Sqrt, bias=float(EPS), scale=1.0 / D),
                      reads=[f"ps{pb}"], writes=["rstd2"])
                P.add("dve", lambda e: e.reciprocal(out=rstd2, in_=rstd2), reads=["rstd2"], writes=["rstd2"])
                for c in range(16):
                    eng = "dve"
                    P.add(eng, lambda e, c=c: e.scalar_tensor_tensor(
                        out=xn2[:, c, :], in0=x1[:, c, :], scalar=gn[:, gffn + c:gffn + c + 1], in1=rstd2,
                        op0=ALU.mult, op1=ALU.mult), reads=[f"x1_{c}", "rstd2"], writes=[f"xn2_{c}"])
                for f in range(NF):
                    if f % 3 == 1:
                        conv_some(1)
                    pbs = []
                    for wi, Wsrc in enumerate((WGb, WUb)):
                        ws = wgc % 4; wgc += 1
                        dk_ = f"D_wg{l}" if wi == 0 else f"D_wu{l}"
                        P.add("sp", lambda e, ws=ws, f=f, Wsrc=Wsrc: e.dma_start(out=wgu_t[ws], in_=Wsrc[l, f].rearrange("p (c m) -> p c m", c=16)),
                              reads=[dk_], writes=[f"wgu{ws}"], dkey=f"wgu{ws}")
                        pb = pc % 8; pc += 1
                        pbs.append(pb)
                        for c in range(16):
                            P.add("pe", lambda e, c=c, ws=ws, pb=pb: e.matmul(ps[pb][:], lhsT=wgu_t[ws][:, c, :], rhs=xn2[:, c, :],
                                                                              start=(c == 0), stop=(c == 15)),
                                  reads=[f"wgu{ws}", f"xn2_{c}"], writes=[f"ps{pb}"])
                    si = sc % 3; sc += 1
                    P.add("act", lambda e, si=si, pb=pbs[0]: e.activation(out=sg[si], in_=ps[pb][:], func=AF.Silu),
                          reads=[f"ps{pbs[0]}"], writes=[f"sg{si}"])
                    P.add("dve", lambda e, si=si, pb=pbs[1], f=f: e.tensor_tensor(out=h_t[:, f, :], in0=ps[pb][:], in1=sg[si], op=ALU.mult),
                          reads=[f"ps{pbs[1]}", f"sg{si}"], writes=[f"h{f}"])
                for o in range(16):
                    ws = wdc % 2; wdc += 1
                    P.add("sp", lambda e, ws=ws, o=o: e.dma_start(out=wd_t[ws], in_=WDb[l, o].rearrange("p (f m) -> p f m", f=NF)),
                          reads=[f"D_wd{l}"], writes=[f"wd{ws}"], dkey=f"wd{ws}")
                    pb = pc % 8; pc += 1
                    for f in range(NF):
                        P.add("pe", lambda e, f=f, ws=ws, pb=pb: e.matmul(ps[pb][:], lhsT=wd_t[ws][:, f, :], rhs=h_t[:, f, :],
                                                                          start=(f == 0), stop=(f == NF - 1)),
                              reads=[f"wd{ws}", f"h{f}"], writes=[f"ps{pb}"])
                    oi = oc % 3; oc += 1
                    P.add("dve", lambda e, pb=pb, oi=oi, o=o: e.tensor_tensor(out=ost[oi], in0=ps[pb][:], in1=x1[:, o, :], op=ALU.add),
                          reads=[f"ps{pb}", f"x1_{o}"], writes=[f"ost{oi}"])
                    if l == 0:
                        dst = X1.rearrange("(c p) u -> p c u", p=128)[:, o, u0:u0 + TT]
                    else:
                        dst = yT.rearrange("(c p) u -> p c u", p=128)[:, o, u0 - 2 * HALO:u0 - 2 * HALO + TT]
                    P.add("pool", lambda e, oi=oi, dst=dst: e.dma_start(out=dst, in_=ost[oi]), reads=[f"ost{oi}"], dkey=f"ost{oi}")
            P.barrier()
            return stop == ("F", l)

        for l_ in range(2):
            if emit_layer(l_):
                break

        with nc.Block() as block:
            P.emit(nc, block)
    return nc


def _blk(w, cols):
    K = w.shape[0]
    return np.ascontiguousarray(w.reshape(K // 128, 128, -1).transpose(1, 0, 2).reshape(128, -1))


def prepare_inputs(x_prompt, x_sample, norm_mix, w_in, qk_norm, sink_a, rpb_c, w_out, norm_ffn, w_gate, w_up, w_down):
    f32 = np.float32
    xs = np.concatenate([np.asarray(x_prompt, f32).reshape(-1, D), np.asarray(x_sample, f32).reshape(-1, D)], axis=0)
    NTOK = xs.shape[0]
    xpad = np.zeros((NTOK + 4 * HALO, D), f32)
    xpad[2 * HALO:2 * HALO + NTOK] = xs
    w_in = np.asarray(w_in, f32); w_out = np.asarray(w_out, f32)
    w_gate = np.asarray(w_gate, f32); w_up = np.asarray(w_up, f32); w_down = np.asarray(w_down, f32)
    cg = {"qa": 0, "ka": 768, "va": 1024, "qb": 1280, "kb": 2048, "vb": 2816, "qc": 3584, "kc": 4096, "vc": 4608}
    WQK = np.zeros((2, 28, 128, 2048), f32)
    WV = np.zeros((2, 3, 128, 8192), f32)
    WOUT = np.zeros((2, 16, 128, 2048), f32)
    WG = np.zeros((2, NF, 128, 2048), f32)
    WU = np.zeros((2, NF, 128, 2048), f32)
    WD = np.zeros((2, 16, 128, DFF), f32)
    for l in range(2):
        for b, (nm, hi) in enumerate(QKBLKS):
            c0 = cg[nm] + hi * 128
            WQK[l, b] = _blk(w_in[l][:, c0:c0 + 128], 128)
        wvv = np.concatenate([w_in[l][:, 1024:1280], w_in[l][:, 2816:3584], w_in[l][:, 4608:5120]], axis=1)
        for b in range(3):
            WV[l, b] = _blk(wvv[:, b * 512:(b + 1) * 512], 512)
        for o in range(16):
            WOUT[l, o] = _blk(w_out[l][:, o * 128:(o + 1) * 128], 128)
            WD[l, o] = _blk(w_down[l][:, o * 128:(o + 1) * 128], 128)
        for f in range(NF):
            WG[l, f] = _blk(w_gate[l][:, f * 128:(f + 1) * 128], 128)
            WU[l, f] = _blk(w_up[l][:, f * 128:(f + 1) * 128], 128)
    GN = np.zeros((128, 64), f32)
    for l in range(2):
        GN[:, l * 32:l * 32 + 16] = np.asarray(norm_mix, f32)[l].reshape(16, 128).T
        GN[:, l * 32 + 16:l * 32 + 32] = np.asarray(norm_ffn, f32)[l].reshape(16, 128).T
    GQK = np.ascontiguousarray(np.asarray(qk_norm, f32).reshape(12, 128).T)
    SINK = np.ascontiguousarray(np.broadcast_to(np.asarray(sink_a, f32).reshape(1, 12), (128, 12)))
    cidx = _ctab_index()
    TABC = np.zeros((2, 128, 7 * 512), f32)
    for l in range(2):
        ext = np.concatenate([np.asarray(rpb_c, f32)[l].reshape(-1), np.array([NEGM], f32)])
        tab = ext[cidx]
        TABC[l] = tab.transpose(1, 0, 2, 3).reshape(128, 7 * 512)
    k = np.arange(128)[:, None]; q = np.arange(128)[None, :]
    CONST = np.zeros((128, 512), f32)
    CONST[:, 0:128] = np.eye(128, dtype=f32)
    CONST[:, 128:256] = np.where(k >= q, 0.0, NEGM)
    CONST[:, 256:384] = np.where(k <= q, 0.0, NEGM)
    for m_ in range(16):
        CONST[m_ + 16, 384 + m_] = -1.0
        CONST[m_, 384 + 16 + m_] = 1.0
    inv = (ROPE_THETA ** (-np.arange(0, 32, 2, dtype=np.float32) / 32)).astype(np.float32)
    pos = np.arange(SEQ, dtype=np.float32)
    ang = pos[:, None] * inv[None, :]
    cos_t = np.cos(ang).astype(f32).T
    sin_t = np.sin(ang).astype(f32).T
    common = dict(WQK=WQK, WV=WV, WOUT=WOUT, WG=WG, WU=WU, WD=WD, GN=GN, GQK=GQK, SINK=SINK, TABC=TABC, CONST=CONST)
    in_maps = []
    for c in range(NCORES):
        g0 = c * OWN
        win = xpad[g0:g0 + U]
        m = dict(common)
        m["xT"] = np.ascontiguousarray(win.T)
        gpos = (np.arange(U) + g0 - 2 * HALO) % SEQ
        COSW = np.ones((128, U), f32)
        COSW[0:16] = cos_t[:, gpos]; COSW[16:32] = cos_t[:, gpos]
        SINW = np.zeros((32, U), f32)
        SINW[0:16] = sin_t[:, gpos]; SINW[16:32] = sin_t[:, gpos]
        m["COSW"] = COSW; m["SINW"] = SINW
        m["COLS"] = _build_cols(c)
        in_maps.append(m)
    return in_maps


_NC_CACHE = {}


def kernel(x_prompt, x_sample, norm_mix, w_in, qk_norm, sink_a, rpb_c, w_out, norm_ffn, w_gate, w_up, w_down):
    in_maps = prepare_inputs(x_prompt, x_sample, norm_mix, w_in, qk_norm, sink_a, rpb_c, w_out, norm_ffn, w_gate, w_up, w_down)
    nc = build_program()
    res = run_bass_kernel_spmd(nc, in_maps, core_ids=list(range(NCORES)))
    ys = [np.asarray(res.results[c]["yT"], np.float32).T for c in range(NCORES)]
    y = np.concatenate(ys, axis=0)
    nb = np.asarray(x_prompt).shape[0] * SEQ
    y_prompt = np.ascontiguousarray(y[:nb].reshape(np.asarray(x_prompt).shape))
    y_sample = np.ascontiguousarray(y[nb:].reshape(np.asarray(x_sample).shape))
    return (y_prompt, y_sample)
```
